# Optimizing a Trainium2 kernel written in Bass

```python
import math
import jax, jax.numpy as jnp
from jax import lax
import numpy as np

D_MODEL = 1024
BATCH = 16
SEQ = 2048
DEPTH = 4

N_MIXERS = 2
MLA_HEADS = 8
MLA_Q_RANK = 384
MLA_KV_RANK = 256
MLA_NOPE_DIM = 128
MLA_ROPE_DIM = 64
MLA_V_DIM = 128
ROPE_THETA = 10000.0
MOBA_HEADS = 8
MOBA_HEAD_DIM = D_MODEL // MOBA_HEADS
MOBA_BLOCK = 256
MOBA_TOP_K = 3
REL_BUCKETS = 32
REL_MAX_DISTANCE = 1024
D_FF = 4 * D_MODEL
Q_BLOCK = 128
RMS_EPS = 1e-6
N_MLA_LAYERS = (DEPTH + 1) // 2
N_MOBA_LAYERS = DEPTH // 2

kernel_name = "hybrid_mla_moba_sqrelu_trunk"


def rms_norm(x, g):
    xf = x.astype(jnp.float32)
    y = xf * lax.rsqrt(jnp.mean(xf * xf, axis=-1, keepdims=True) + RMS_EPS)
    return (y * g.astype(jnp.float32)).astype(x.dtype)


def rope_tables(positions):
    inv_freq = ROPE_THETA ** (-jnp.arange(0, MLA_ROPE_DIM, 2, dtype=jnp.float32) / MLA_ROPE_DIM)
    ang = positions.astype(jnp.float32)[..., None] * inv_freq
    return jnp.cos(ang), jnp.sin(ang)


def rope(x, cos, sin):
    half = x.shape[-1] // 2
    x1 = x[..., :half].astype(jnp.float32)
    x2 = x[..., half:].astype(jnp.float32)
    return jnp.concatenate([x1 * cos - x2 * sin, x2 * cos + x1 * sin], axis=-1).astype(x.dtype)


def t5_bucket(dist):
    n = jnp.maximum(dist, 0)
    max_exact = REL_BUCKETS // 2
    nf = jnp.maximum(n, 1).astype(jnp.float32)
    large = max_exact + (jnp.log(nf / max_exact) / math.log(REL_MAX_DISTANCE / max_exact)
                         * (REL_BUCKETS - max_exact)).astype(jnp.int32)
    large = jnp.minimum(large, REL_BUCKETS - 1)
    return jnp.where(n < max_exact, n, large)


def sq_relu_mlp(h, w_in, w_out):
    a = jax.nn.relu(h @ w_in)
    return (a * a) @ w_out


def mla_attention(q_nope, q_pe, k_nope, k_pe, v):
    B, H, S, _ = q_nope.shape
    nqb = S // Q_BLOCK
    scale = 1.0 / math.sqrt(MLA_NOPE_DIM + MLA_ROPE_DIM)
    k_idx = jnp.arange(S)

    def to_blocks(t):
        return jnp.moveaxis(t.reshape(B, H, nqb, Q_BLOCK, t.shape[-1]), 2, 0)

    def body(args):
        qn, qp, c = args
        logits = (jnp.einsum('bhqd,bhkd->bhqk', qn, k_nope)
                  + jnp.einsum('bhqd,bkd->bhqk', qp, k_pe)).astype(jnp.float32) * scale
        q_idx = c * Q_BLOCK + jnp.arange(Q_BLOCK)
        causal = k_idx[None, :] <= q_idx[:, None]
        logits = jnp.where(causal, logits, -jnp.inf)
        p = jax.nn.softmax(logits, axis=-1).astype(v.dtype)
        return jnp.einsum('bhqk,bhkd->bhqd', p, v)

    out = lax.map(body, (to_blocks(q_nope), to_blocks(q_pe), jnp.arange(nqb, dtype=jnp.int32)))
    return jnp.moveaxis(out, 0, 2).reshape(B, H, S, v.shape[-1])


def mla_mixer(h, w_in, q_a_norm, kv_a_norm, w_uq, w_ukv, q_nope_norm, q_rope_norm,
              k_nope_norm, k_rope_norm, w_o, cos, sin):
    B, S, _ = h.shape
    proj = h @ w_in
    c_q = proj[..., :MLA_Q_RANK]
    c_kv = proj[..., MLA_Q_RANK:MLA_Q_RANK + MLA_KV_RANK]
    k_pe = proj[..., MLA_Q_RANK + MLA_KV_RANK:]
    q = (rms_norm(c_q, q_a_norm) @ w_uq).reshape(B, S, MLA_HEADS, MLA_NOPE_DIM + MLA_ROPE_DIM)
    kv = (rms_norm(c_kv, kv_a_norm) @ w_ukv).reshape(B, S, MLA_HEADS, MLA_NOPE_DIM + MLA_V_DIM)
    q_nope = rms_norm(q[..., :MLA_NOPE_DIM], q_nope_norm)
    q_pe = rope(rms_norm(q[..., MLA_NOPE_DIM:], q_rope_norm), cos[:, :, None, :], sin[:, :, None, :])
    k_nope = rms_norm(kv[..., :MLA_NOPE_DIM], k_nope_norm)
    v = kv[..., MLA_NOPE_DIM:]
    k_pe = rope(rms_norm(k_pe, k_rope_norm), cos, sin)
    out = mla_attention(q_nope.transpose(0, 2, 1, 3), q_pe.transpose(0, 2, 1, 3),
                        k_nope.transpose(0, 2, 1, 3), k_pe, v.transpose(0, 2, 1, 3))
    return out.transpose(0, 2, 1, 3).reshape(B, S, MLA_HEADS * MLA_V_DIM) @ w_o


def moba_attention(q, k, v, positions, rel_bias_table):
    B, H, S, dh = q.shape
    nb = -(-S // MOBA_BLOCK)
    pad = nb * MOBA_BLOCK - S
    n_sel = min(MOBA_TOP_K, nb - 1)
    nqb = S // Q_BLOCK
    scale = 1.0 / math.sqrt(dh)
    kb = jnp.pad(k, ((0, 0), (0, 0), (0, pad), (0, 0))).reshape(B, H, nb, MOBA_BLOCK, dh)
    vb = jnp.pad(v, ((0, 0), (0, 0), (0, pad), (0, 0))).reshape(B, H, nb, MOBA_BLOCK, dh)
    k_mean = jnp.mean(kb.astype(jnp.float32), axis=3)
    pos_b = jnp.pad(positions, ((0, 0), (0, pad)), mode='edge').reshape(B, nb, MOBA_BLOCK)
    pos_q = jnp.moveaxis(positions.reshape(B, nqb, Q_BLOCK), 1, 0)
    q_blocks = jnp.moveaxis(q.reshape(B, H, nqb, Q_BLOCK, dh), 2, 0)
    table_h = rel_bias_table.T
    b_i = jnp.arange(B)[:, None, None]
    h_i = jnp.arange(H)[None, :, None]
    h_i4 = jnp.arange(H)[None, :, None, None]
    blk_ids = jnp.arange(nb)
    t_ids = jnp.arange(MOBA_BLOCK)

    def rel_bias(pq, pk):
        return table_h[h_i4, t5_bucket(pq - pk)].astype(jnp.float32)

    def body(args):
        qi, pq, c = args
        q_start = c * Q_BLOCK
        own = q_start // MOBA_BLOCK
        q_idx = q_start + jnp.arange(Q_BLOCK)
        pq4 = pq[:, None, :, None]
        parts = []
        sel = None
        if n_sel > 0:
            gate = jnp.einsum('bhqd,bhnd->bhqn', qi.astype(jnp.float32), k_mean)
            gate = jnp.where(blk_ids < own, gate, -jnp.inf)
            _, sel = lax.top_k(gate, n_sel)
            for r in range(n_sel):
                sel_r = sel[..., r]
                k_sel = kb[b_i, h_i, sel_r]
                lg = (jnp.einsum('bhqd,bhqtd->bhqt', qi, k_sel).astype(jnp.float32) * scale
                      + rel_bias(pq4, pos_b[b_i, sel_r]))
                parts.append(jnp.where(r < own, lg, -jnp.inf))
        k_own = lax.dynamic_index_in_dim(kb, own, axis=2, keepdims=False)
        v_own = lax.dynamic_index_in_dim(vb, own, axis=2, keepdims=False)
        pos_own = lax.dynamic_index_in_dim(pos_b, own, axis=1, keepdims=False)
        lg_own = (jnp.einsum('bhqd,bhtd->bhqt', qi, k_own).astype(jnp.float32) * scale
                  + rel_bias(pq4, pos_own[:, None, None, :]))
        causal = (own * MOBA_BLOCK + t_ids)[None, :] <= q_idx[:, None]
        parts.append(jnp.where(causal, lg_own, -jnp.inf))
        p = jax.nn.softmax(jnp.concatenate(parts, axis=-1), axis=-1).astype(v.dtype)
        p = p.reshape(B, H, Q_BLOCK, n_sel + 1, MOBA_BLOCK)
        out = jnp.einsum('bhqt,bhtd->bhqd', p[..., n_sel, :], v_own)
        for r in range(n_sel):
            v_sel = vb[b_i, h_i, sel[..., r]]
            out = out + jnp.einsum('bhqt,bhqtd->bhqd', p[..., r, :], v_sel)
        return out

    out = lax.map(body, (q_blocks, pos_q, jnp.arange(nqb, dtype=jnp.int32)))
    return jnp.moveaxis(out, 0, 2).reshape(B, H, S, dh)


def moba_mixer(h, w_qkv, q_norm, k_norm, w_o, positions, rel_bias_table):
    B, S, _ = h.shape
    qkv = (h @ w_qkv).reshape(B, S, 3, MOBA_HEADS, MOBA_HEAD_DIM)
    q = rms_norm(qkv[:, :, 0], q_norm).transpose(0, 2, 1, 3)
    k = rms_norm(qkv[:, :, 1], k_norm).transpose(0, 2, 1, 3)
    v = qkv[:, :, 2].transpose(0, 2, 1, 3)
    out = moba_attention(q, k, v, positions, rel_bias_table)
    return out.transpose(0, 2, 1, 3).reshape(B, S, MOBA_HEADS * MOBA_HEAD_DIM) @ w_o


def setup_inputs(seed: int = 0) -> dict:
    key = jax.random.key(seed)
    ks = jax.random.split(key, 24)

    def nrm(k, shape, scale):
        return scale * jax.random.normal(k, shape, jnp.float32)

    def gain(k, shape):
        return 1.0 + 0.05 * jax.random.normal(k, shape, jnp.float32)

    qk_dim = MLA_NOPE_DIM + MLA_ROPE_DIM
    offset = jax.random.randint(ks[1], (BATCH,), 0, 4096, dtype=jnp.int32)
    positions = offset[:, None] + jnp.arange(SEQ, dtype=jnp.int32)[None, :]
    return {
        "x": nrm(ks[0], (BATCH, SEQ, D_MODEL), 1.0),
        "positions": positions,
        "rel_bias_table": nrm(ks[2], (REL_BUCKETS, MOBA_HEADS), 0.5),
        "attn_norm": gain(ks[3], (DEPTH, D_MODEL)),
        "mlp_norm": gain(ks[4], (DEPTH, D_MODEL)),
        "mla_w_in": nrm(ks[5], (N_MLA_LAYERS, D_MODEL, MLA_Q_RANK + MLA_KV_RANK + MLA_ROPE_DIM), D_MODEL ** -0.5),
        "mla_q_a_norm": gain(ks[6], (N_MLA_LAYERS, MLA_Q_RANK)),
        "mla_kv_a_norm": gain(ks[7], (N_MLA_LAYERS, MLA_KV_RANK)),
        "mla_w_uq": nrm(ks[8], (N_MLA_LAYERS, MLA_Q_RANK, MLA_HEADS * qk_dim), MLA_Q_RANK ** -0.5),
        "mla_w_ukv": nrm(ks[9], (N_MLA_LAYERS, MLA_KV_RANK, MLA_HEADS * (MLA_NOPE_DIM + MLA_V_DIM)), MLA_KV_RANK ** -0.5),
        "mla_q_nope_norm": gain(ks[10], (N_MLA_LAYERS, MLA_NOPE_DIM)),
        "mla_q_rope_norm": gain(ks[11], (N_MLA_LAYERS, MLA_ROPE_DIM)),
        "mla_k_nope_norm": gain(ks[12], (N_MLA_LAYERS, MLA_NOPE_DIM)),
        "mla_k_rope_norm": gain(ks[13], (N_MLA_LAYERS, MLA_ROPE_DIM)),
        "mla_w_o": nrm(ks[14], (N_MLA_LAYERS, MLA_HEADS * MLA_V_DIM, D_MODEL), (MLA_HEADS * MLA_V_DIM) ** -0.5),
        "moba_w_qkv": nrm(ks[15], (N_MOBA_LAYERS, D_MODEL, 3 * MOBA_HEADS * MOBA_HEAD_DIM), D_MODEL ** -0.5),
        "moba_q_norm": gain(ks[16], (N_MOBA_LAYERS, MOBA_HEAD_DIM)),
        "moba_k_norm": gain(ks[17], (N_MOBA_LAYERS, MOBA_HEAD_DIM)),
        "moba_w_o": nrm(ks[18], (N_MOBA_LAYERS, MOBA_HEADS * MOBA_HEAD_DIM, D_MODEL), (MOBA_HEADS * MOBA_HEAD_DIM) ** -0.5),
        "mlp_w_in": nrm(ks[19], (DEPTH, D_MODEL, D_FF), D_MODEL ** -0.5),
        "mlp_w_out": nrm(ks[20], (DEPTH, D_FF, D_MODEL), D_FF ** -0.5),
    }


def reference(x, positions, rel_bias_table, attn_norm, mlp_norm, mla_w_in, mla_q_a_norm, mla_kv_a_norm,
              mla_w_uq, mla_w_ukv, mla_q_nope_norm, mla_q_rope_norm, mla_k_nope_norm, mla_k_rope_norm,
              mla_w_o, moba_w_qkv, moba_q_norm, moba_k_norm, moba_w_o, mlp_w_in, mlp_w_out):
    cos, sin = rope_tables(positions)
    for i in range(DEPTH):
        h = rms_norm(x, attn_norm[i])
        j = i // N_MIXERS
        if i % N_MIXERS == 0:
            x = x + mla_mixer(h, mla_w_in[j], mla_q_a_norm[j], mla_kv_a_norm[j], mla_w_uq[j], mla_w_ukv[j],
                              mla_q_nope_norm[j], mla_q_rope_norm[j], mla_k_nope_norm[j], mla_k_rope_norm[j],
                              mla_w_o[j], cos, sin)
        else:
            x = x + moba_mixer(h, moba_w_qkv[j], moba_q_norm[j], moba_k_norm[j], moba_w_o[j],
                               positions, rel_bias_table)
        h = rms_norm(x, mlp_norm[i])
        x = x + sq_relu_mlp(h, mlp_w_in[i], mlp_w_out[i])
    return x
```

```python
import math
import numpy as np
from contextlib import ExitStack
import concourse.bass as bass
import concourse.mybir as mybir
from concourse.bass_utils import run_bass_kernel_spmd

F32 = mybir.dt.float32; BF16 = mybir.dt.bfloat16; I32 = mybir.dt.int32
ALU = mybir.AluOpType; AF = mybir.ActivationFunctionType; AX = mybir.AxisListType

SAME_ENGINE_SYNC = True
COMPUTE = ("pe", "act", "dve", "pool")
S = 2048; D = 1024; NCH = 8; TG = 512; NG = 4; DFF = 4096
EPS = 1e-6
NEG = -30000.0
SB_BASE = 16512; SB_END = 229344


class Buf:
    __slots__ = ("name", "writers", "readers", "dsem", "dcnt", "excl")

    def __init__(self, name, excl=False):
        self.name = name; self.writers = []; self.readers = []; self.dsem = None; self.dcnt = 0
        self.excl = excl


class Op:
    __slots__ = ("eng", "idx", "fn", "deps", "marked", "dma")

    def __init__(self, eng, idx, fn, dma=None):
        self.eng = eng; self.idx = idx; self.fn = fn; self.deps = []; self.marked = False; self.dma = dma


class Prog:
    def __init__(self, nc):
        self.nc = nc
        self.streams = {e: [] for e in ("pe", "act", "dve", "pool", "sp")}
        self.seen = {e: {} for e in self.streams}
        self.dbufs = []

    def _resolve(self, eng, toks):
        out = []
        seen = self.seen[eng]
        for t in toks:
            if t[0] == "c":
                _, E, j = t
                if E == eng and (eng == "pe" or not SAME_ENGINE_SYNC):
                    continue
                if seen.get(E, -1) >= j:
                    continue
                seen[E] = j
                self.streams[E][j].marked = True
                out.append(t)
            else:
                b = t[1]
                v = b.dcnt
                key = id(b)
                if seen.get(key, -1) >= v:
                    continue
                seen[key] = v
                out.append(("d", b, v))
        return out

    def _record(self, eng, op, tok, reads, writes):
        toks = []
        for b in reads:
            toks.extend(b.writers)
            if b.excl:
                toks.extend(b.readers)
        for b in writes:
            toks.extend(b.writers); toks.extend(b.readers)
        op.deps = self._resolve(eng, toks)
        for b in reads:
            b.readers.append(tok)
        for b in writes:
            b.writers = [tok]; b.readers = []
        self.streams[eng].append(op)

    def op(self, eng, fn, reads=(), writes=()):
        o = Op(eng, len(self.streams[eng]), fn)
        self._record(eng, o, ("c", eng, o.idx), reads, writes)

    def dma(self, q, out_ap, in_ap, sembuf, reads=(), writes=()):
        if sembuf.dsem is None:
            sembuf.dsem = True
            self.dbufs.append(sembuf)
        o = Op(q, len(self.streams[q]), None, dma=(out_ap, in_ap, sembuf))
        self._record(q, o, ("d", sembuf, sembuf.dcnt + 16), reads, writes)
        sembuf.dcnt += 16

    def emit(self):
        nc = self.nc
        with ExitStack() as es:
            esem = {e: es.enter_context(nc.semaphore("sem_" + e)) for e in COMPUTE}
            for i, b in enumerate(self.dbufs):
                b.dsem = es.enter_context(nc.semaphore("ds%d" % i))
            val = {}
            for e in COMPUTE:
                c = 0
                for o in self.streams[e]:
                    if o.marked:
                        c += 1
                    val[(e, o.idx)] = c
            block = es.enter_context(nc.Block())

            def run(engobj, name):
                for o in self.streams[name]:
                    for t in o.deps:
                        if t[0] == "c":
                            engobj.wait_ge(esem[t[1]], val[(t[1], t[2])])
                        else:
                            engobj.wait_ge(t[1].dsem, t[2])
                    if o.dma is not None:
                        out_ap, in_ap, sb = o.dma
                        engobj.dma_start(out=out_ap, in_=in_ap).then_inc(sb.dsem, 16)
                    else:
                        ins = o.fn(engobj)
                        if o.marked:
                            ins.then_inc(esem[name], 1)
                if name == "sp":
                    for b in self.dbufs:
                        engobj.wait_ge(b.dsem, b.dcnt)

            @block.tensor
            def _(e): run(e, "pe")

            @block.scalar
            def _(e): run(e, "act")

            @block.vector
            def _(e): run(e, "dve")

            @block.gpsimd
            def _(e): run(e, "pool")

            @block.sync
            def _(e): run(e, "sp")


def _dsize(dt):
    return 2 if dt == BF16 else 4


class Arena:
    def __init__(self, nc, base, end):
        self.nc = nc; self.base = base; self.end = end; self.top = base; self.hist = []; self.n = 0

    def alloc(self, name, shape, dtype, nbufs=None):
        nbytes = int(np.prod(shape[1:])) * _dsize(dtype)
        start = (self.top + 31) // 32 * 32
        assert start + nbytes <= self.end, "SBUF arena overflow: %s needs %d at %d (end %d)" % (name, nbytes, start, self.end)
        self.n += 1
        h = self.nc.alloc_sbuf_tensor_at("%s_%d" % (name, self.n), list(shape), dtype, offset=start)
        inh = []
        for (s0, e0, ob) in self.hist:
            if s0 < start + nbytes and e0 > start:
                inh.extend(ob.writers); inh.extend(ob.readers)
        bufs = []
        for i in range(nbufs or 1):
            b = Buf(name if nbufs is None else "%s.%d" % (name, i))
            b.writers.extend(inh)
            bufs.append(b)
        for b in bufs:
            self.hist.append((start, start + nbytes, b))
        self.top = start + nbytes
        return (h, bufs[0]) if nbufs is None else (h, bufs)

    def mark(self):
        return self.top

    def release(self, m):
        self.top = m


class Ring:
    def __init__(self, arena, name, shape, dtype, n):
        self.items = [arena.alloc("%s%d" % (name, i), shape, dtype) for i in range(n)]
        self.i = 0

    def next(self):
        it = self.items[self.i % len(self.items)]
        self.i += 1
        return it


class PsRing:
    def __init__(self, items):
        self.items = items; self.i = 0

    def next(self):
        it = self.items[self.i % len(self.items)]
        self.i += 1
        return it


CF_ID = 0; CF_ROT = 128; CF_INVF = 192; CF_NEGM = 193; CF_PAST = 257; CF_CSTRIP = 321; CF_W = 321 + 896
CB_ID = 0; CB_ONES = 128; CB_IND = 256; CB_W = 256 + 1024
STRIP_W = 1920
G_AN = 0; G_MN = 32; G_MLA = 64; G_MOBA = 82; G_W = 86


def _t5_bucket_np(d):
    n = np.maximum(d, 0)
    nf = np.maximum(n, 1).astype(np.float32)
    large = 16 + (np.log(nf / np.float32(16)) / np.float32(math.log(64.0)) * np.float32(16)).astype(np.int32)
    large = np.minimum(large, 31)
    return np.where(n < 16, n, large)


def _host_consts():
    cf = np.zeros((128, CF_W), np.float32)
    cf[:, CF_ID:CF_ID + 128] = np.eye(128, dtype=np.float32)
    rotT = np.zeros((64, 64), np.float32)
    for m in range(32):
        rotT[m + 32, m] = -1.0
        rotT[m, m + 32] = 1.0
    cf[:64, CF_ROT:CF_ROT + 64] = rotT
    inv = (10000.0 ** (-np.arange(0, 64, 2, dtype=np.float32) / np.float32(64))).astype(np.float32)
    cf[:32, CF_INVF] = inv; cf[32:64, CF_INVF] = inv
    for i in range(8):
        own = 4 + i // 2
        for n in range(8):
            cf[:, CF_NEGM + i * 8 + n] = -1e30 if n >= own else 0.0
            cf[:, CF_PAST + i * 8 + n] = 30000.0 if n < own else 0.0
    k = np.arange(128)[:, None]; j = np.arange(896)[None, :]
    cf[:, CF_CSTRIP:CF_CSTRIP + 896] = np.where(j - k - 384 >= 0, 0.0, NEG)
    cb = np.zeros((128, CB_W), np.float32)
    cb[:, CB_ID:CB_ID + 128] = np.eye(128, dtype=np.float32)
    cb[:, CB_ONES:CB_ONES + 128] = 1.0
    for n in range(8):
        cb[n, CB_IND + n * 128: CB_IND + (n + 1) * 128] = 1.0
    return cf, cb


def _host_gains(inp):
    g = np.zeros((128, G_W), np.float32)

    def put(col, vec):
        vec = np.asarray(vec, np.float32)
        n = vec.shape[0]
        if n >= 128:
            c = n // 128
            g[:, col:col + c] = vec.reshape(c, 128).T
        else:
            g[:n, col] = vec
    for l in range(4):
        put(G_AN + 8 * l, inp["attn_norm"][l]); put(G_MN + 8 * l, inp["mlp_norm"][l])
    for j in range(2):
        b = G_MLA + 9 * j
        put(b, inp["mla_q_a_norm"][j]); put(b + 3, inp["mla_kv_a_norm"][j])
        put(b + 5, inp["mla_q_nope_norm"][j]); put(b + 6, inp["mla_q_rope_norm"][j])
        put(b + 7, inp["mla_k_nope_norm"][j]); put(b + 8, inp["mla_k_rope_norm"][j])
        b2 = G_MOBA + 2 * j
        put(b2, inp["moba_q_norm"][j]); put(b2 + 1, inp["moba_k_norm"][j])
    return g


def _host_strip(table):
    k = np.arange(128)[:, None]; j = np.arange(STRIP_W)[None, :]
    d = j - k - 384
    bidx = _t5_bucket_np(d)
    t = np.asarray(table, np.float32)
    strip = np.transpose(t[bidx], (2, 0, 1)).copy()
    strip[:, d < 0] = NEG
    return np.ascontiguousarray(strip, dtype=np.float32)


class KB:
    def __init__(self, nseq=2, plan=None, debug_out=False, stop=99):
        self.nseq = nseq; self.stop = stop
        self.plan = plan if plan is not None else [("mla", 0, 0), ("mlp", 0, 0), ("moba", 0, 1), ("mlp", 1, 1),
                                                    ("mla", 1, 2), ("mlp", 2, 2), ("moba", 1, 3), ("mlp", 3, 3)]
        nc = self.nc = bass.Bass("TRN2", target_bir_lowering=False)
        self.P = Prog(nc)
        dt = nc.dram_tensor
        self.d = {}
        def inp(name, shape, dtype=F32):
            self.d[name] = dt(name, list(shape), dtype, kind="ExternalInput")
        inp("x", [nseq, S, D]); self.d["pos"] = None
        inp("cf", [128, CF_W]); inp("cb", [128, CB_W]); inp("gains", [128, G_W])
        kinds = set(k for (k, _, _) in self.plan)
        if "mla" in kinds:
            inp("pos", [nseq, S], I32)
            inp("mla_w_in", [2, 1024, 704]); inp("mla_w_uq", [2, 384, 1536]); inp("mla_w_ukv", [2, 256, 2048])
            inp("mla_w_o", [2, 1024, 1024])
        else:
            del self.d["pos"]
        if "moba" in kinds:
            inp("strip", [8, 128, STRIP_W]); inp("moba_w_qkv", [2, 1024, 3072]); inp("moba_w_o", [2, 1024, 1024])
        if "mlp" in kinds:
            inp("mlp_w_in", [4, 1024, 4096]); inp("mlp_w_out", [4, 4096, 1024])
        self.d["out"] = dt("out", [nseq, S, D], F32, kind="ExternalOutput")
        off = SB_BASE
        def static(name, shape, dtype):
            nonlocal off
            nbytes = int(np.prod(shape[1:])) * _dsize(dtype)
            h = nc.alloc_sbuf_tensor_at(name, list(shape), dtype, offset=off)
            off = (off + nbytes + 31) // 32 * 32
            return h
        self.CF = static("CF", [128, CF_W], F32); self.CFb = Buf("CF")
        self.CB = static("CB", [128, CB_W], BF16); self.CBb = Buf("CB")
        self.G = static("G", [128, G_W], F32); self.Gb = Buf("G")
        self.X = static("X", [128, NCH, S], F32)
        self.XB = [[Buf("X%d_%d" % (c, g)) for g in range(NG)] for c in range(NCH)]
        self.A = Arena(nc, off, SB_END)
        self.PS = []
        for i in range(8):
            h = nc.alloc_psum_tensor("ps%d" % i, [128, 512], F32)
            self.PS.append((h, Buf("ps%d" % i, excl=True)))
        self.psAll = PsRing(self.PS)
        self.psA = PsRing(self.PS[0:4]); self.psB = PsRing(self.PS[4:8])
        P = self.P
        P.dma("sp", self.CF[:], self.d["cf"].ap(), self.CFb, writes=[self.CFb])
        P.dma("pool", self.CB[:], self.d["cb"].ap(), self.CBb, writes=[self.CBb])
        P.dma("sp", self.G[:], self.d["gains"].ap(), self.Gb, writes=[self.Gb])
        self.ident = self.CF[:, CF_ID:CF_ID + 128]
        self.onesb = self.CB[:, CB_ONES:CB_ONES + 128]
        self.evac_i = 0

    def dump(self, name, ap, buf):
        shape = [int(v) for v in ap.shape]
        dtn = self.nc.dram_tensor("dbg_" + name, shape, F32, kind="ExternalOutput")
        self.d["dbg_" + name] = dtn
        self.P.dma("pool", dtn.ap(), ap, buf, reads=[buf])

    def mm(self, out, lhsT, rhs, start, stop, reads, writes):
        self.P.op("pe", lambda e: e.matmul(out, lhsT=lhsT, rhs=rhs, start=start, stop=stop), reads=reads, writes=writes)

    def tr(self, out, in_, reads, writes):
        ident = self.ident[0:in_.shape[0], 0:in_.shape[0]]
        self.P.op("pe", lambda e: e.transpose(out, in_, ident), reads=list(reads) + [self.CFb], writes=writes)

    def act(self, out, in_, func, reads, writes, **kw):
        self.P.op("act", lambda e: e.activation(out, in_, func, **kw), reads=reads, writes=writes)

    def copy(self, out, in_, reads, writes, eng=None):
        if eng is None:
            eng = "act" if (self.evac_i % 2 == 0) else "dve"
            self.evac_i += 1
        if eng == "act":
            self.P.op("act", lambda e: e.copy(out, in_), reads=reads, writes=writes)
        else:
            self.P.op("dve", lambda e: e.tensor_copy(out, in_), reads=reads, writes=writes)

    def dve(self, fn, reads, writes):
        self.P.op("dve", fn, reads=reads, writes=writes)

    def rstd_from_psum(self, ps, npart, n_total, rs, reads, rsb):
        self.act(rs[0:npart, :], ps[0:npart, :], AF.Sqrt, reads=reads, writes=[rsb], bias=EPS, scale=1.0 / n_total)
        self.dve(lambda e: e.reciprocal(rs[0:npart, :], rs[0:npart, :]), reads=[rsb], writes=[rsb])

    def load_x(self, s):
        A = self.A; m = A.mark()
        ring = Ring(A, "xin", [128, D], F32, 2)
        xd = self.d["x"].ap()
        for g in range(NG):
            banks = [self.psAll.next() for _ in range(NCH)]
            for j in range(4):
                tt = g * 4 + j
                st, sb = ring.next()
                self.P.dma("sp", st[:], xd[s, tt * 128:(tt + 1) * 128, :], sb, writes=[sb])
                for c in range(NCH):
                    ph, pb = banks[c]
                    self.tr(ph[:, j * 128:(j + 1) * 128], st[:, c * 128:(c + 1) * 128], reads=[sb], writes=[pb])
            for c in range(NCH):
                ph, pb = banks[c]
                self.copy(self.X[:, c, g * TG:(g + 1) * TG], ph[:], reads=[pb], writes=[self.XB[c][g]])
        A.release(m)

    def store_x(self, s):
        A = self.A; m = A.mark()
        ring = Ring(A, "xout", [128, D], F32, 2)
        od = self.d["out"].ap()
        for tt in range(16):
            g = tt // 4
            st, sb = ring.next()
            for half in range(2):
                ph, pb = self.psAll.next()
                for cc in range(4):
                    c = half * 4 + cc
                    self.tr(ph[:, cc * 128:(cc + 1) * 128], self.X[:, c, tt * 128:(tt + 1) * 128],
                            reads=[self.XB[c][g]], writes=[pb])
                self.copy(st[:, half * 512:(half + 1) * 512], ph[:], reads=[pb], writes=[sb])
            self.P.dma("sp", od[s, tt * 128:(tt + 1) * 128, :], st[:], sb, reads=[sb])
        A.release(m)

    def norm_x(self, gcol, H, HB, sqr, rsr, ps_ring):
        for g in range(NG):
            ph, pb = ps_ring.next()
            sl = slice(g * TG, (g + 1) * TG)
            for c in range(NCH):
                sq, sqb = sqr.next()
                self.act(sq[:], self.X[:, c, sl], AF.Square, reads=[self.XB[c][g]], writes=[sqb])
                self.mm(ph[:], self.onesb, sq[:], c == 0, c == NCH - 1, reads=[sqb, self.CBb], writes=[pb])
            rs, rsb = rsr.next()
            self.rstd_from_psum(ph, 128, float(D), rs, [pb], rsb)
            for c in range(NCH):
                gc = self.G[:, gcol + c:gcol + c + 1]
                self.dve(lambda e, c=c, gc=gc, rs=rs, sl=sl: e.scalar_tensor_tensor(
                    H[:, c, sl], self.X[:, c, sl], gc, rs[:], ALU.mult, ALU.mult),
                    reads=[self.XB[c][g], rsb, self.Gb], writes=[HB[g]])

    def mlp(self, l):
        A = self.A; P = self.P; m0 = A.mark()
        H, HB = A.alloc("H", [128, NCH, S], BF16, nbufs=NG)
        m1 = A.mark()
        sqr = Ring(A, "sq", [128, TG], BF16, 4); rsr = Ring(A, "rs", [128, TG], F32, 2)
        self.norm_x(G_MN + 8 * l, H, HB, sqr, rsr, self.psAll)
        A.release(m1)
        W1r = Ring(A, "W1", [128, 8, 1024], BF16, 2); W2r = Ring(A, "W2", [128, 8, 1024], BF16, 2)
        Ar = Ring(A, "Aa", [128, 8, TG], BF16, 2); Rr = Ring(A, "Rr", [128, TG], F32, 3)
        w_in = self.d["mlp_w_in"].ap(); w_out = self.d["mlp_w_out"].ap()

        def load_w(q):
            (w1, b1) = W1r.next(); (w2, b2) = W2r.next()
            src1 = w_in[l, :, q * 1024:(q + 1) * 1024].rearrange("(c p) n -> p c n", p=128)
            src2 = w_out[l, q * 1024:(q + 1) * 1024, :].rearrange("(c p) n -> p c n", p=128)
            for cc in range(0, 8, 2):
                P.dma("pool", w1[:, cc:cc + 2, :], src1[:, cc:cc + 2, :], b1, writes=[b1])
            for cc in range(0, 8, 2):
                P.dma("pool", w2[:, cc:cc + 2, :], src2[:, cc:cc + 2, :], b2, writes=[b2])
            return (w1, b1, w2, b2)

        def up(W, g):
            w1, b1, _, _ = W
            a, ab = Ar.next()
            sl = slice(g * TG, (g + 1) * TG)
            for f in range(8):
                ph, pb = self.psAll.next()
                for k in range(NCH):
                    self.mm(ph[:], w1[:, k, f * 128:(f + 1) * 128], H[:, k, sl], k == 0, k == NCH - 1,
                            reads=[b1, HB[g]], writes=[pb])
                r, rb = Rr.next()
                self.act(r[:], ph[:], AF.Relu, reads=[pb], writes=[rb])
                self.dve(lambda e, a=a, f=f, r=r: e.tensor_tensor(a[:, f, :], r[:], r[:], ALU.mult), reads=[rb], writes=[ab])
            return (a, ab)

        def down(W, g, at):
            _, _, w2, b2 = W
            a, ab = at
            sl = slice(g * TG, (g + 1) * TG)
            for mch in range(NCH):
                ph, pb = self.psAll.next()
                for f in range(8):
                    self.mm(ph[:], w2[:, f, mch * 128:(mch + 1) * 128], a[:, f, :], f == 0, f == 7,
                            reads=[b2, ab], writes=[pb])
                xs = self.X[:, mch, sl]
                self.dve(lambda e, xs=xs, ph=ph: e.tensor_tensor(xs, xs, ph[:], ALU.add),
                         reads=[pb, self.XB[mch][g]], writes=[self.XB[mch][g]])

        Ws = {0: load_w(0)}
        steps = [(q, g) for q in range(4) for g in range(NG)]
        pend = None
        for i, (q, g) in enumerate(steps):
            at = up(Ws[q], g)
            if pend is not None:
                down(*pend)
            pend = (Ws[q], g, at)
            if g == 0 and q + 1 < 4:
                Ws[q + 1] = load_w(q + 1)
        down(*pend)
        A.release(m0)

    def rope_tables(self, s, cos, cosb, sin, sinb):
        A = self.A; m = A.mark()
        pi_, pib = A.alloc("posi", [64, S], I32)
        ang, angb = A.alloc("ang", [64, S], F32)
        kf, kfb = A.alloc("kf", [64, S], F32)
        r, rb = A.alloc("r", [64, S], F32)
        shh, shb = A.alloc("shh", [64, S], F32)
        chh, chb = A.alloc("chh", [64, S], F32)
        src = bass.AP(self.d["pos"], s * S, [[0, 64], [1, S]])
        self.P.dma("sp", pi_[:], src, pib, writes=[pib])
        self.dve(lambda e: e.tensor_copy(ang[:], pi_[:]), reads=[pib], writes=[angb])
        invf = self.CF[0:64, CF_INVF:CF_INVF + 1]
        self.dve(lambda e: e.tensor_scalar(ang[:], ang[:], invf, None, ALU.mult), reads=[angb, self.CFb], writes=[angb])
        self.dve(lambda e: e.tensor_scalar(pi_[:], ang[:], 1.0 / (2.0 * math.pi), None, ALU.mult), reads=[angb], writes=[pib])
        self.dve(lambda e: e.tensor_copy(kf[:], pi_[:]), reads=[pib], writes=[kfb])
        self.dve(lambda e: e.scalar_tensor_tensor(r[:], kf[:], -2.0 * math.pi, ang[:], ALU.mult, ALU.add),
                 reads=[kfb, angb], writes=[rb])
        sh = 0.999999
        self.act(shh[:], r[:], AF.Sin, reads=[rb], writes=[shb], scale=0.5 * sh)
        self.act(chh[:], r[:], AF.Sin, reads=[rb], writes=[chb], bias=0.5 * math.pi * sh, scale=-0.5 * sh)
        self.dve(lambda e: e.scalar_tensor_tensor(sin[:], shh[:], 2.0, chh[:], ALU.mult, ALU.mult), reads=[shb, chb], writes=[sinb])
        self.dve(lambda e: e.tensor_tensor(kf[:], shh[:], shh[:], ALU.mult), reads=[shb], writes=[kfb])
        self.dve(lambda e: e.tensor_scalar(cos[:], kf[:], -2.0, 1.0, ALU.mult, ALU.add), reads=[kfb], writes=[cosb])
        A.release(m)

    def head_norm(self, ph, pb, npart, gcol, sqr, rsr, psr, out_fn):
        sq, sqb = sqr.next()
        self.act(sq[0:npart, :], ph[0:npart, :], AF.Square, reads=[pb], writes=[sqb])
        p2, p2b = psr.next()
        self.mm(p2[0:npart, :], self.onesb[0:npart, 0:npart], sq[0:npart, :], True, True, reads=[sqb, self.CBb], writes=[p2b])
        rs, rsb = rsr.next()
        self.rstd_from_psum(p2, npart, float(npart), rs, [p2b], rsb)
        gc = self.G[0:npart, gcol:gcol + 1]
        out_fn(gc, rs, rsb)

    def attention(self, qT, qb, kT, kb, V, Vb, extra_qk, scale, strip_fn, mask_fn, far_bias,
                  outT, outb, Pr, Tr, rsr):
        for qt in range(4):
            qs = slice(qt * TG, (qt + 1) * TG)
            oh, ob = self.psB.next(); lh, lb = self.psB.next()
            nk = 4 * qt + 4
            for kt in range(nk):
                ks = slice(kt * 128, (kt + 1) * 128)
                sh_, sb_ = self.psA.next()
                mk = mask_fn(qt, kt) if mask_fn is not None else None
                last_qk = (extra_qk is None) and (mk is None)
                self.mm(sh_[:], kT[:, ks], qT[:, qs], True, last_qk, reads=[kb, qb], writes=[sb_])
                if extra_qk is not None:
                    l2, r2, rd2 = extra_qk
                    self.mm(sh_[:], l2(kt), r2(qt), False, mk is None, reads=rd2, writes=[sb_])
                if mk is not None:
                    ml, mr, mrd = mk
                    self.mm(sh_[:], ml, mr, False, True, reads=mrd, writes=[sb_])
                p, pbuf = Pr.next()
                st = strip_fn(qt, kt)
                if st is not None:
                    sap, sbuf = st
                    t, tb = Tr.next()
                    self.dve(lambda e, t=t, sh_=sh_, sap=sap: e.scalar_tensor_tensor(
                        t[:], sh_[:], scale, sap, ALU.mult, ALU.add), reads=[sb_, sbuf], writes=[tb])
                    self.act(p[:], t[:], AF.Exp, reads=[tb], writes=[pbuf])
                elif far_bias is not None:
                    fb, fbb = far_bias
                    self.act(p[:], sh_[:], AF.Exp, reads=[sb_, fbb], writes=[pbuf], bias=fb, scale=scale)
                else:
                    self.act(p[:], sh_[:], AF.Exp, reads=[sb_], writes=[pbuf], scale=scale)
                self.mm(oh[:], V[:, kt, :], p[:], kt == 0, kt == nk - 1, reads=[Vb, pbuf], writes=[ob])
                self.mm(lh[:], self.onesb, p[:], kt == 0, kt == nk - 1, reads=[pbuf, self.CBb], writes=[lb])
            rl, rlb = rsr.next()
            self.dve(lambda e, rl=rl, lh=lh: e.reciprocal(rl[:], lh[:]), reads=[lb], writes=[rlb])
            self.dve(lambda e, rl=rl, oh=oh, qs=qs: e.tensor_tensor(outT[:, qs], oh[:], rl[:], ALU.mult),
                     reads=[ob, rlb], writes=[outb])

    def wo_half(self, wname, j, half, attn, attnb, Wo, Wob):
        wd = self.d[wname].ap()
        src = wd[j, half * 512:(half + 1) * 512, :].rearrange("(c p) n -> p c n", p=128)
        for hh in range(4):
            self.P.dma("pool", Wo[:, hh:hh + 1, :], src[:, hh:hh + 1, :], Wob, writes=[Wob])
        for g in range(NG):
            sl = slice(g * TG, (g + 1) * TG)
            for mch in range(NCH):
                ph, pb = self.psA.next()
                for hh in range(4):
                    self.mm(ph[:], Wo[:, hh, mch * 128:(mch + 1) * 128], attn[:, hh, sl], hh == 0, hh == 3,
                            reads=[Wob, attnb[hh]], writes=[pb])
                xs = self.X[:, mch, sl]
                self.dve(lambda e, xs=xs, ph=ph: e.tensor_tensor(xs, xs, ph[:], ALU.add),
                         reads=[pb, self.XB[mch][g]], writes=[self.XB[mch][g]])

    def mla(self, j, l, s):
        A = self.A; P = self.P; m0 = A.mark()
        gb = G_MLA + 9 * j
        cqn, cqnB = A.alloc("cqn", [128, 3, S], BF16, nbufs=NG)
        ckvn, ckvnB = A.alloc("ckvn", [128, 2, S], BF16, nbufs=NG)
        kpe, kpeb = A.alloc("kpe", [64, S], BF16)
        cos, cosb = A.alloc("cos", [64, S], F32); sin, sinb = A.alloc("sin", [64, S], F32)
        self.rope_tables(s, cos, cosb, sin, sinb)
        rotT = self.CF[0:64, CF_ROT:CF_ROT + 64]
        if self.stop <= 1:
            A.release(m0); return
        m1 = A.mark()
        H, HB = A.alloc("H", [128, NCH, S], BF16, nbufs=NG)
        Win, Winb = A.alloc("Win", [128, NCH, 704], BF16)
        src = self.d["mla_w_in"].ap()[j].rearrange("(c p) n -> p c n", p=128)
        for cc in range(0, 8, 4):
            P.dma("pool", Win[:, cc:cc + 4, :], src[:, cc:cc + 4, :], Winb, writes=[Winb])
        sqr = Ring(A, "sq", [128, TG], BF16, 4); rsr = Ring(A, "rs", [128, TG], F32, 3)
        rawr = Ring(A, "raw", [128, 3, TG], F32, 2)
        Tr = Ring(A, "tmp", [128, TG], F32, 3)
        self.norm_x(G_AN + 8 * l, H, HB, sqr, rsr, self.psA)
        for g in range(NG):
            sl = slice(g * TG, (g + 1) * TG)
            for (dst, dstB, c0, nchunk, gcol) in ((cqn, cqnB, 0, 3, gb), (ckvn, ckvnB, 3, 2, gb + 3)):
                raw, rawb = rawr.next()
                p2, p2b = self.psB.next()
                for cc in range(nchunk):
                    ph, pb = self.psA.next()
                    col = (c0 + cc) * 128
                    for k in range(NCH):
                        self.mm(ph[:], Win[:, k, col:col + 128], H[:, k, sl], k == 0, k == NCH - 1,
                                reads=[Winb, HB[g]], writes=[pb])
                    self.copy(raw[:, cc, :], ph[:], reads=[pb], writes=[rawb], eng="dve")
                    sq, sqb = sqr.next()
                    self.act(sq[:], ph[:], AF.Square, reads=[pb], writes=[sqb])
                    self.mm(p2[:], self.onesb, sq[:], cc == 0, cc == nchunk - 1, reads=[sqb, self.CBb], writes=[p2b])
                rs, rsb = rsr.next()
                self.rstd_from_psum(p2, 128, float(nchunk * 128), rs, [p2b], rsb)
                for cc in range(nchunk):
                    gc = self.G[:, gcol + cc:gcol + cc + 1]
                    self.dve(lambda e, dst=dst, cc=cc, raw=raw, gc=gc, rs=rs, sl=sl: e.scalar_tensor_tensor(
                        dst[:, cc, sl], raw[:, cc, :], gc, rs[:], ALU.mult, ALU.mult),
                        reads=[rawb, rsb, self.Gb], writes=[dstB[g]])
            ph, pb = self.psA.next()
            for k in range(NCH):
                self.mm(ph[0:64, :], Win[:, k, 640:704], H[:, k, sl], k == 0, k == NCH - 1, reads=[Winb, HB[g]], writes=[pb])
            self.rope_head(ph, pb, gb + 8, sqr, rsr, Tr, cos, cosb, sin, sinb, rotT, sl, kpe[:, sl], kpeb)
        A.release(m1)
        if self.stop <= 2:
            for nm, t, bb in (("cqn", cqn, cqnB), ("ckvn", ckvn, ckvnB)):
                for g in range(NG):
                    self.dump("%s%d" % (nm, g), t[:, :, g * TG:(g + 1) * TG], bb[g])
            self.dump("kpe", kpe[:], kpeb); self.dump("cos", cos[:], cosb); self.dump("sin", sin[:], sinb)
            A.release(m0); return
        Hd = [dict(qn=A.alloc("qn%d" % i, [128, S], BF16), qr=A.alloc("qr%d" % i, [64, S], BF16),
                   kn=A.alloc("kn%d" % i, [128, S], BF16), V=A.alloc("V%d" % i, [128, 16, 128], BF16),
                   wq=A.alloc("wq%d" % i, [128, 3, 192], BF16), wkv=A.alloc("wkv%d" % i, [128, 2, 256], BF16))
              for i in range(2)]
        attn, attnB = A.alloc("attn", [128, 4, S], BF16, nbufs=4)
        Wor = Ring(A, "Wo", [128, 4, 1024], BF16, 2)
        Pr = Ring(A, "P", [128, TG], BF16, 4)
        Tr = Ring(A, "tmp", [128, TG], F32, 3)
        sqr = Ring(A, "sq", [128, TG], BF16, 4); rsr = Ring(A, "rs", [128, TG], F32, 3)
        wuq = self.d["mla_w_uq"].ap(); wukv = self.d["mla_w_ukv"].ap()
        scale = 1.0 / math.sqrt(192.0)

        def proj(h):
            hd = Hd[h % 2]
            wq, wqb = hd["wq"]; wkv, wkvb = hd["wkv"]
            P.dma("pool", wq[:], wuq[j, :, h * 192:(h + 1) * 192].rearrange("(c p) n -> p c n", p=128), wqb, writes=[wqb])
            P.dma("pool", wkv[:], wukv[j, :, h * 256:(h + 1) * 256].rearrange("(c p) n -> p c n", p=128), wkvb, writes=[wkvb])
            qn, qnb = hd["qn"]; qr, qrb = hd["qr"]; kn, knb = hd["kn"]; V, Vb = hd["V"]
            for g in range(NG):
                sl = slice(g * TG, (g + 1) * TG)
                ph, pb = self.psA.next()
                for c in range(3):
                    self.mm(ph[:], wq[:, c, 0:128], cqn[:, c, sl], c == 0, c == 2, reads=[wqb, cqnB[g]], writes=[pb])
                self.head_norm(ph, pb, 128, gb + 5, sqr, rsr, self.psA,
                               lambda gc, rs, rsb, ph=ph, pb=pb, sl=sl: self.dve(lambda e: e.scalar_tensor_tensor(
                                   qn[:, sl], ph[:], gc, rs[:], ALU.mult, ALU.mult), reads=[pb, rsb, self.Gb], writes=[qnb]))
                ph, pb = self.psA.next()
                for c in range(3):
                    self.mm(ph[0:64, :], wq[:, c, 128:192], cqn[:, c, sl], c == 0, c == 2, reads=[wqb, cqnB[g]], writes=[pb])
                self.rope_head(ph, pb, gb + 6, sqr, rsr, Tr, cos, cosb, sin, sinb, rotT, sl, qr[:, sl], qrb)
                ph, pb = self.psA.next()
                for c in range(2):
                    self.mm(ph[:], wkv[:, c, 0:128], ckvn[:, c, sl], c == 0, c == 1, reads=[wkvb, ckvnB[g]], writes=[pb])
                self.head_norm(ph, pb, 128, gb + 7, sqr, rsr, self.psA,
                               lambda gc, rs, rsb, ph=ph, pb=pb, sl=sl: self.dve(lambda e: e.scalar_tensor_tensor(
                                   kn[:, sl], ph[:], gc, rs[:], ALU.mult, ALU.mult), reads=[pb, rsb, self.Gb], writes=[knb]))
                ph, pb = self.psA.next()
                for t4 in range(4):
                    ts_ = slice(g * TG + t4 * 128, g * TG + (t4 + 1) * 128)
                    for c in range(2):
                        self.mm(ph[:, t4 * 128:(t4 + 1) * 128], ckvn[:, c, ts_], wkv[:, c, 128:256], c == 0, c == 1,
                                reads=[wkvb, ckvnB[g]], writes=[pb])
                self.copy(V[:, g * 4:(g + 1) * 4, :], ph[:].rearrange("p (a b) -> p a b", a=4), reads=[pb], writes=[Vb])

        def attend(h):
            hd = Hd[h % 2]
            qn, qnb = hd["qn"]; qr, qrb = hd["qr"]; kn, knb = hd["kn"]; V, Vb = hd["V"]
            hh = h % 4

            def strip_fn(qt, kt):
                if kt < 4 * qt:
                    return None
                s0 = CF_CSTRIP + (512 * qt - 128 * kt) + 384
                return (self.CF[:, s0:s0 + 512], self.CFb)
            extra = (lambda kt: kpe[:, kt * 128:(kt + 1) * 128], lambda qt: qr[:, qt * TG:(qt + 1) * TG], [kpeb, qrb])
            self.attention(qn, qnb, kn, knb, V, Vb, extra, scale, strip_fn, None, None,
                           attn[:, hh, :], attnB[hh], Pr, Tr, rsr)

        proj(0)
        if self.stop <= 3:
            for nm in ("qn", "qr", "kn", "V"):
                t, bb = Hd[0][nm]
                self.dump(nm, t[:], bb)
            A.release(m0); return
        for h in range(8):
            if h + 1 < 8:
                proj(h + 1)
            attend(h)
            if self.stop <= 4:
                self.dump("attn0", attn[:, 0, :], attnB[0])
                A.release(m0); return
            if h % 4 == 3:
                Wo, Wob = Wor.next()
                self.wo_half("mla_w_o", j, h // 4, attn, attnB, Wo, Wob)
        A.release(m0)

    def rope_head(self, ph, pb, gcol, sqr, rsr, Tr, cos, cosb, sin, sinb, rotT, sl, dst, dstb):
        def fin(gc, rs, rsb):
            xn, xnb = Tr.next()
            self.dve(lambda e: e.scalar_tensor_tensor(xn[0:64, :], ph[0:64, :], gc, rs[0:64, :], ALU.mult, ALU.mult),
                     reads=[pb, rsb, self.Gb], writes=[xnb])
            p3, p3b = self.psA.next()
            self.mm(p3[0:64, :], rotT, xn[0:64, :], True, True, reads=[xnb, self.CFb], writes=[p3b])
            t1, t1b = Tr.next()
            self.dve(lambda e: e.tensor_tensor(t1[0:64, :], xn[0:64, :], cos[:, sl], ALU.mult), reads=[xnb, cosb], writes=[t1b])
            t2, t2b = Tr.next()
            self.dve(lambda e: e.tensor_tensor(t2[0:64, :], p3[0:64, :], sin[:, sl], ALU.mult), reads=[p3b, sinb], writes=[t2b])
            self.dve(lambda e: e.tensor_tensor(dst, t1[0:64, :], t2[0:64, :], ALU.add), reads=[t1b, t2b], writes=[dstb])
        self.head_norm(ph, pb, 64, gcol, sqr, rsr, self.psA, fin)

    def moba(self, j, l, s):
        A = self.A; P = self.P; m0 = A.mark()
        gb = G_MOBA + 2 * j
        H, HB = A.alloc("H", [128, NCH, S], BF16, nbufs=NG)
        m1 = A.mark()
        sqr = Ring(A, "sq", [128, TG], BF16, 4); rsr = Ring(A, "rs", [128, TG], F32, 3)
        self.norm_x(G_AN + 8 * l, H, HB, sqr, rsr, self.psA)
        A.release(m1)
        Hd = [dict(qn=A.alloc("qn%d" % i, [128, S], BF16), kn=A.alloc("kn%d" % i, [128, S], BF16),
                   V=A.alloc("V%d" % i, [128, 16, 128], BF16), w=A.alloc("w%d" % i, [128, NCH, 384], BF16),
                   strip=A.alloc("strip%d" % i, [128, STRIP_W], F32),
                   MT=A.alloc("MT%d" % i, [8, 1024], BF16))
              for i in range(2)]
        q32, q32b = A.alloc("q32", [128, 1024], F32)
        attn, attnB = A.alloc("attn", [128, 4, S], BF16, nbufs=4)
        Wo, Wob = A.alloc("Wo", [128, 4, 1024], BF16)
        Pr = Ring(A, "P", [128, TG], BF16, 4)
        Tr = Ring(A, "tmp", [128, TG], F32, 3)
        sqr = Ring(A, "sq", [128, TG], BF16, 4); rsr = Ring(A, "rs", [128, TG], F32, 3)
        km, kmb = A.alloc("kmean", [128, 8], F32)
        gm, gmb = A.alloc("gm", [128, 64], F32)
        mx, mxb = A.alloc("mx", [128, 64], F32)
        mv, mvb = A.alloc("mv", [128, 64], F32)
        wqkv = self.d["moba_w_qkv"].ap(); stripd = self.d["strip"].ap()
        scale = 1.0 / math.sqrt(128.0)
        indn = lambda n: self.CB[0:8, CB_IND + n * 128: CB_IND + (n + 1) * 128]

        def proj(h):
            hd = Hd[h % 2]
            w, wb = hd["w"]; strip, stripb = hd["strip"]
            for part in range(3):
                srcw = wqkv[j, :, part * 1024 + h * 128: part * 1024 + (h + 1) * 128].rearrange("(c p) n -> p c n", p=128)
                P.dma("pool", w[:, :, part * 128:(part + 1) * 128], srcw, wb, writes=[wb])
            P.dma("sp", strip[:], stripd[h], stripb, writes=[stripb])
            qn, qnb = hd["qn"]; kn, knb = hd["kn"]; V, Vb = hd["V"]
            for g in range(NG):
                sl = slice(g * TG, (g + 1) * TG)
                for part, (dst, dstb, gcol) in enumerate(((qn, qnb, gb), (kn, knb, gb + 1))):
                    ph, pb = self.psA.next()
                    for c in range(NCH):
                        self.mm(ph[:], w[:, c, part * 128:(part + 1) * 128], H[:, c, sl], c == 0, c == NCH - 1,
                                reads=[wb, HB[g]], writes=[pb])

                    def fin(gc, rs, rsb, ph=ph, pb=pb, part=part, dst=dst, dstb=dstb, g=g, sl=sl):
                        if part == 0 and g < 2:
                            self.dve(lambda e: e.scalar_tensor_tensor(dst[:, sl], ph[:], gc, rs[:], ALU.mult, ALU.mult),
                                     reads=[pb, rsb, self.Gb], writes=[dstb])
                            return
                        if part == 0:
                            t = q32[:, (g - 2) * TG:(g - 1) * TG]; tb = q32b
                        else:
                            t_, tb = Tr.next(); t = t_[:]
                        self.dve(lambda e: e.scalar_tensor_tensor(t, ph[:], gc, rs[:], ALU.mult, ALU.mult),
                                 reads=[pb, rsb, self.Gb], writes=[tb])
                        self.copy(dst[:, sl], t, reads=[tb], writes=[dstb], eng="act")
                        if part == 1:
                            self.dve(lambda e: e.tensor_reduce(km[:, 2 * g:2 * g + 2], t.rearrange("p (a b) -> p a b", a=2),
                                                               AX.X, ALU.add), reads=[tb], writes=[kmb])
                    self.head_norm(ph, pb, 128, gcol, sqr, rsr, self.psA, fin)
                ph, pb = self.psA.next()
                for t4 in range(4):
                    ts_ = slice(g * TG + t4 * 128, g * TG + (t4 + 1) * 128)
                    for c in range(NCH):
                        self.mm(ph[:, t4 * 128:(t4 + 1) * 128], H[:, c, ts_], w[:, c, 256:384], c == 0, c == NCH - 1,
                                reads=[wb, HB[g]], writes=[pb])
                self.copy(V[:, g * 4:(g + 1) * 4, :], ph[:].rearrange("p (a b) -> p a b", a=4), reads=[pb], writes=[Vb])
            MT, MTb = hd["MT"]
            gh, gpb = self.psA.next()
            for i in range(8):
                self.mm(gh[:, i * 8:(i + 1) * 8], q32[:, i * 128:(i + 1) * 128], km[:, :], True, True,
                        reads=[q32b, kmb], writes=[gpb])
            negm = self.CF[:, CF_NEGM:CF_NEGM + 64]; past = self.CF[:, CF_PAST:CF_PAST + 64]
            self.dve(lambda e: e.tensor_tensor(gm[:], gh[:, 0:64], negm, ALU.add), reads=[gpb, self.CFb], writes=[gmb])
            for i in range(8):
                self.dve(lambda e, i=i: e.max(mx[:, i * 8:(i + 1) * 8], gm[:, i * 8:(i + 1) * 8]), reads=[gmb], writes=[mxb])
            for i in range(8):
                self.dve(lambda e, i=i: e.tensor_scalar(mv[:, i * 8:(i + 1) * 8], gm[:, i * 8:(i + 1) * 8],
                                                        mx[:, i * 8 + 2:i * 8 + 3], 1.0, ALU.is_ge, ALU.subtract),
                         reads=[gmb, mxb], writes=[mvb])
            self.dve(lambda e: e.tensor_tensor(mv[:], mv[:], past, ALU.mult), reads=[mvb, self.CFb], writes=[mvb])
            for half in range(2):
                th, tpb = self.psA.next()
                for ii in range(4):
                    i = half * 4 + ii
                    self.tr(th[0:8, ii * 128:(ii + 1) * 128], mv[:, i * 8:(i + 1) * 8], reads=[mvb], writes=[tpb])
                self.copy(MT[:, half * 512:(half + 1) * 512], th[0:8, :], reads=[tpb], writes=[MTb], eng="act")

        def attend(h):
            hd = Hd[h % 2]
            qn, qnb = hd["qn"]; kn, knb = hd["kn"]; V, Vb = hd["V"]
            strip, stripb = hd["strip"]; MT, MTb = hd["MT"]
            hh = h % 4

            def strip_fn(qt, kt):
                dlt = 512 * qt - 128 * kt
                if dlt >= 1024:
                    return None
                s0 = dlt + 384
                return (strip[:, s0:s0 + 512], stripb)

            def mask_fn(qt, kt):
                n = kt // 2
                if qt < 2 or n > 2 * qt:
                    return None
                return (indn(n), MT[:, (qt - 2) * TG:(qt - 1) * TG], [self.CBb, MTb])
            far = (strip[:, STRIP_W - 1:STRIP_W], stripb)
            self.attention(qn, qnb, kn, knb, V, Vb, None, scale, strip_fn, mask_fn, far,
                           attn[:, hh, :], attnB[hh], Pr, Tr, rsr)

        proj(0)
        for h in range(8):
            if h + 1 < 8:
                proj(h + 1)
            attend(h)
            if h % 4 == 3:
                self.wo_half("moba_w_o", j, h // 4, attn, attnB, Wo, Wob)
        A.release(m0)

    def build(self):
        for s in range(self.nseq):
            self.load_x(s)
            for (kind, j, l) in self.plan:
                if kind == "mla":
                    self.mla(j, l, s)
                elif kind == "moba":
                    self.moba(j, l, s)
                else:
                    self.mlp(l)
            self.store_x(s)
        self.P.emit()
        return self.nc


_HOST_CACHE = {}


def _prep_shared(inp):
    cf, cb = _host_consts()
    sh = dict(cf=cf, cb=cb, gains=_host_gains(inp), strip=_host_strip(inp["rel_bias_table"]))
    for k in ("mla_w_in", "mla_w_uq", "mla_w_ukv", "mla_w_o", "moba_w_qkv", "moba_w_o", "mlp_w_in", "mlp_w_out"):
        sh[k] = np.ascontiguousarray(inp[k], dtype=np.float32)
    return sh


def kernel(**inputs):
    x = np.asarray(inputs["x"], np.float32); pos = np.asarray(inputs["positions"], np.int32)
    B = x.shape[0]
    ncores = 8
    nseq = B // ncores
    sh = _prep_shared(inputs)
    nc = KB(nseq=nseq).build()
    in_maps = []
    for c in range(ncores):
        m = dict(sh)
        m["x"] = np.ascontiguousarray(x[c * nseq:(c + 1) * nseq])
        m["pos"] = np.ascontiguousarray(pos[c * nseq:(c + 1) * nseq])
        in_maps.append(m)
    res = run_bass_kernel_spmd(nc, in_maps, core_ids=list(range(ncores)))
    out = np.concatenate([np.asarray(r["out"], np.float32) for r in res.results], axis=0)
    return out
```

```python
import math
import numpy as np
from contextlib import ExitStack
import concourse.bass as bass
import concourse.mybir as mybir
from concourse.bass_utils import run_bass_kernel_spmd

F32 = mybir.dt.float32; BF16 = mybir.dt.bfloat16; I32 = mybir.dt.int32
ALU = mybir.AluOpType; AF = mybir.ActivationFunctionType; AX = mybir.AxisListType

SAME_ENGINE_SYNC = True
COMPUTE = ("pe", "act", "dve", "pool")
S = 2048; D = 1024; NCH = 8; TG = 512; NG = 4; DFF = 4096
EPS = 1e-6
NEG = -30000.0
SB_BASE = 16512; SB_END = 229344


class Buf:
    __slots__ = ("name", "writers", "readers", "dsem", "dcnt", "excl")

    def __init__(self, name, excl=False):
        self.name = name; self.writers = []; self.readers = []; self.dsem = None; self.dcnt = 0
        self.excl = excl


class Op:
    __slots__ = ("eng", "idx", "fn", "deps", "marked", "dma")

    def __init__(self, eng, idx, fn, dma=None):
        self.eng = eng; self.idx = idx; self.fn = fn; self.deps = []; self.marked = False; self.dma = dma


class Prog:
    def __init__(self, nc):
        self.nc = nc
        self.streams = {e: [] for e in ("pe", "act", "dve", "pool", "sp")}
        self.seen = {e: {} for e in self.streams}
        self.dbufs = []

    def _resolve(self, eng, toks):
        out = []
        seen = self.seen[eng]
        for t in toks:
            if t[0] == "c":
                _, E, j = t
                if E == eng and (eng == "pe" or not SAME_ENGINE_SYNC):
                    continue
                if seen.get(E, -1) >= j:
                    continue
                seen[E] = j
                self.streams[E][j].marked = True
                out.append(t)
            else:
                b = t[1]
                v = b.dcnt
                key = id(b)
                if seen.get(key, -1) >= v:
                    continue
                seen[key] = v
                out.append(("d", b, v))
        return out

    def _record(self, eng, op, tok, reads, writes):
        toks = []
        def other(ts):
            return [t for t in ts if not (t[0] == "c" and t[1] == eng)]
        for b in reads:
            toks.extend(b.writers)
            if b.excl:
                toks.extend(other(b.readers))
        for b in writes:
            toks.extend(b.writers); toks.extend(other(b.readers))
        op.deps = self._resolve(eng, toks)
        for b in reads:
            b.readers.append(tok)
        for b in writes:
            b.writers = [tok]; b.readers = []
        self.streams[eng].append(op)

    def op(self, eng, fn, reads=(), writes=()):
        o = Op(eng, len(self.streams[eng]), fn)
        self._record(eng, o, ("c", eng, o.idx), reads, writes)

    def dma(self, q, out_ap, in_ap, sembuf, reads=(), writes=()):
        if sembuf.dsem is None:
            sembuf.dsem = True
            self.dbufs.append(sembuf)
        o = Op(q, len(self.streams[q]), None, dma=(out_ap, in_ap, sembuf))
        self._record(q, o, ("d", sembuf, sembuf.dcnt + 16), reads, writes)
        sembuf.dcnt += 16

    def emit(self):
        nc = self.nc
        with ExitStack() as es:
            esem = {e: es.enter_context(nc.semaphore("sem_" + e)) for e in COMPUTE}
            for i, b in enumerate(self.dbufs):
                b.dsem = es.enter_context(nc.semaphore("ds%d" % i))
            val = {}
            for e in COMPUTE:
                c = 0
                for o in self.streams[e]:
                    if o.marked:
                        c += 1
                    val[(e, o.idx)] = c
            block = es.enter_context(nc.Block())

            def run(engobj, name):
                for o in self.streams[name]:
                    for t in o.deps:
                        if t[0] == "c":
                            engobj.wait_ge(esem[t[1]], val[(t[1], t[2])])
                        else:
                            engobj.wait_ge(t[1].dsem, t[2])
                    if o.dma is not None:
                        out_ap, in_ap, sb = o.dma
                        engobj.dma_start(out=out_ap, in_=in_ap).then_inc(sb.dsem, 16)
                    else:
                        ins = o.fn(engobj)
                        if o.marked:
                            ins.then_inc(esem[name], 1)
                if name == "sp":
                    for b in self.dbufs:
                        engobj.wait_ge(b.dsem, b.dcnt)

            @block.tensor
            def _(e): run(e, "pe")

            @block.scalar
            def _(e): run(e, "act")

            @block.vector
            def _(e): run(e, "dve")

            @block.gpsimd
            def _(e): run(e, "pool")

            @block.sync
            def _(e): run(e, "sp")


def _dsize(dt):
    return 2 if dt == BF16 else 4


class Arena:
    def __init__(self, nc, base, end):
        self.nc = nc; self.base = base; self.end = end; self.top = base; self.hist = []; self.n = 0

    def alloc(self, name, shape, dtype, nbufs=None):
        nbytes = int(np.prod(shape[1:])) * _dsize(dtype)
        start = (self.top + 31) // 32 * 32
        assert start + nbytes <= self.end, "SBUF arena overflow: %s needs %d at %d (end %d)" % (name, nbytes, start, self.end)
        self.n += 1
        h = self.nc.alloc_sbuf_tensor_at("%s_%d" % (name, self.n), list(shape), dtype, offset=start)
        inh = []
        for (s0, e0, ob) in self.hist:
            if s0 < start + nbytes and e0 > start:
                inh.extend(ob.writers); inh.extend(ob.readers)
        bufs = []
        for i in range(nbufs or 1):
            b = Buf(name if nbufs is None else "%s.%d" % (name, i))
            b.writers.extend(inh)
            bufs.append(b)
        for b in bufs:
            self.hist.append((start, start + nbytes, b))
        self.top = start + nbytes
        return (h, bufs[0]) if nbufs is None else (h, bufs)

    def mark(self):
        return self.top

    def release(self, m):
        self.top = m


class Ring:
    def __init__(self, arena, name, shape, dtype, n):
        self.items = [arena.alloc("%s%d" % (name, i), shape, dtype) for i in range(n)]
        self.i = 0

    def next(self):
        it = self.items[self.i % len(self.items)]
        self.i += 1
        return it


class PsRing:
    def __init__(self, items):
        self.items = items; self.i = 0

    def next(self):
        it = self.items[self.i % len(self.items)]
        self.i += 1
        return it


CF_ID = 0; CF_ROT = 128; CF_INVF = 192; CF_NEGM = 193; CF_PAST = 257; CF_CSTRIP = 321; CF_W = 321 + 896
CB_ID = 0; CB_ONES = 128; CB_IND = 256; CB_W = 256 + 1024
STRIP_W = 1920
G_AN = 0; G_MN = 32; G_MLA = 64; G_MOBA = 82; G_W = 86


def _t5_bucket_np(d):
    n = np.maximum(d, 0)
    nf = np.maximum(n, 1).astype(np.float32)
    large = 16 + (np.log(nf / np.float32(16)) / np.float32(math.log(64.0)) * np.float32(16)).astype(np.int32)
    large = np.minimum(large, 31)
    return np.where(n < 16, n, large)


def _host_consts():
    cf = np.zeros((128, CF_W), np.float32)
    cf[:, CF_ID:CF_ID + 128] = np.eye(128, dtype=np.float32)
    rotT = np.zeros((64, 64), np.float32)
    for m in range(32):
        rotT[m + 32, m] = -1.0
        rotT[m, m + 32] = 1.0
    cf[:64, CF_ROT:CF_ROT + 64] = rotT
    inv = (10000.0 ** (-np.arange(0, 64, 2, dtype=np.float32) / np.float32(64))).astype(np.float32)
    cf[:32, CF_INVF] = inv; cf[32:64, CF_INVF] = inv
    for i in range(8):
        own = 4 + i // 2
        for n in range(8):
            cf[:, CF_NEGM + i * 8 + n] = -1e30 if n >= own else 0.0
            cf[:, CF_PAST + i * 8 + n] = 30000.0 if n < own else 0.0
    k = np.arange(128)[:, None]; j = np.arange(896)[None, :]
    cf[:, CF_CSTRIP:CF_CSTRIP + 896] = np.where(j - k - 384 >= 0, 0.0, NEG)
    cb = np.zeros((128, CB_W), np.float32)
    cb[:, CB_ID:CB_ID + 128] = np.eye(128, dtype=np.float32)
    cb[:, CB_ONES:CB_ONES + 128] = 1.0
    for n in range(8):
        cb[n, CB_IND + n * 128: CB_IND + (n + 1) * 128] = 1.0
    return cf, cb


def _host_gains(inp):
    g = np.zeros((128, G_W), np.float32)

    def put(col, vec):
        vec = np.asarray(vec, np.float32)
        n = vec.shape[0]
        if n >= 128:
            c = n // 128
            g[:, col:col + c] = vec.reshape(c, 128).T
        else:
            g[:n, col] = vec
    for l in range(4):
        put(G_AN + 8 * l, inp["attn_norm"][l]); put(G_MN + 8 * l, inp["mlp_norm"][l])
    for j in range(2):
        b = G_MLA + 9 * j
        put(b, inp["mla_q_a_norm"][j]); put(b + 3, inp["mla_kv_a_norm"][j])
        put(b + 5, inp["mla_q_nope_norm"][j]); put(b + 6, inp["mla_q_rope_norm"][j])
        put(b + 7, inp["mla_k_nope_norm"][j]); put(b + 8, inp["mla_k_rope_norm"][j])
        b2 = G_MOBA + 2 * j
        put(b2, inp["moba_q_norm"][j]); put(b2 + 1, inp["moba_k_norm"][j])
    return g


def _host_strip(table):
    k = np.arange(128)[:, None]; j = np.arange(STRIP_W)[None, :]
    d = j - k - 384
    bidx = _t5_bucket_np(d)
    t = np.asarray(table, np.float32)
    strip = np.transpose(t[bidx], (2, 0, 1)).copy()
    strip[:, d < 0] = NEG
    return np.ascontiguousarray(strip, dtype=np.float32)


class KB:
    def __init__(self, nseq=2, plan=None, debug_out=False, stop=99):
        self.nseq = nseq; self.stop = stop
        self.plan = plan if plan is not None else [("mla", 0, 0), ("mlp", 0, 0), ("moba", 0, 1), ("mlp", 1, 1),
                                                    ("mla", 1, 2), ("mlp", 2, 2), ("moba", 1, 3), ("mlp", 3, 3)]
        nc = self.nc = bass.Bass("TRN2", target_bir_lowering=False)
        self.P = Prog(nc)
        dt = nc.dram_tensor
        self.d = {}
        def inp(name, shape, dtype=F32):
            self.d[name] = dt(name, list(shape), dtype, kind="ExternalInput")
        inp("x", [nseq, S, D]); self.d["pos"] = None
        inp("cf", [128, CF_W]); inp("cb", [128, CB_W]); inp("gains", [128, G_W])
        kinds = set(k for (k, _, _) in self.plan)
        if "mla" in kinds:
            inp("pos", [nseq, S], I32)
            inp("mla_w_in", [2, 1024, 704]); inp("mla_w_uq", [2, 384, 1536]); inp("mla_w_ukv", [2, 256, 2048])
            inp("mla_w_o", [2, 1024, 1024])
        else:
            del self.d["pos"]
        if "moba" in kinds:
            inp("strip", [8, 128, STRIP_W]); inp("moba_w_qkv", [2, 1024, 3072]); inp("moba_w_o", [2, 1024, 1024])
        if "mlp" in kinds:
            inp("mlp_w_in", [4, 1024, 4096]); inp("mlp_w_out", [4, 4096, 1024])
        self.d["out"] = dt("out", [nseq, S, D], F32, kind="ExternalOutput")
        off = SB_BASE
        def static(name, shape, dtype):
            nonlocal off
            nbytes = int(np.prod(shape[1:])) * _dsize(dtype)
            h = nc.alloc_sbuf_tensor_at(name, list(shape), dtype, offset=off)
            off = (off + nbytes + 31) // 32 * 32
            return h
        self.CF = static("CF", [128, CF_W], F32); self.CFb = Buf("CF")
        self.CB = static("CB", [128, CB_W], BF16); self.CBb = Buf("CB")
        self.G = static("G", [128, G_W], F32); self.Gb = Buf("G")
        self.X = static("X", [128, NCH, S], F32)
        self.XB = [[Buf("X%d_%d" % (c, g)) for g in range(NG)] for c in range(NCH)]
        self.A = Arena(nc, off, SB_END)
        self.PS = []
        for i in range(8):
            h = nc.alloc_psum_tensor("ps%d" % i, [128, 512], F32)
            self.PS.append((h, Buf("ps%d" % i, excl=True)))
        self.psAll = PsRing(self.PS)
        self.psA = PsRing(self.PS[0:4]); self.psB = PsRing(self.PS[4:8])
        P = self.P
        P.dma("sp", self.CF[:], self.d["cf"].ap(), self.CFb, writes=[self.CFb])
        P.dma("pool", self.CB[:], self.d["cb"].ap(), self.CBb, writes=[self.CBb])
        P.dma("sp", self.G[:], self.d["gains"].ap(), self.Gb, writes=[self.Gb])
        self.ident = self.CF[:, CF_ID:CF_ID + 128]
        self.onesb = self.CB[:, CB_ONES:CB_ONES + 128]
        self.evac_i = 0

    def dump(self, name, ap, buf):
        shape = [int(v) for v in ap.shape]
        dtn = self.nc.dram_tensor("dbg_" + name, shape, F32, kind="ExternalOutput")
        self.d["dbg_" + name] = dtn
        self.P.dma("pool", dtn.ap(), ap, buf, reads=[buf])

    def mm(self, out, lhsT, rhs, start, stop, reads, writes):
        self.P.op("pe", lambda e: e.matmul(out, lhsT=lhsT, rhs=rhs, start=start, stop=stop), reads=reads, writes=writes)

    def tr(self, out, in_, reads, writes):
        ident = self.ident[0:in_.shape[0], 0:in_.shape[0]]
        self.P.op("pe", lambda e: e.transpose(out, in_, ident), reads=list(reads) + [self.CFb], writes=writes)

    def act(self, out, in_, func, reads, writes, **kw):
        self.P.op("act", lambda e: e.activation(out, in_, func, **kw), reads=reads, writes=writes)

    def copy(self, out, in_, reads, writes, eng=None):
        if eng is None:
            eng = "act" if (self.evac_i % 2 == 0) else "dve"
            self.evac_i += 1
        if eng == "act":
            self.P.op("act", lambda e: e.copy(out, in_), reads=reads, writes=writes)
        else:
            self.P.op("dve", lambda e: e.tensor_copy(out, in_), reads=reads, writes=writes)

    def dve(self, fn, reads, writes):
        self.P.op("dve", fn, reads=reads, writes=writes)

    def rstd_from_psum(self, ps, npart, n_total, rs, reads, rsb):
        self.act(rs[0:npart, :], ps[0:npart, :], AF.Sqrt, reads=reads, writes=[rsb], bias=EPS, scale=1.0 / n_total)
        self.dve(lambda e: e.reciprocal(rs[0:npart, :], rs[0:npart, :]), reads=[rsb], writes=[rsb])

    def load_x(self, s):
        A = self.A; m = A.mark()
        ring = Ring(A, "xin", [128, D], F32, 2)
        xd = self.d["x"].ap()
        for g in range(NG):
            banks = [self.psAll.next() for _ in range(NCH)]
            for j in range(4):
                tt = g * 4 + j
                st, sb = ring.next()
                self.P.dma("sp", st[:], xd[s, tt * 128:(tt + 1) * 128, :], sb, writes=[sb])
                for c in range(NCH):
                    ph, pb = banks[c]
                    self.tr(ph[:, j * 128:(j + 1) * 128], st[:, c * 128:(c + 1) * 128], reads=[sb], writes=[pb])
            for c in range(NCH):
                ph, pb = banks[c]
                self.copy(self.X[:, c, g * TG:(g + 1) * TG], ph[:], reads=[pb], writes=[self.XB[c][g]])
        A.release(m)

    def store_x(self, s):
        A = self.A; m = A.mark()
        ring = Ring(A, "xout", [128, D], F32, 2)
        od = self.d["out"].ap()
        for tt in range(16):
            g = tt // 4
            st, sb = ring.next()
            for half in range(2):
                ph, pb = self.psAll.next()
                for cc in range(4):
                    c = half * 4 + cc
                    self.tr(ph[:, cc * 128:(cc + 1) * 128], self.X[:, c, tt * 128:(tt + 1) * 128],
                            reads=[self.XB[c][g]], writes=[pb])
                self.copy(st[:, half * 512:(half + 1) * 512], ph[:], reads=[pb], writes=[sb])
            self.P.dma("sp", od[s, tt * 128:(tt + 1) * 128, :], st[:], sb, reads=[sb])
        A.release(m)

    def norm_x(self, gcol, H, HB, sqr, rsr, ps_ring):
        for g in range(NG):
            ph, pb = ps_ring.next()
            sl = slice(g * TG, (g + 1) * TG)
            for c in range(NCH):
                sq, sqb = sqr.next()
                self.act(sq[:], self.X[:, c, sl], AF.Square, reads=[self.XB[c][g]], writes=[sqb])
                self.mm(ph[:], self.onesb, sq[:], c == 0, c == NCH - 1, reads=[sqb, self.CBb], writes=[pb])
            rs, rsb = rsr.next()
            self.rstd_from_psum(ph, 128, float(D), rs, [pb], rsb)
            for c in range(NCH):
                gc = self.G[:, gcol + c:gcol + c + 1]
                self.dve(lambda e, c=c, gc=gc, rs=rs, sl=sl: e.scalar_tensor_tensor(
                    H[:, c, sl], self.X[:, c, sl], gc, rs[:], ALU.mult, ALU.mult),
                    reads=[self.XB[c][g], rsb, self.Gb], writes=[HB[g]])

    def mlp(self, l):
        A = self.A; P = self.P; m0 = A.mark()
        H, HB = A.alloc("H", [128, NCH, S], BF16, nbufs=NG)
        m1 = A.mark()
        sqr = Ring(A, "sq", [128, TG], BF16, 4); rsr = Ring(A, "rs", [128, TG], F32, 2)
        self.norm_x(G_MN + 8 * l, H, HB, sqr, rsr, self.psAll)
        A.release(m1)
        W1r = Ring(A, "W1", [128, 8, 1024], BF16, 2); W2r = Ring(A, "W2", [128, 8, 1024], BF16, 2)
        Ar = Ring(A, "Aa", [128, 8, TG], BF16, 2); Rr = Ring(A, "Rr", [128, TG], F32, 3)
        w_in = self.d["mlp_w_in"].ap(); w_out = self.d["mlp_w_out"].ap()

        def load_w(q):
            (w1, b1) = W1r.next(); (w2, b2) = W2r.next()
            src1 = w_in[l, :, q * 1024:(q + 1) * 1024].rearrange("(c p) n -> p c n", p=128)
            src2 = w_out[l, q * 1024:(q + 1) * 1024, :].rearrange("(c p) n -> p c n", p=128)
            for cc in range(0, 8, 2):
                P.dma("pool", w1[:, cc:cc + 2, :], src1[:, cc:cc + 2, :], b1, writes=[b1])
            for cc in range(0, 8, 2):
                P.dma("pool", w2[:, cc:cc + 2, :], src2[:, cc:cc + 2, :], b2, writes=[b2])
            return (w1, b1, w2, b2)

        def up(W, g):
            w1, b1, _, _ = W
            a, ab = Ar.next()
            sl = slice(g * TG, (g + 1) * TG)
            for f in range(8):
                ph, pb = self.psAll.next()
                for k in range(NCH):
                    self.mm(ph[:], w1[:, k, f * 128:(f + 1) * 128], H[:, k, sl], k == 0, k == NCH - 1,
                            reads=[b1, HB[g]], writes=[pb])
                r, rb = Rr.next()
                self.act(r[:], ph[:], AF.Relu, reads=[pb], writes=[rb])
                self.dve(lambda e, a=a, f=f, r=r: e.tensor_tensor(a[:, f, :], r[:], r[:], ALU.mult), reads=[rb], writes=[ab])
            return (a, ab)

        def down(W, g, at):
            _, _, w2, b2 = W
            a, ab = at
            sl = slice(g * TG, (g + 1) * TG)
            for mch in range(NCH):
                ph, pb = self.psAll.next()
                for f in range(8):
                    self.mm(ph[:], w2[:, f, mch * 128:(mch + 1) * 128], a[:, f, :], f == 0, f == 7,
                            reads=[b2, ab], writes=[pb])
                xs = self.X[:, mch, sl]
                self.dve(lambda e, xs=xs, ph=ph: e.tensor_tensor(xs, xs, ph[:], ALU.add),
                         reads=[pb, self.XB[mch][g]], writes=[self.XB[mch][g]])

        Ws = {0: load_w(0)}
        steps = [(q, g) for q in range(4) for g in range(NG)]
        pend = None
        for i, (q, g) in enumerate(steps):
            at = up(Ws[q], g)
            if pend is not None:
                down(*pend)
            pend = (Ws[q], g, at)
            if g == 0 and q + 1 < 4:
                Ws[q + 1] = load_w(q + 1)
        down(*pend)
        A.release(m0)

    def rope_tables(self, s, cos, cosb, sin, sinb):
        A = self.A; m = A.mark()
        pi_, pib = A.alloc("posi", [64, S], I32)
        ang, angb = A.alloc("ang", [64, S], F32)
        kf, kfb = A.alloc("kf", [64, S], F32)
        r, rb = A.alloc("r", [64, S], F32)
        shh, shb = A.alloc("shh", [64, S], F32)
        chh, chb = A.alloc("chh", [64, S], F32)
        src = bass.AP(self.d["pos"], s * S, [[0, 64], [1, S]])
        self.P.dma("sp", pi_[:], src, pib, writes=[pib])
        self.dve(lambda e: e.tensor_copy(ang[:], pi_[:]), reads=[pib], writes=[angb])
        invf = self.CF[0:64, CF_INVF:CF_INVF + 1]
        self.dve(lambda e: e.tensor_scalar(ang[:], ang[:], invf, None, ALU.mult), reads=[angb, self.CFb], writes=[angb])
        self.dve(lambda e: e.tensor_scalar(pi_[:], ang[:], 1.0 / (2.0 * math.pi), None, ALU.mult), reads=[angb], writes=[pib])
        self.dve(lambda e: e.tensor_copy(kf[:], pi_[:]), reads=[pib], writes=[kfb])
        self.dve(lambda e: e.scalar_tensor_tensor(r[:], kf[:], -2.0 * math.pi, ang[:], ALU.mult, ALU.add),
                 reads=[kfb, angb], writes=[rb])
        sh = 0.999999
        self.act(shh[:], r[:], AF.Sin, reads=[rb], writes=[shb], scale=0.5 * sh)
        self.act(chh[:], r[:], AF.Sin, reads=[rb], writes=[chb], bias=0.5 * math.pi * sh, scale=-0.5 * sh)
        self.dve(lambda e: e.scalar_tensor_tensor(sin[:], shh[:], 2.0, chh[:], ALU.mult, ALU.mult), reads=[shb, chb], writes=[sinb])
        self.dve(lambda e: e.tensor_tensor(kf[:], shh[:], shh[:], ALU.mult), reads=[shb], writes=[kfb])
        self.dve(lambda e: e.tensor_scalar(cos[:], kf[:], -2.0, 1.0, ALU.mult, ALU.add), reads=[kfb], writes=[cosb])
        A.release(m)

    def head_norm(self, ph, pb, npart, gcol, sqr, rsr, psr, out_fn):
        sq, sqb = sqr.next()
        self.act(sq[0:npart, :], ph[0:npart, :], AF.Square, reads=[pb], writes=[sqb])
        p2, p2b = psr.next()
        self.mm(p2[0:npart, :], self.onesb[0:npart, 0:npart], sq[0:npart, :], True, True, reads=[sqb, self.CBb], writes=[p2b])
        rs, rsb = rsr.next()
        self.rstd_from_psum(p2, npart, float(npart), rs, [p2b], rsb)
        gc = self.G[0:npart, gcol:gcol + 1]
        out_fn(gc, rs, rsb)

    def attention(self, qT, qb, kT, kb, V, Vb, extra_qk, scale, strip_fn, mask_fn, far_bias,
                  outT, outb, Pr, Tr, rsr):
        LOOK = 2
        for qt in range(4):
            qs = slice(qt * TG, (qt + 1) * TG)
            oh, ob = self.psB.next(); lh, lb = self.psB.next()
            nk = 4 * qt + 4

            def qk(kt):
                ks = slice(kt * 128, (kt + 1) * 128)
                sh_, sb_ = self.psA.next()
                mk = mask_fn(qt, kt) if mask_fn is not None else None
                last_qk = (extra_qk is None) and (mk is None)
                self.mm(sh_[:], kT[:, ks], qT[:, qs], True, last_qk, reads=[kb, qb], writes=[sb_])
                if extra_qk is not None:
                    l2, r2, rd2 = extra_qk
                    self.mm(sh_[:], l2(kt), r2(qt), False, mk is None, reads=rd2, writes=[sb_])
                if mk is not None:
                    ml, mr, mrd = mk
                    self.mm(sh_[:], ml, mr, False, True, reads=mrd, writes=[sb_])
                p, pbuf = Pr.next()
                st = strip_fn(qt, kt)
                if st is not None:
                    sap, sbuf = st
                    t, tb = Tr.next()
                    self.dve(lambda e, t=t, sh_=sh_, sap=sap: e.scalar_tensor_tensor(
                        t[:], sh_[:], scale, sap, ALU.mult, ALU.add), reads=[sb_, sbuf], writes=[tb])
                    self.act(p[:], t[:], AF.Exp, reads=[tb], writes=[pbuf])
                elif far_bias is not None:
                    fb, fbb = far_bias
                    self.act(p[:], sh_[:], AF.Exp, reads=[sb_, fbb], writes=[pbuf], bias=fb, scale=scale)
                else:
                    self.act(p[:], sh_[:], AF.Exp, reads=[sb_], writes=[pbuf], scale=scale)
                return (kt, p, pbuf)

            def pv(item):
                kt, p, pbuf = item
                self.mm(oh[:], V[:, kt, :], p[:], kt == 0, kt == nk - 1, reads=[Vb, pbuf], writes=[ob])
                self.mm(lh[:], self.onesb, p[:], kt == 0, kt == nk - 1, reads=[pbuf, self.CBb], writes=[lb])

            pend = []
            for kt in range(nk):
                pend.append(qk(kt))
                if len(pend) > LOOK:
                    pv(pend.pop(0))
            while pend:
                pv(pend.pop(0))
            rl, rlb = rsr.next()
            self.dve(lambda e, rl=rl, lh=lh: e.reciprocal(rl[:], lh[:]), reads=[lb], writes=[rlb])
            self.dve(lambda e, rl=rl, oh=oh, qs=qs: e.tensor_tensor(outT[:, qs], oh[:], rl[:], ALU.mult),
                     reads=[ob, rlb], writes=[outb])

    def wo_half(self, wname, j, half, attn, attnb, Wo, Wob):
        wd = self.d[wname].ap()
        src = wd[j, half * 512:(half + 1) * 512, :].rearrange("(c p) n -> p c n", p=128)
        for hh in range(4):
            self.P.dma("pool", Wo[:, hh:hh + 1, :], src[:, hh:hh + 1, :], Wob, writes=[Wob])
        for g in range(NG):
            sl = slice(g * TG, (g + 1) * TG)
            for mch in range(NCH):
                ph, pb = self.psA.next()
                for hh in range(4):
                    self.mm(ph[:], Wo[:, hh, mch * 128:(mch + 1) * 128], attn[:, hh, sl], hh == 0, hh == 3,
                            reads=[Wob, attnb[hh]], writes=[pb])
                xs = self.X[:, mch, sl]
                self.dve(lambda e, xs=xs, ph=ph: e.tensor_tensor(xs, xs, ph[:], ALU.add),
                         reads=[pb, self.XB[mch][g]], writes=[self.XB[mch][g]])

    def mla(self, j, l, s):
        A = self.A; P = self.P; m0 = A.mark()
        gb = G_MLA + 9 * j
        cqn, cqnB = A.alloc("cqn", [128, 3, S], BF16, nbufs=NG)
        ckvn, ckvnB = A.alloc("ckvn", [128, 2, S], BF16, nbufs=NG)
        kpe, kpeb = A.alloc("kpe", [64, S], BF16)
        cos, cosb = A.alloc("cos", [64, S], F32); sin, sinb = A.alloc("sin", [64, S], F32)
        self.rope_tables(s, cos, cosb, sin, sinb)
        rotT = self.CF[0:64, CF_ROT:CF_ROT + 64]
        if self.stop <= 1:
            A.release(m0); return
        m1 = A.mark()
        H, HB = A.alloc("H", [128, NCH, S], BF16, nbufs=NG)
        Win, Winb = A.alloc("Win", [128, NCH, 704], BF16)
        src = self.d["mla_w_in"].ap()[j].rearrange("(c p) n -> p c n", p=128)
        for cc in range(0, 8, 4):
            P.dma("pool", Win[:, cc:cc + 4, :], src[:, cc:cc + 4, :], Winb, writes=[Winb])
        sqr = Ring(A, "sq", [128, TG], BF16, 4); rsr = Ring(A, "rs", [128, TG], F32, 3)
        rawr = Ring(A, "raw", [128, 3, TG], F32, 2)
        Tr = Ring(A, "tmp", [128, TG], F32, 3)
        self.norm_x(G_AN + 8 * l, H, HB, sqr, rsr, self.psA)
        for g in range(NG):
            sl = slice(g * TG, (g + 1) * TG)
            for (dst, dstB, c0, nchunk, gcol) in ((cqn, cqnB, 0, 3, gb), (ckvn, ckvnB, 3, 2, gb + 3)):
                raw, rawb = rawr.next()
                p2, p2b = self.psB.next()
                for cc in range(nchunk):
                    ph, pb = self.psA.next()
                    col = (c0 + cc) * 128
                    for k in range(NCH):
                        self.mm(ph[:], Win[:, k, col:col + 128], H[:, k, sl], k == 0, k == NCH - 1,
                                reads=[Winb, HB[g]], writes=[pb])
                    self.copy(raw[:, cc, :], ph[:], reads=[pb], writes=[rawb], eng="dve")
                    sq, sqb = sqr.next()
                    self.act(sq[:], ph[:], AF.Square, reads=[pb], writes=[sqb])
                    self.mm(p2[:], self.onesb, sq[:], cc == 0, cc == nchunk - 1, reads=[sqb, self.CBb], writes=[p2b])
                rs, rsb = rsr.next()
                self.rstd_from_psum(p2, 128, float(nchunk * 128), rs, [p2b], rsb)
                for cc in range(nchunk):
                    gc = self.G[:, gcol + cc:gcol + cc + 1]
                    self.dve(lambda e, dst=dst, cc=cc, raw=raw, gc=gc, rs=rs, sl=sl: e.scalar_tensor_tensor(
                        dst[:, cc, sl], raw[:, cc, :], gc, rs[:], ALU.mult, ALU.mult),
                        reads=[rawb, rsb, self.Gb], writes=[dstB[g]])
            ph, pb = self.psA.next()
            for k in range(NCH):
                self.mm(ph[0:64, :], Win[:, k, 640:704], H[:, k, sl], k == 0, k == NCH - 1, reads=[Winb, HB[g]], writes=[pb])
            self.rope_head(ph, pb, gb + 8, sqr, rsr, Tr, cos, cosb, sin, sinb, rotT, sl, kpe[:, sl], kpeb)
        A.release(m1)
        if self.stop <= 2:
            for nm, t, bb in (("cqn", cqn, cqnB), ("ckvn", ckvn, ckvnB)):
                for g in range(NG):
                    self.dump("%s%d" % (nm, g), t[:, :, g * TG:(g + 1) * TG], bb[g])
            self.dump("kpe", kpe[:], kpeb); self.dump("cos", cos[:], cosb); self.dump("sin", sin[:], sinb)
            A.release(m0); return
        Hd = [dict(qn=A.alloc("qn%d" % i, [128, S], BF16), qr=A.alloc("qr%d" % i, [64, S], BF16),
                   kn=A.alloc("kn%d" % i, [128, S], BF16), V=A.alloc("V%d" % i, [128, 16, 128], BF16),
                   wq=A.alloc("wq%d" % i, [128, 3, 192], BF16), wkv=A.alloc("wkv%d" % i, [128, 2, 256], BF16))
              for i in range(2)]
        attn, attnB = A.alloc("attn", [128, 4, S], BF16, nbufs=4)
        Wor = Ring(A, "Wo", [128, 4, 1024], BF16, 2)
        Pr = Ring(A, "P", [128, TG], BF16, 4)
        Tr = Ring(A, "tmp", [128, TG], F32, 3)
        sqr = Ring(A, "sq", [128, TG], BF16, 4); rsr = Ring(A, "rs", [128, TG], F32, 3)
        wuq = self.d["mla_w_uq"].ap(); wukv = self.d["mla_w_ukv"].ap()
        scale = 1.0 / math.sqrt(192.0)

        def proj(h):
            hd = Hd[h % 2]
            wq, wqb = hd["wq"]; wkv, wkvb = hd["wkv"]
            P.dma("pool", wq[:], wuq[j, :, h * 192:(h + 1) * 192].rearrange("(c p) n -> p c n", p=128), wqb, writes=[wqb])
            P.dma("pool", wkv[:], wukv[j, :, h * 256:(h + 1) * 256].rearrange("(c p) n -> p c n", p=128), wkvb, writes=[wkvb])
            qn, qnb = hd["qn"]; qr, qrb = hd["qr"]; kn, knb = hd["kn"]; V, Vb = hd["V"]
            for g in range(NG):
                sl = slice(g * TG, (g + 1) * TG)
                ph, pb = self.psA.next()
                for c in range(3):
                    self.mm(ph[:], wq[:, c, 0:128], cqn[:, c, sl], c == 0, c == 2, reads=[wqb, cqnB[g]], writes=[pb])
                self.head_norm(ph, pb, 128, gb + 5, sqr, rsr, self.psA,
                               lambda gc, rs, rsb, ph=ph, pb=pb, sl=sl: self.dve(lambda e: e.scalar_tensor_tensor(
                                   qn[:, sl], ph[:], gc, rs[:], ALU.mult, ALU.mult), reads=[pb, rsb, self.Gb], writes=[qnb]))
                ph, pb = self.psA.next()
                for c in range(3):
                    self.mm(ph[0:64, :], wq[:, c, 128:192], cqn[:, c, sl], c == 0, c == 2, reads=[wqb, cqnB[g]], writes=[pb])
                self.rope_head(ph, pb, gb + 6, sqr, rsr, Tr, cos, cosb, sin, sinb, rotT, sl, qr[:, sl], qrb)
                ph, pb = self.psA.next()
                for c in range(2):
                    self.mm(ph[:], wkv[:, c, 0:128], ckvn[:, c, sl], c == 0, c == 1, reads=[wkvb, ckvnB[g]], writes=[pb])
                self.head_norm(ph, pb, 128, gb + 7, sqr, rsr, self.psA,
                               lambda gc, rs, rsb, ph=ph, pb=pb, sl=sl: self.dve(lambda e: e.scalar_tensor_tensor(
                                   kn[:, sl], ph[:], gc, rs[:], ALU.mult, ALU.mult), reads=[pb, rsb, self.Gb], writes=[knb]))
                ph, pb = self.psA.next()
                for t4 in range(4):
                    ts_ = slice(g * TG + t4 * 128, g * TG + (t4 + 1) * 128)
                    for c in range(2):
                        self.mm(ph[:, t4 * 128:(t4 + 1) * 128], ckvn[:, c, ts_], wkv[:, c, 128:256], c == 0, c == 1,
                                reads=[wkvb, ckvnB[g]], writes=[pb])
                self.copy(V[:, g * 4:(g + 1) * 4, :], ph[:].rearrange("p (a b) -> p a b", a=4), reads=[pb], writes=[Vb])

        def attend(h):
            hd = Hd[h % 2]
            qn, qnb = hd["qn"]; qr, qrb = hd["qr"]; kn, knb = hd["kn"]; V, Vb = hd["V"]
            hh = h % 4

            def strip_fn(qt, kt):
                if kt < 4 * qt:
                    return None
                s0 = CF_CSTRIP + (512 * qt - 128 * kt) + 384
                return (self.CF[:, s0:s0 + 512], self.CFb)
            extra = (lambda kt: kpe[:, kt * 128:(kt + 1) * 128], lambda qt: qr[:, qt * TG:(qt + 1) * TG], [kpeb, qrb])
            self.attention(qn, qnb, kn, knb, V, Vb, extra, scale, strip_fn, None, None,
                           attn[:, hh, :], attnB[hh], Pr, Tr, rsr)

        proj(0)
        if self.stop <= 3:
            for nm in ("qn", "qr", "kn", "V"):
                t, bb = Hd[0][nm]
                self.dump(nm, t[:], bb)
            A.release(m0); return
        for h in range(8):
            if h + 1 < 8:
                proj(h + 1)
            attend(h)
            if self.stop <= 4:
                self.dump("attn0", attn[:, 0, :], attnB[0])
                A.release(m0); return
            if h % 4 == 3:
                Wo, Wob = Wor.next()
                self.wo_half("mla_w_o", j, h // 4, attn, attnB, Wo, Wob)
        A.release(m0)

    def rope_head(self, ph, pb, gcol, sqr, rsr, Tr, cos, cosb, sin, sinb, rotT, sl, dst, dstb):
        def fin(gc, rs, rsb):
            xn, xnb = Tr.next()
            self.dve(lambda e: e.scalar_tensor_tensor(xn[0:64, :], ph[0:64, :], gc, rs[0:64, :], ALU.mult, ALU.mult),
                     reads=[pb, rsb, self.Gb], writes=[xnb])
            p3, p3b = self.psA.next()
            self.mm(p3[0:64, :], rotT, xn[0:64, :], True, True, reads=[xnb, self.CFb], writes=[p3b])
            t1, t1b = Tr.next()
            self.dve(lambda e: e.tensor_tensor(t1[0:64, :], xn[0:64, :], cos[:, sl], ALU.mult), reads=[xnb, cosb], writes=[t1b])
            t2, t2b = Tr.next()
            self.dve(lambda e: e.tensor_tensor(t2[0:64, :], p3[0:64, :], sin[:, sl], ALU.mult), reads=[p3b, sinb], writes=[t2b])
            self.dve(lambda e: e.tensor_tensor(dst, t1[0:64, :], t2[0:64, :], ALU.add), reads=[t1b, t2b], writes=[dstb])
        self.head_norm(ph, pb, 64, gcol, sqr, rsr, self.psA, fin)

    def moba(self, j, l, s):
        A = self.A; P = self.P; m0 = A.mark()
        gb = G_MOBA + 2 * j
        H, HB = A.alloc("H", [128, NCH, S], BF16, nbufs=NG)
        m1 = A.mark()
        sqr = Ring(A, "sq", [128, TG], BF16, 4); rsr = Ring(A, "rs", [128, TG], F32, 3)
        self.norm_x(G_AN + 8 * l, H, HB, sqr, rsr, self.psA)
        A.release(m1)
        Hd = [dict(qn=A.alloc("qn%d" % i, [128, S], BF16), kn=A.alloc("kn%d" % i, [128, S], BF16),
                   V=A.alloc("V%d" % i, [128, 16, 128], BF16), w=A.alloc("w%d" % i, [128, NCH, 384], BF16),
                   strip=A.alloc("strip%d" % i, [128, STRIP_W], F32),
                   MT=A.alloc("MT%d" % i, [8, 1024], BF16))
              for i in range(2)]
        q32, q32b = A.alloc("q32", [128, 1024], F32)
        attn, attnB = A.alloc("attn", [128, 4, S], BF16, nbufs=4)
        Wo, Wob = A.alloc("Wo", [128, 4, 1024], BF16)
        Pr = Ring(A, "P", [128, TG], BF16, 4)
        Tr = Ring(A, "tmp", [128, TG], F32, 3)
        sqr = Ring(A, "sq", [128, TG], BF16, 4); rsr = Ring(A, "rs", [128, TG], F32, 3)
        km, kmb = A.alloc("kmean", [128, 8], F32)
        gm, gmb = A.alloc("gm", [128, 64], F32)
        mx, mxb = A.alloc("mx", [128, 64], F32)
        mv, mvb = A.alloc("mv", [128, 64], F32)
        wqkv = self.d["moba_w_qkv"].ap(); stripd = self.d["strip"].ap()
        scale = 1.0 / math.sqrt(128.0)
        indn = lambda n: self.CB[0:8, CB_IND + n * 128: CB_IND + (n + 1) * 128]

        def proj(h):
            hd = Hd[h % 2]
            w, wb = hd["w"]; strip, stripb = hd["strip"]
            for part in range(3):
                srcw = wqkv[j, :, part * 1024 + h * 128: part * 1024 + (h + 1) * 128].rearrange("(c p) n -> p c n", p=128)
                P.dma("pool", w[:, :, part * 128:(part + 1) * 128], srcw, wb, writes=[wb])
            P.dma("sp", strip[:], stripd[h], stripb, writes=[stripb])
            qn, qnb = hd["qn"]; kn, knb = hd["kn"]; V, Vb = hd["V"]
            for g in range(NG):
                sl = slice(g * TG, (g + 1) * TG)
                for part, (dst, dstb, gcol) in enumerate(((qn, qnb, gb), (kn, knb, gb + 1))):
                    ph, pb = self.psA.next()
                    for c in range(NCH):
                        self.mm(ph[:], w[:, c, part * 128:(part + 1) * 128], H[:, c, sl], c == 0, c == NCH - 1,
                                reads=[wb, HB[g]], writes=[pb])

                    def fin(gc, rs, rsb, ph=ph, pb=pb, part=part, dst=dst, dstb=dstb, g=g, sl=sl):
                        if part == 0 and g < 2:
                            self.dve(lambda e: e.scalar_tensor_tensor(dst[:, sl], ph[:], gc, rs[:], ALU.mult, ALU.mult),
                                     reads=[pb, rsb, self.Gb], writes=[dstb])
                            return
                        if part == 0:
                            t = q32[:, (g - 2) * TG:(g - 1) * TG]; tb = q32b
                        else:
                            t_, tb = Tr.next(); t = t_[:]
                        self.dve(lambda e: e.scalar_tensor_tensor(t, ph[:], gc, rs[:], ALU.mult, ALU.mult),
                                 reads=[pb, rsb, self.Gb], writes=[tb])
                        self.copy(dst[:, sl], t, reads=[tb], writes=[dstb], eng="act")
                        if part == 1:
                            self.dve(lambda e: e.tensor_reduce(km[:, 2 * g:2 * g + 2], t.rearrange("p (a b) -> p a b", a=2),
                                                               AX.X, ALU.add), reads=[tb], writes=[kmb])
                    self.head_norm(ph, pb, 128, gcol, sqr, rsr, self.psA, fin)
                ph, pb = self.psA.next()
                for t4 in range(4):
                    ts_ = slice(g * TG + t4 * 128, g * TG + (t4 + 1) * 128)
                    for c in range(NCH):
                        self.mm(ph[:, t4 * 128:(t4 + 1) * 128], H[:, c, ts_], w[:, c, 256:384], c == 0, c == NCH - 1,
                                reads=[wb, HB[g]], writes=[pb])
                self.copy(V[:, g * 4:(g + 1) * 4, :], ph[:].rearrange("p (a b) -> p a b", a=4), reads=[pb], writes=[Vb])
            MT, MTb = hd["MT"]
            gh, gpb = self.psA.next()
            for i in range(8):
                self.mm(gh[:, i * 8:(i + 1) * 8], q32[:, i * 128:(i + 1) * 128], km[:, :], True, True,
                        reads=[q32b, kmb], writes=[gpb])
            negm = self.CF[:, CF_NEGM:CF_NEGM + 64]; past = self.CF[:, CF_PAST:CF_PAST + 64]
            self.dve(lambda e: e.tensor_tensor(gm[:], gh[:, 0:64], negm, ALU.add), reads=[gpb, self.CFb], writes=[gmb])
            for i in range(8):
                self.dve(lambda e, i=i: e.max(mx[:, i * 8:(i + 1) * 8], gm[:, i * 8:(i + 1) * 8]), reads=[gmb], writes=[mxb])
            for i in range(8):
                self.dve(lambda e, i=i: e.tensor_scalar(mv[:, i * 8:(i + 1) * 8], gm[:, i * 8:(i + 1) * 8],
                                                        mx[:, i * 8 + 2:i * 8 + 3], 1.0, ALU.is_ge, ALU.subtract),
                         reads=[gmb, mxb], writes=[mvb])
            self.dve(lambda e: e.tensor_tensor(mv[:], mv[:], past, ALU.mult), reads=[mvb, self.CFb], writes=[mvb])
            for half in range(2):
                th, tpb = self.psA.next()
                for ii in range(4):
                    i = half * 4 + ii
                    self.tr(th[0:8, ii * 128:(ii + 1) * 128], mv[:, i * 8:(i + 1) * 8], reads=[mvb], writes=[tpb])
                self.copy(MT[:, half * 512:(half + 1) * 512], th[0:8, :], reads=[tpb], writes=[MTb], eng="act")

        def attend(h):
            hd = Hd[h % 2]
            qn, qnb = hd["qn"]; kn, knb = hd["kn"]; V, Vb = hd["V"]
            strip, stripb = hd["strip"]; MT, MTb = hd["MT"]
            hh = h % 4

            def strip_fn(qt, kt):
                dlt = 512 * qt - 128 * kt
                if dlt >= 1024:
                    return None
                s0 = dlt + 384
                return (strip[:, s0:s0 + 512], stripb)

            def mask_fn(qt, kt):
                n = kt // 2
                if qt < 2 or n > 2 * qt:
                    return None
                return (indn(n), MT[:, (qt - 2) * TG:(qt - 1) * TG], [self.CBb, MTb])
            far = (strip[:, STRIP_W - 1:STRIP_W], stripb)
            self.attention(qn, qnb, kn, knb, V, Vb, None, scale, strip_fn, mask_fn, far,
                           attn[:, hh, :], attnB[hh], Pr, Tr, rsr)

        proj(0)
        for h in range(8):
            if h + 1 < 8:
                proj(h + 1)
            attend(h)
            if h % 4 == 3:
                self.wo_half("moba_w_o", j, h // 4, attn, attnB, Wo, Wob)
        A.release(m0)

    def build(self):
        for s in range(self.nseq):
            self.load_x(s)
            for (kind, j, l) in self.plan:
                if kind == "mla":
                    self.mla(j, l, s)
                elif kind == "moba":
                    self.moba(j, l, s)
                else:
                    self.mlp(l)
            self.store_x(s)
        self.P.emit()
        return self.nc


_HOST_CACHE = {}


def _prep_shared(inp):
    cf, cb = _host_consts()
    sh = dict(cf=cf, cb=cb, gains=_host_gains(inp), strip=_host_strip(inp["rel_bias_table"]))
    for k in ("mla_w_in", "mla_w_uq", "mla_w_ukv", "mla_w_o", "moba_w_qkv", "moba_w_o", "mlp_w_in", "mlp_w_out"):
        sh[k] = np.ascontiguousarray(inp[k], dtype=np.float32)
    return sh


def kernel(**inputs):
    x = np.asarray(inputs["x"], np.float32); pos = np.asarray(inputs["positions"], np.int32)
    B = x.shape[0]
    ncores = 8
    nseq = B // ncores
    sh = _prep_shared(inputs)
    nc = KB(nseq=nseq).build()
    in_maps = []
    for c in range(ncores):
        m = dict(sh)
        m["x"] = np.ascontiguousarray(x[c * nseq:(c + 1) * nseq])
        m["pos"] = np.ascontiguousarray(pos[c * nseq:(c + 1) * nseq])
        in_maps.append(m)
    res = run_bass_kernel_spmd(nc, in_maps, core_ids=list(range(ncores)))
    out = np.concatenate([np.asarray(r["out"], np.float32) for r in res.results], axis=0)
    return out
```

```python
import math
import numpy as np
from contextlib import ExitStack
import concourse.bass as bass
import concourse.mybir as mybir
from concourse.bass_utils import run_bass_kernel_spmd

F32 = mybir.dt.float32; BF16 = mybir.dt.bfloat16; I32 = mybir.dt.int32
ALU = mybir.AluOpType; AF = mybir.ActivationFunctionType; AX = mybir.AxisListType

SAME_ENGINE_SYNC = True
COMPUTE = ("pe", "act", "dve", "pool")
S = 2048; D = 1024; NCH = 8; TG = 512; NG = 4; DFF = 4096
EPS = 1e-6
NEG = -30000.0
SB_BASE = 16512; SB_END = 229344


class Buf:
    __slots__ = ("name", "writers", "readers", "dsem", "dcnt", "excl")

    def __init__(self, name, excl=False):
        self.name = name; self.writers = []; self.readers = []; self.dsem = None; self.dcnt = 0
        self.excl = excl


class Op:
    __slots__ = ("eng", "idx", "fn", "deps", "marked", "dma")

    def __init__(self, eng, idx, fn, dma=None):
        self.eng = eng; self.idx = idx; self.fn = fn; self.deps = []; self.marked = False; self.dma = dma


class Prog:
    def __init__(self, nc):
        self.nc = nc
        self.streams = {e: [] for e in ("pe", "act", "dve", "pool", "sp")}
        self.seen = {e: {} for e in self.streams}
        self.dbufs = []

    def _resolve(self, eng, toks):
        out = []
        seen = self.seen[eng]
        for t in toks:
            if t[0] == "c":
                _, E, j = t
                if E == eng and (eng == "pe" or not SAME_ENGINE_SYNC):
                    continue
                if seen.get(E, -1) >= j:
                    continue
                seen[E] = j
                self.streams[E][j].marked = True
                out.append(t)
            else:
                b = t[1]
                v = b.dcnt
                key = id(b)
                if seen.get(key, -1) >= v:
                    continue
                seen[key] = v
                out.append(("d", b, v))
        return out

    def _record(self, eng, op, tok, reads, writes):
        toks = []
        def other(ts):
            return [t for t in ts if not (t[0] == "c" and t[1] == eng)]
        for b in reads:
            toks.extend(b.writers)
            if b.excl:
                toks.extend(other(b.readers))
        for b in writes:
            toks.extend(b.writers); toks.extend(other(b.readers))
        op.deps = self._resolve(eng, toks)
        for b in reads:
            b.readers.append(tok)
        for b in writes:
            b.writers = [tok]; b.readers = []
        self.streams[eng].append(op)

    def op(self, eng, fn, reads=(), writes=()):
        o = Op(eng, len(self.streams[eng]), fn)
        self._record(eng, o, ("c", eng, o.idx), reads, writes)

    def dma(self, q, out_ap, in_ap, sembuf, reads=(), writes=()):
        if sembuf.dsem is None:
            sembuf.dsem = True
            self.dbufs.append(sembuf)
        o = Op(q, len(self.streams[q]), None, dma=(out_ap, in_ap, sembuf))
        self._record(q, o, ("d", sembuf, sembuf.dcnt + 16), reads, writes)
        sembuf.dcnt += 16

    def emit(self):
        nc = self.nc
        with ExitStack() as es:
            esem = {e: es.enter_context(nc.semaphore("sem_" + e)) for e in COMPUTE}
            for i, b in enumerate(self.dbufs):
                b.dsem = es.enter_context(nc.semaphore("ds%d" % i))
            val = {}
            for e in COMPUTE:
                c = 0
                for o in self.streams[e]:
                    if o.marked:
                        c += 1
                    val[(e, o.idx)] = c
            block = es.enter_context(nc.Block())

            def run(engobj, name):
                for o in self.streams[name]:
                    for t in o.deps:
                        if t[0] == "c":
                            engobj.wait_ge(esem[t[1]], val[(t[1], t[2])])
                        else:
                            engobj.wait_ge(t[1].dsem, t[2])
                    if o.dma is not None:
                        out_ap, in_ap, sb = o.dma
                        engobj.dma_start(out=out_ap, in_=in_ap).then_inc(sb.dsem, 16)
                    else:
                        ins = o.fn(engobj)
                        if o.marked:
                            ins.then_inc(esem[name], 1)
                if name == "sp":
                    for b in self.dbufs:
                        engobj.wait_ge(b.dsem, b.dcnt)

            @block.tensor
            def _(e): run(e, "pe")

            @block.scalar
            def _(e): run(e, "act")

            @block.vector
            def _(e): run(e, "dve")

            @block.gpsimd
            def _(e): run(e, "pool")

            @block.sync
            def _(e): run(e, "sp")


def _dsize(dt):
    return 2 if dt == BF16 else 4


class Arena:
    def __init__(self, nc, base, end):
        self.nc = nc; self.base = base; self.end = end; self.top = base; self.hist = []; self.n = 0

    def alloc(self, name, shape, dtype, nbufs=None):
        nbytes = int(np.prod(shape[1:])) * _dsize(dtype)
        start = (self.top + 31) // 32 * 32
        assert start + nbytes <= self.end, "SBUF arena overflow: %s needs %d at %d (end %d)" % (name, nbytes, start, self.end)
        self.n += 1
        h = self.nc.alloc_sbuf_tensor_at("%s_%d" % (name, self.n), list(shape), dtype, offset=start)
        inh = []
        for (s0, e0, ob) in self.hist:
            if s0 < start + nbytes and e0 > start:
                inh.extend(ob.writers); inh.extend(ob.readers)
        bufs = []
        for i in range(nbufs or 1):
            b = Buf(name if nbufs is None else "%s.%d" % (name, i))
            b.writers.extend(inh)
            bufs.append(b)
        for b in bufs:
            self.hist.append((start, start + nbytes, b))
        self.top = start + nbytes
        return (h, bufs[0]) if nbufs is None else (h, bufs)

    def mark(self):
        return self.top

    def release(self, m):
        self.top = m


class Ring:
    def __init__(self, arena, name, shape, dtype, n):
        self.items = [arena.alloc("%s%d" % (name, i), shape, dtype) for i in range(n)]
        self.i = 0

    def next(self):
        it = self.items[self.i % len(self.items)]
        self.i += 1
        return it


class PsRing:
    def __init__(self, items):
        self.items = items; self.i = 0

    def next(self):
        it = self.items[self.i % len(self.items)]
        self.i += 1
        return it


CF_ID = 0; CF_ROT = 128; CF_INVF = 192; CF_NEGM = 193; CF_PAST = 257; CF_CSTRIP = 321; CF_W = 321 + 896
CB_ID = 0; CB_ONES = 128; CB_IND = 256; CB_W = 256 + 1024
STRIP_W = 1920
G_AN = 0; G_MN = 32; G_MLA = 64; G_MOBA = 82; G_W = 86


def _t5_bucket_np(d):
    n = np.maximum(d, 0)
    nf = np.maximum(n, 1).astype(np.float32)
    large = 16 + (np.log(nf / np.float32(16)) / np.float32(math.log(64.0)) * np.float32(16)).astype(np.int32)
    large = np.minimum(large, 31)
    return np.where(n < 16, n, large)


def _host_consts():
    cf = np.zeros((128, CF_W), np.float32)
    cf[:, CF_ID:CF_ID + 128] = np.eye(128, dtype=np.float32)
    rotT = np.zeros((64, 64), np.float32)
    for m in range(32):
        rotT[m + 32, m] = -1.0
        rotT[m, m + 32] = 1.0
    cf[:64, CF_ROT:CF_ROT + 64] = rotT
    inv = (10000.0 ** (-np.arange(0, 64, 2, dtype=np.float32) / np.float32(64))).astype(np.float32)
    cf[:32, CF_INVF] = inv; cf[32:64, CF_INVF] = inv
    for i in range(8):
        own = 4 + i // 2
        for n in range(8):
            cf[:, CF_NEGM + i * 8 + n] = -1e30 if n >= own else 0.0
            cf[:, CF_PAST + i * 8 + n] = 30000.0 if n < own else 0.0
    k = np.arange(128)[:, None]; j = np.arange(896)[None, :]
    cf[:, CF_CSTRIP:CF_CSTRIP + 896] = np.where(j - k - 384 >= 0, 0.0, NEG)
    cb = np.zeros((128, CB_W), np.float32)
    cb[:, CB_ID:CB_ID + 128] = np.eye(128, dtype=np.float32)
    cb[:, CB_ONES:CB_ONES + 128] = 1.0
    for n in range(8):
        cb[n, CB_IND + n * 128: CB_IND + (n + 1) * 128] = 1.0
    return cf, cb


def _host_gains(inp):
    g = np.zeros((128, G_W), np.float32)

    def put(col, vec):
        vec = np.asarray(vec, np.float32)
        n = vec.shape[0]
        if n >= 128:
            c = n // 128
            g[:, col:col + c] = vec.reshape(c, 128).T
        else:
            g[:n, col] = vec
    for l in range(4):
        put(G_AN + 8 * l, inp["attn_norm"][l]); put(G_MN + 8 * l, inp["mlp_norm"][l])
    for j in range(2):
        b = G_MLA + 9 * j
        put(b, inp["mla_q_a_norm"][j]); put(b + 3, inp["mla_kv_a_norm"][j])
        put(b + 5, inp["mla_q_nope_norm"][j]); put(b + 6, inp["mla_q_rope_norm"][j])
        put(b + 7, inp["mla_k_nope_norm"][j]); put(b + 8, inp["mla_k_rope_norm"][j])
        b2 = G_MOBA + 2 * j
        put(b2, inp["moba_q_norm"][j]); put(b2 + 1, inp["moba_k_norm"][j])
    return g


def _host_strip(table):
    k = np.arange(128)[:, None]; j = np.arange(STRIP_W)[None, :]
    d = j - k - 384
    bidx = _t5_bucket_np(d)
    t = np.asarray(table, np.float32)
    strip = np.transpose(t[bidx], (2, 0, 1)).copy()
    strip[:, d < 0] = NEG
    return np.ascontiguousarray(strip, dtype=np.float32)


class KB:
    def __init__(self, nseq=2, plan=None, debug_out=False, stop=99):
        self.nseq = nseq; self.stop = stop
        self.plan = plan if plan is not None else [("mla", 0, 0), ("mlp", 0, 0), ("moba", 0, 1), ("mlp", 1, 1),
                                                    ("mla", 1, 2), ("mlp", 2, 2), ("moba", 1, 3), ("mlp", 3, 3)]
        nc = self.nc = bass.Bass("TRN2", target_bir_lowering=False)
        self.P = Prog(nc)
        dt = nc.dram_tensor
        self.d = {}
        def inp(name, shape, dtype=F32):
            self.d[name] = dt(name, list(shape), dtype, kind="ExternalInput")
        inp("x", [nseq, S, D]); self.d["pos"] = None
        inp("cf", [128, CF_W]); inp("cb", [128, CB_W]); inp("gains", [128, G_W])
        kinds = set(k for (k, _, _) in self.plan)
        if "mla" in kinds:
            inp("pos", [nseq, S], I32)
            inp("mla_w_in", [2, 1024, 704]); inp("mla_w_uq", [2, 384, 1536]); inp("mla_w_ukv", [2, 256, 2048])
            inp("mla_w_o", [2, 1024, 1024])
        else:
            del self.d["pos"]
        if "moba" in kinds:
            inp("strip", [8, 128, STRIP_W]); inp("moba_w_qkv", [2, 1024, 3072]); inp("moba_w_o", [2, 1024, 1024])
        if "mlp" in kinds:
            inp("mlp_w_in", [4, 1024, 4096]); inp("mlp_w_out", [4, 4096, 1024])
        self.d["out"] = dt("out", [nseq, S, D], F32, kind="ExternalOutput")
        off = SB_BASE
        def static(name, shape, dtype):
            nonlocal off
            nbytes = int(np.prod(shape[1:])) * _dsize(dtype)
            h = nc.alloc_sbuf_tensor_at(name, list(shape), dtype, offset=off)
            off = (off + nbytes + 31) // 32 * 32
            return h
        self.CF = static("CF", [128, CF_W], F32); self.CFb = Buf("CF")
        self.CB = static("CB", [128, CB_W], BF16); self.CBb = Buf("CB")
        self.G = static("G", [128, G_W], F32); self.Gb = Buf("G")
        self.X = static("X", [128, NCH, S], F32)
        self.XB = [[Buf("X%d_%d" % (c, g)) for g in range(NG)] for c in range(NCH)]
        self.A = Arena(nc, off, SB_END)
        self.PS = []
        for i in range(8):
            h = nc.alloc_psum_tensor("ps%d" % i, [128, 512], F32)
            self.PS.append((h, Buf("ps%d" % i, excl=True)))
        self.psAll = PsRing(self.PS)
        self.psA = PsRing(self.PS[0:4]); self.psB = PsRing(self.PS[4:8])
        P = self.P
        P.dma("sp", self.CF[:], self.d["cf"].ap(), self.CFb, writes=[self.CFb])
        P.dma("pool", self.CB[:], self.d["cb"].ap(), self.CBb, writes=[self.CBb])
        P.dma("sp", self.G[:], self.d["gains"].ap(), self.Gb, writes=[self.Gb])
        self.ident = self.CF[:, CF_ID:CF_ID + 128]
        self.onesb = self.CB[:, CB_ONES:CB_ONES + 128]
        self.evac_i = 0

    def dump(self, name, ap, buf):
        shape = [int(v) for v in ap.shape]
        dtn = self.nc.dram_tensor("dbg_" + name, shape, F32, kind="ExternalOutput")
        self.d["dbg_" + name] = dtn
        self.P.dma("pool", dtn.ap(), ap, buf, reads=[buf])

    def mm(self, out, lhsT, rhs, start, stop, reads, writes):
        self.P.op("pe", lambda e: e.matmul(out, lhsT=lhsT, rhs=rhs, start=start, stop=stop), reads=reads, writes=writes)

    def tr(self, out, in_, reads, writes):
        ident = self.ident[0:in_.shape[0], 0:in_.shape[0]]
        self.P.op("pe", lambda e: e.transpose(out, in_, ident), reads=list(reads) + [self.CFb], writes=writes)

    def act(self, out, in_, func, reads, writes, **kw):
        self.P.op("act", lambda e: e.activation(out, in_, func, **kw), reads=reads, writes=writes)

    def copy(self, out, in_, reads, writes, eng=None):
        if eng is None:
            eng = "act" if (self.evac_i % 2 == 0) else "dve"
            self.evac_i += 1
        if eng == "act":
            self.P.op("act", lambda e: e.copy(out, in_), reads=reads, writes=writes)
        else:
            self.P.op("dve", lambda e: e.tensor_copy(out, in_), reads=reads, writes=writes)

    def dve(self, fn, reads, writes):
        self.P.op("dve", fn, reads=reads, writes=writes)

    def rstd_from_psum(self, ps, npart, n_total, rs, reads, rsb):
        self.act(rs[0:npart, :], ps[0:npart, :], AF.Ln, reads=reads, writes=[rsb], bias=EPS, scale=1.0 / n_total)
        self.act(rs[0:npart, :], rs[0:npart, :], AF.Exp, reads=[rsb], writes=[rsb], scale=-0.5)

    def load_x(self, s):
        A = self.A; m = A.mark()
        ring = Ring(A, "xin", [128, D], F32, 2)
        xd = self.d["x"].ap()
        for g in range(NG):
            banks = [self.psAll.next() for _ in range(NCH)]
            for j in range(4):
                tt = g * 4 + j
                st, sb = ring.next()
                self.P.dma("sp", st[:], xd[s, tt * 128:(tt + 1) * 128, :], sb, writes=[sb])
                for c in range(NCH):
                    ph, pb = banks[c]
                    self.tr(ph[:, j * 128:(j + 1) * 128], st[:, c * 128:(c + 1) * 128], reads=[sb], writes=[pb])
            for c in range(NCH):
                ph, pb = banks[c]
                self.copy(self.X[:, c, g * TG:(g + 1) * TG], ph[:], reads=[pb], writes=[self.XB[c][g]])
        A.release(m)

    def store_x(self, s):
        A = self.A; m = A.mark()
        ring = Ring(A, "xout", [128, D], F32, 2)
        od = self.d["out"].ap()
        for tt in range(16):
            g = tt // 4
            st, sb = ring.next()
            for half in range(2):
                ph, pb = self.psAll.next()
                for cc in range(4):
                    c = half * 4 + cc
                    self.tr(ph[:, cc * 128:(cc + 1) * 128], self.X[:, c, tt * 128:(tt + 1) * 128],
                            reads=[self.XB[c][g]], writes=[pb])
                self.copy(st[:, half * 512:(half + 1) * 512], ph[:], reads=[pb], writes=[sb])
            self.P.dma("sp", od[s, tt * 128:(tt + 1) * 128, :], st[:], sb, reads=[sb])
        A.release(m)

    def norm_x(self, gcol, H, HB, sqr, rsr, ps_ring):
        for g in range(NG):
            ph, pb = ps_ring.next()
            sl = slice(g * TG, (g + 1) * TG)
            for c in range(NCH):
                sq, sqb = sqr.next()
                self.act(sq[:], self.X[:, c, sl], AF.Square, reads=[self.XB[c][g]], writes=[sqb])
                self.mm(ph[:], self.onesb, sq[:], c == 0, c == NCH - 1, reads=[sqb, self.CBb], writes=[pb])
            rs, rsb = rsr.next()
            self.rstd_from_psum(ph, 128, float(D), rs, [pb], rsb)
            for c in range(NCH):
                gc = self.G[:, gcol + c:gcol + c + 1]
                self.dve(lambda e, c=c, gc=gc, rs=rs, sl=sl: e.scalar_tensor_tensor(
                    H[:, c, sl], self.X[:, c, sl], gc, rs[:], ALU.mult, ALU.mult),
                    reads=[self.XB[c][g], rsb, self.Gb], writes=[HB[g]])

    def mlp(self, l):
        A = self.A; P = self.P; m0 = A.mark()
        H, HB = A.alloc("H", [128, NCH, S], BF16, nbufs=NG)
        m1 = A.mark()
        sqr = Ring(A, "sq", [128, TG], BF16, 4); rsr = Ring(A, "rs", [128, TG], F32, 2)
        self.norm_x(G_MN + 8 * l, H, HB, sqr, rsr, self.psAll)
        A.release(m1)
        W1r = Ring(A, "W1", [128, 8, 1024], BF16, 2); W2r = Ring(A, "W2", [128, 8, 1024], BF16, 2)
        Ar = Ring(A, "Aa", [128, 8, TG], BF16, 2); Rr = Ring(A, "Rr", [128, TG], F32, 3)
        w_in = self.d["mlp_w_in"].ap(); w_out = self.d["mlp_w_out"].ap()

        def load_w(q):
            (w1, b1) = W1r.next(); (w2, b2) = W2r.next()
            src1 = w_in[l, :, q * 1024:(q + 1) * 1024].rearrange("(c p) n -> p c n", p=128)
            src2 = w_out[l, q * 1024:(q + 1) * 1024, :].rearrange("(c p) n -> p c n", p=128)
            for cc in range(0, 8, 2):
                P.dma("pool", w1[:, cc:cc + 2, :], src1[:, cc:cc + 2, :], b1, writes=[b1])
            for cc in range(0, 8, 2):
                P.dma("pool", w2[:, cc:cc + 2, :], src2[:, cc:cc + 2, :], b2, writes=[b2])
            return (w1, b1, w2, b2)

        def up(W, g):
            w1, b1, _, _ = W
            a, ab = Ar.next()
            sl = slice(g * TG, (g + 1) * TG)
            for f in range(8):
                ph, pb = self.psAll.next()
                for k in range(NCH):
                    self.mm(ph[:], w1[:, k, f * 128:(f + 1) * 128], H[:, k, sl], k == 0, k == NCH - 1,
                            reads=[b1, HB[g]], writes=[pb])
                r, rb = Rr.next()
                self.act(r[:], ph[:], AF.Relu, reads=[pb], writes=[rb])
                self.dve(lambda e, a=a, f=f, r=r: e.tensor_tensor(a[:, f, :], r[:], r[:], ALU.mult), reads=[rb], writes=[ab])
            return (a, ab)

        def down(W, g, at):
            _, _, w2, b2 = W
            a, ab = at
            sl = slice(g * TG, (g + 1) * TG)
            for mch in range(NCH):
                ph, pb = self.psAll.next()
                for f in range(8):
                    self.mm(ph[:], w2[:, f, mch * 128:(mch + 1) * 128], a[:, f, :], f == 0, f == 7,
                            reads=[b2, ab], writes=[pb])
                xs = self.X[:, mch, sl]
                self.dve(lambda e, xs=xs, ph=ph: e.tensor_tensor(xs, xs, ph[:], ALU.add),
                         reads=[pb, self.XB[mch][g]], writes=[self.XB[mch][g]])

        Ws = {0: load_w(0)}
        steps = [(q, g) for q in range(4) for g in range(NG)]
        pend = None
        for i, (q, g) in enumerate(steps):
            at = up(Ws[q], g)
            if pend is not None:
                down(*pend)
            pend = (Ws[q], g, at)
            if g == 0 and q + 1 < 4:
                Ws[q + 1] = load_w(q + 1)
        down(*pend)
        A.release(m0)

    def rope_tables(self, s, cos, cosb, sin, sinb):
        A = self.A; m = A.mark()
        pi_, pib = A.alloc("posi", [64, S], I32)
        ang, angb = A.alloc("ang", [64, S], F32)
        kf, kfb = A.alloc("kf", [64, S], F32)
        r, rb = A.alloc("r", [64, S], F32)
        shh, shb = A.alloc("shh", [64, S], F32)
        chh, chb = A.alloc("chh", [64, S], F32)
        src = bass.AP(self.d["pos"], s * S, [[0, 64], [1, S]])
        self.P.dma("sp", pi_[:], src, pib, writes=[pib])
        self.dve(lambda e: e.tensor_copy(ang[:], pi_[:]), reads=[pib], writes=[angb])
        invf = self.CF[0:64, CF_INVF:CF_INVF + 1]
        self.dve(lambda e: e.tensor_scalar(ang[:], ang[:], invf, None, ALU.mult), reads=[angb, self.CFb], writes=[angb])
        self.dve(lambda e: e.tensor_scalar(pi_[:], ang[:], 1.0 / (2.0 * math.pi), None, ALU.mult), reads=[angb], writes=[pib])
        self.dve(lambda e: e.tensor_copy(kf[:], pi_[:]), reads=[pib], writes=[kfb])
        self.dve(lambda e: e.scalar_tensor_tensor(r[:], kf[:], -2.0 * math.pi, ang[:], ALU.mult, ALU.add),
                 reads=[kfb, angb], writes=[rb])
        sh = 0.999999
        self.act(shh[:], r[:], AF.Sin, reads=[rb], writes=[shb], scale=0.5 * sh)
        self.act(chh[:], r[:], AF.Sin, reads=[rb], writes=[chb], bias=0.5 * math.pi * sh, scale=-0.5 * sh)
        self.dve(lambda e: e.scalar_tensor_tensor(sin[:], shh[:], 2.0, chh[:], ALU.mult, ALU.mult), reads=[shb, chb], writes=[sinb])
        self.dve(lambda e: e.tensor_tensor(kf[:], shh[:], shh[:], ALU.mult), reads=[shb], writes=[kfb])
        self.dve(lambda e: e.tensor_scalar(cos[:], kf[:], -2.0, 1.0, ALU.mult, ALU.add), reads=[kfb], writes=[cosb])
        A.release(m)

    def head_norm(self, ph, pb, npart, gcol, sqr, rsr, psr, out_fn):
        sq, sqb = sqr.next()
        self.act(sq[0:npart, :], ph[0:npart, :], AF.Square, reads=[pb], writes=[sqb])
        p2, p2b = psr.next()
        self.mm(p2[0:npart, :], self.onesb[0:npart, 0:npart], sq[0:npart, :], True, True, reads=[sqb, self.CBb], writes=[p2b])
        rs, rsb = rsr.next()
        self.rstd_from_psum(p2, npart, float(npart), rs, [p2b], rsb)
        gc = self.G[0:npart, gcol:gcol + 1]
        out_fn(gc, rs, rsb)

    def attention(self, qT, qb, kT, kb, V, Vb, extra_qk, scale, strip_fn, mask_fn, far_bias,
                  outT, outb, Pr, Tr, rsr):
        LOOK = 2
        for qt in range(4):
            qs = slice(qt * TG, (qt + 1) * TG)
            oh, ob = self.psB.next(); lh, lb = self.psB.next()
            nk = 4 * qt + 4

            def qk(kt):
                ks = slice(kt * 128, (kt + 1) * 128)
                sh_, sb_ = self.psA.next()
                mk = mask_fn(qt, kt) if mask_fn is not None else None
                last_qk = (extra_qk is None) and (mk is None)
                self.mm(sh_[:], kT[:, ks], qT[:, qs], True, last_qk, reads=[kb, qb], writes=[sb_])
                if extra_qk is not None:
                    l2, r2, rd2 = extra_qk
                    self.mm(sh_[:], l2(kt), r2(qt), False, mk is None, reads=rd2, writes=[sb_])
                if mk is not None:
                    ml, mr, mrd = mk
                    self.mm(sh_[:], ml, mr, False, True, reads=mrd, writes=[sb_])
                p, pbuf = Pr.next()
                st = strip_fn(qt, kt)
                if st is not None:
                    sap, sbuf = st
                    t, tb = Tr.next()
                    self.dve(lambda e, t=t, sh_=sh_, sap=sap: e.scalar_tensor_tensor(
                        t[:], sh_[:], scale, sap, ALU.mult, ALU.add), reads=[sb_, sbuf], writes=[tb])
                    self.act(p[:], t[:], AF.Exp, reads=[tb], writes=[pbuf])
                elif far_bias is not None:
                    fb, fbb = far_bias
                    self.act(p[:], sh_[:], AF.Exp, reads=[sb_, fbb], writes=[pbuf], bias=fb, scale=scale)
                else:
                    self.act(p[:], sh_[:], AF.Exp, reads=[sb_], writes=[pbuf], scale=scale)
                return (kt, p, pbuf)

            def pv(item):
                kt, p, pbuf = item
                self.mm(oh[:], V[:, kt, :], p[:], kt == 0, kt == nk - 1, reads=[Vb, pbuf], writes=[ob])
                self.mm(lh[:], self.onesb, p[:], kt == 0, kt == nk - 1, reads=[pbuf, self.CBb], writes=[lb])

            pend = []
            for kt in range(nk):
                pend.append(qk(kt))
                if len(pend) > LOOK:
                    pv(pend.pop(0))
            while pend:
                pv(pend.pop(0))
            rl, rlb = rsr.next()
            self.act(rl[:], lh[:], AF.Ln, reads=[lb], writes=[rlb])
            self.act(rl[:], rl[:], AF.Exp, reads=[rlb], writes=[rlb], scale=-1.0)
            self.dve(lambda e, rl=rl, oh=oh, qs=qs: e.tensor_tensor(outT[:, qs], oh[:], rl[:], ALU.mult),
                     reads=[ob, rlb], writes=[outb])

    def wo_half(self, wname, j, half, attn, attnb, Wo, Wob):
        wd = self.d[wname].ap()
        src = wd[j, half * 512:(half + 1) * 512, :].rearrange("(c p) n -> p c n", p=128)
        for hh in range(4):
            self.P.dma("pool", Wo[:, hh:hh + 1, :], src[:, hh:hh + 1, :], Wob, writes=[Wob])
        for g in range(NG):
            sl = slice(g * TG, (g + 1) * TG)
            for mch in range(NCH):
                ph, pb = self.psA.next()
                for hh in range(4):
                    self.mm(ph[:], Wo[:, hh, mch * 128:(mch + 1) * 128], attn[:, hh, sl], hh == 0, hh == 3,
                            reads=[Wob, attnb[hh]], writes=[pb])
                xs = self.X[:, mch, sl]
                self.dve(lambda e, xs=xs, ph=ph: e.tensor_tensor(xs, xs, ph[:], ALU.add),
                         reads=[pb, self.XB[mch][g]], writes=[self.XB[mch][g]])

    def mla(self, j, l, s):
        A = self.A; P = self.P; m0 = A.mark()
        gb = G_MLA + 9 * j
        cqn, cqnB = A.alloc("cqn", [128, 3, S], BF16, nbufs=NG)
        ckvn, ckvnB = A.alloc("ckvn", [128, 2, S], BF16, nbufs=NG)
        kpe, kpeb = A.alloc("kpe", [64, S], BF16)
        cos, cosb = A.alloc("cos", [64, S], F32); sin, sinb = A.alloc("sin", [64, S], F32)
        self.rope_tables(s, cos, cosb, sin, sinb)
        rotT = self.CF[0:64, CF_ROT:CF_ROT + 64]
        if self.stop <= 1:
            A.release(m0); return
        m1 = A.mark()
        H, HB = A.alloc("H", [128, NCH, S], BF16, nbufs=NG)
        Win, Winb = A.alloc("Win", [128, NCH, 704], BF16)
        src = self.d["mla_w_in"].ap()[j].rearrange("(c p) n -> p c n", p=128)
        for cc in range(0, 8, 4):
            P.dma("pool", Win[:, cc:cc + 4, :], src[:, cc:cc + 4, :], Winb, writes=[Winb])
        sqr = Ring(A, "sq", [128, TG], BF16, 4); rsr = Ring(A, "rs", [128, TG], F32, 3)
        rawr = Ring(A, "raw", [128, 3, TG], F32, 2)
        Tr = Ring(A, "tmp", [128, TG], F32, 3)
        self.norm_x(G_AN + 8 * l, H, HB, sqr, rsr, self.psA)
        for g in range(NG):
            sl = slice(g * TG, (g + 1) * TG)
            for (dst, dstB, c0, nchunk, gcol) in ((cqn, cqnB, 0, 3, gb), (ckvn, ckvnB, 3, 2, gb + 3)):
                raw, rawb = rawr.next()
                p2, p2b = self.psB.next()
                for cc in range(nchunk):
                    ph, pb = self.psA.next()
                    col = (c0 + cc) * 128
                    for k in range(NCH):
                        self.mm(ph[:], Win[:, k, col:col + 128], H[:, k, sl], k == 0, k == NCH - 1,
                                reads=[Winb, HB[g]], writes=[pb])
                    self.copy(raw[:, cc, :], ph[:], reads=[pb], writes=[rawb], eng="dve")
                    sq, sqb = sqr.next()
                    self.act(sq[:], ph[:], AF.Square, reads=[pb], writes=[sqb])
                    self.mm(p2[:], self.onesb, sq[:], cc == 0, cc == nchunk - 1, reads=[sqb, self.CBb], writes=[p2b])
                rs, rsb = rsr.next()
                self.rstd_from_psum(p2, 128, float(nchunk * 128), rs, [p2b], rsb)
                for cc in range(nchunk):
                    gc = self.G[:, gcol + cc:gcol + cc + 1]
                    self.dve(lambda e, dst=dst, cc=cc, raw=raw, gc=gc, rs=rs, sl=sl: e.scalar_tensor_tensor(
                        dst[:, cc, sl], raw[:, cc, :], gc, rs[:], ALU.mult, ALU.mult),
                        reads=[rawb, rsb, self.Gb], writes=[dstB[g]])
            ph, pb = self.psA.next()
            for k in range(NCH):
                self.mm(ph[0:64, :], Win[:, k, 640:704], H[:, k, sl], k == 0, k == NCH - 1, reads=[Winb, HB[g]], writes=[pb])
            self.rope_head(ph, pb, gb + 8, sqr, rsr, Tr, cos, cosb, sin, sinb, rotT, sl, kpe[:, sl], kpeb)
        A.release(m1)
        if self.stop <= 2:
            for nm, t, bb in (("cqn", cqn, cqnB), ("ckvn", ckvn, ckvnB)):
                for g in range(NG):
                    self.dump("%s%d" % (nm, g), t[:, :, g * TG:(g + 1) * TG], bb[g])
            self.dump("kpe", kpe[:], kpeb); self.dump("cos", cos[:], cosb); self.dump("sin", sin[:], sinb)
            A.release(m0); return
        Hd = [dict(qn=A.alloc("qn%d" % i, [128, S], BF16), qr=A.alloc("qr%d" % i, [64, S], BF16),
                   kn=A.alloc("kn%d" % i, [128, S], BF16), V=A.alloc("V%d" % i, [128, 16, 128], BF16),
                   wq=A.alloc("wq%d" % i, [128, 3, 192], BF16), wkv=A.alloc("wkv%d" % i, [128, 2, 256], BF16))
              for i in range(2)]
        attn, attnB = A.alloc("attn", [128, 4, S], BF16, nbufs=4)
        Wor = Ring(A, "Wo", [128, 4, 1024], BF16, 2)
        Pr = Ring(A, "P", [128, TG], BF16, 4)
        Tr = Ring(A, "tmp", [128, TG], F32, 3)
        sqr = Ring(A, "sq", [128, TG], BF16, 4); rsr = Ring(A, "rs", [128, TG], F32, 3)
        wuq = self.d["mla_w_uq"].ap(); wukv = self.d["mla_w_ukv"].ap()
        scale = 1.0 / math.sqrt(192.0)

        def proj(h):
            hd = Hd[h % 2]
            wq, wqb = hd["wq"]; wkv, wkvb = hd["wkv"]
            P.dma("pool", wq[:], wuq[j, :, h * 192:(h + 1) * 192].rearrange("(c p) n -> p c n", p=128), wqb, writes=[wqb])
            P.dma("pool", wkv[:], wukv[j, :, h * 256:(h + 1) * 256].rearrange("(c p) n -> p c n", p=128), wkvb, writes=[wkvb])
            qn, qnb = hd["qn"]; qr, qrb = hd["qr"]; kn, knb = hd["kn"]; V, Vb = hd["V"]
            for g in range(NG):
                sl = slice(g * TG, (g + 1) * TG)
                ph, pb = self.psA.next()
                for c in range(3):
                    self.mm(ph[:], wq[:, c, 0:128], cqn[:, c, sl], c == 0, c == 2, reads=[wqb, cqnB[g]], writes=[pb])
                self.head_norm(ph, pb, 128, gb + 5, sqr, rsr, self.psA,
                               lambda gc, rs, rsb, ph=ph, pb=pb, sl=sl: self.dve(lambda e: e.scalar_tensor_tensor(
                                   qn[:, sl], ph[:], gc, rs[:], ALU.mult, ALU.mult), reads=[pb, rsb, self.Gb], writes=[qnb]))
                ph, pb = self.psA.next()
                for c in range(3):
                    self.mm(ph[0:64, :], wq[:, c, 128:192], cqn[:, c, sl], c == 0, c == 2, reads=[wqb, cqnB[g]], writes=[pb])
                self.rope_head(ph, pb, gb + 6, sqr, rsr, Tr, cos, cosb, sin, sinb, rotT, sl, qr[:, sl], qrb)
                ph, pb = self.psA.next()
                for c in range(2):
                    self.mm(ph[:], wkv[:, c, 0:128], ckvn[:, c, sl], c == 0, c == 1, reads=[wkvb, ckvnB[g]], writes=[pb])
                self.head_norm(ph, pb, 128, gb + 7, sqr, rsr, self.psA,
                               lambda gc, rs, rsb, ph=ph, pb=pb, sl=sl: self.dve(lambda e: e.scalar_tensor_tensor(
                                   kn[:, sl], ph[:], gc, rs[:], ALU.mult, ALU.mult), reads=[pb, rsb, self.Gb], writes=[knb]))
                ph, pb = self.psA.next()
                for t4 in range(4):
                    ts_ = slice(g * TG + t4 * 128, g * TG + (t4 + 1) * 128)
                    for c in range(2):
                        self.mm(ph[:, t4 * 128:(t4 + 1) * 128], ckvn[:, c, ts_], wkv[:, c, 128:256], c == 0, c == 1,
                                reads=[wkvb, ckvnB[g]], writes=[pb])
                self.copy(V[:, g * 4:(g + 1) * 4, :], ph[:].rearrange("p (a b) -> p a b", a=4), reads=[pb], writes=[Vb])

        def attend(h):
            hd = Hd[h % 2]
            qn, qnb = hd["qn"]; qr, qrb = hd["qr"]; kn, knb = hd["kn"]; V, Vb = hd["V"]
            hh = h % 4

            def strip_fn(qt, kt):
                if kt < 4 * qt:
                    return None
                s0 = CF_CSTRIP + (512 * qt - 128 * kt) + 384
                return (self.CF[:, s0:s0 + 512], self.CFb)
            extra = (lambda kt: kpe[:, kt * 128:(kt + 1) * 128], lambda qt: qr[:, qt * TG:(qt + 1) * TG], [kpeb, qrb])
            self.attention(qn, qnb, kn, knb, V, Vb, extra, scale, strip_fn, None, None,
                           attn[:, hh, :], attnB[hh], Pr, Tr, rsr)

        proj(0)
        if self.stop <= 3:
            for nm in ("qn", "qr", "kn", "V"):
                t, bb = Hd[0][nm]
                self.dump(nm, t[:], bb)
            A.release(m0); return
        for h in range(8):
            if h + 1 < 8:
                proj(h + 1)
            attend(h)
            if self.stop <= 4:
                self.dump("attn0", attn[:, 0, :], attnB[0])
                A.release(m0); return
            if h % 4 == 3:
                Wo, Wob = Wor.next()
                self.wo_half("mla_w_o", j, h // 4, attn, attnB, Wo, Wob)
        A.release(m0)

    def rope_head(self, ph, pb, gcol, sqr, rsr, Tr, cos, cosb, sin, sinb, rotT, sl, dst, dstb):
        def fin(gc, rs, rsb):
            xn, xnb = Tr.next()
            self.dve(lambda e: e.scalar_tensor_tensor(xn[0:64, :], ph[0:64, :], gc, rs[0:64, :], ALU.mult, ALU.mult),
                     reads=[pb, rsb, self.Gb], writes=[xnb])
            p3, p3b = self.psA.next()
            self.mm(p3[0:64, :], rotT, xn[0:64, :], True, True, reads=[xnb, self.CFb], writes=[p3b])
            t1, t1b = Tr.next()
            self.dve(lambda e: e.tensor_tensor(t1[0:64, :], xn[0:64, :], cos[:, sl], ALU.mult), reads=[xnb, cosb], writes=[t1b])
            t2, t2b = Tr.next()
            self.dve(lambda e: e.tensor_tensor(t2[0:64, :], p3[0:64, :], sin[:, sl], ALU.mult), reads=[p3b, sinb], writes=[t2b])
            self.dve(lambda e: e.tensor_tensor(dst, t1[0:64, :], t2[0:64, :], ALU.add), reads=[t1b, t2b], writes=[dstb])
        self.head_norm(ph, pb, 64, gcol, sqr, rsr, self.psA, fin)

    def moba(self, j, l, s):
        A = self.A; P = self.P; m0 = A.mark()
        gb = G_MOBA + 2 * j
        H, HB = A.alloc("H", [128, NCH, S], BF16, nbufs=NG)
        m1 = A.mark()
        sqr = Ring(A, "sq", [128, TG], BF16, 4); rsr = Ring(A, "rs", [128, TG], F32, 3)
        self.norm_x(G_AN + 8 * l, H, HB, sqr, rsr, self.psA)
        A.release(m1)
        Hd = [dict(qn=A.alloc("qn%d" % i, [128, S], BF16), kn=A.alloc("kn%d" % i, [128, S], BF16),
                   V=A.alloc("V%d" % i, [128, 16, 128], BF16), w=A.alloc("w%d" % i, [128, NCH, 384], BF16),
                   strip=A.alloc("strip%d" % i, [128, STRIP_W], F32),
                   MT=A.alloc("MT%d" % i, [8, 1024], BF16))
              for i in range(2)]
        q32, q32b = A.alloc("q32", [128, 1024], F32)
        attn, attnB = A.alloc("attn", [128, 4, S], BF16, nbufs=4)
        Wo, Wob = A.alloc("Wo", [128, 4, 1024], BF16)
        Pr = Ring(A, "P", [128, TG], BF16, 4)
        Tr = Ring(A, "tmp", [128, TG], F32, 3)
        sqr = Ring(A, "sq", [128, TG], BF16, 4); rsr = Ring(A, "rs", [128, TG], F32, 3)
        km, kmb = A.alloc("kmean", [128, 8], F32)
        gm, gmb = A.alloc("gm", [128, 64], F32)
        mx, mxb = A.alloc("mx", [128, 64], F32)
        mv, mvb = A.alloc("mv", [128, 64], F32)
        wqkv = self.d["moba_w_qkv"].ap(); stripd = self.d["strip"].ap()
        scale = 1.0 / math.sqrt(128.0)
        indn = lambda n: self.CB[0:8, CB_IND + n * 128: CB_IND + (n + 1) * 128]

        def proj(h):
            hd = Hd[h % 2]
            w, wb = hd["w"]; strip, stripb = hd["strip"]
            for part in range(3):
                srcw = wqkv[j, :, part * 1024 + h * 128: part * 1024 + (h + 1) * 128].rearrange("(c p) n -> p c n", p=128)
                P.dma("pool", w[:, :, part * 128:(part + 1) * 128], srcw, wb, writes=[wb])
            P.dma("sp", strip[:], stripd[h], stripb, writes=[stripb])
            qn, qnb = hd["qn"]; kn, knb = hd["kn"]; V, Vb = hd["V"]
            for g in range(NG):
                sl = slice(g * TG, (g + 1) * TG)
                for part, (dst, dstb, gcol) in enumerate(((qn, qnb, gb), (kn, knb, gb + 1))):
                    ph, pb = self.psA.next()
                    for c in range(NCH):
                        self.mm(ph[:], w[:, c, part * 128:(part + 1) * 128], H[:, c, sl], c == 0, c == NCH - 1,
                                reads=[wb, HB[g]], writes=[pb])

                    def fin(gc, rs, rsb, ph=ph, pb=pb, part=part, dst=dst, dstb=dstb, g=g, sl=sl):
                        if part == 0 and g < 2:
                            self.dve(lambda e: e.scalar_tensor_tensor(dst[:, sl], ph[:], gc, rs[:], ALU.mult, ALU.mult),
                                     reads=[pb, rsb, self.Gb], writes=[dstb])
                            return
                        if part == 0:
                            t = q32[:, (g - 2) * TG:(g - 1) * TG]; tb = q32b
                        else:
                            t_, tb = Tr.next(); t = t_[:]
                        self.dve(lambda e: e.scalar_tensor_tensor(t, ph[:], gc, rs[:], ALU.mult, ALU.mult),
                                 reads=[pb, rsb, self.Gb], writes=[tb])
                        self.copy(dst[:, sl], t, reads=[tb], writes=[dstb], eng="act")
                        if part == 1:
                            self.dve(lambda e: e.tensor_reduce(km[:, 2 * g:2 * g + 2], t.rearrange("p (a b) -> p a b", a=2),
                                                               AX.X, ALU.add), reads=[tb], writes=[kmb])
                    self.head_norm(ph, pb, 128, gcol, sqr, rsr, self.psA, fin)
                ph, pb = self.psA.next()
                for t4 in range(4):
                    ts_ = slice(g * TG + t4 * 128, g * TG + (t4 + 1) * 128)
                    for c in range(NCH):
                        self.mm(ph[:, t4 * 128:(t4 + 1) * 128], H[:, c, ts_], w[:, c, 256:384], c == 0, c == NCH - 1,
                                reads=[wb, HB[g]], writes=[pb])
                self.copy(V[:, g * 4:(g + 1) * 4, :], ph[:].rearrange("p (a b) -> p a b", a=4), reads=[pb], writes=[Vb])
            MT, MTb = hd["MT"]
            gh, gpb = self.psA.next()
            for i in range(8):
                self.mm(gh[:, i * 8:(i + 1) * 8], q32[:, i * 128:(i + 1) * 128], km[:, :], True, True,
                        reads=[q32b, kmb], writes=[gpb])
            negm = self.CF[:, CF_NEGM:CF_NEGM + 64]; past = self.CF[:, CF_PAST:CF_PAST + 64]
            self.dve(lambda e: e.tensor_tensor(gm[:], gh[:, 0:64], negm, ALU.add), reads=[gpb, self.CFb], writes=[gmb])
            for i in range(8):
                self.dve(lambda e, i=i: e.max(mx[:, i * 8:(i + 1) * 8], gm[:, i * 8:(i + 1) * 8]), reads=[gmb], writes=[mxb])
            for i in range(8):
                self.dve(lambda e, i=i: e.tensor_scalar(mv[:, i * 8:(i + 1) * 8], gm[:, i * 8:(i + 1) * 8],
                                                        mx[:, i * 8 + 2:i * 8 + 3], 1.0, ALU.is_ge, ALU.subtract),
                         reads=[gmb, mxb], writes=[mvb])
            self.dve(lambda e: e.tensor_tensor(mv[:], mv[:], past, ALU.mult), reads=[mvb, self.CFb], writes=[mvb])
            for half in range(2):
                th, tpb = self.psA.next()
                for ii in range(4):
                    i = half * 4 + ii
                    self.tr(th[0:8, ii * 128:(ii + 1) * 128], mv[:, i * 8:(i + 1) * 8], reads=[mvb], writes=[tpb])
                self.copy(MT[:, half * 512:(half + 1) * 512], th[0:8, :], reads=[tpb], writes=[MTb], eng="act")

        def attend(h):
            hd = Hd[h % 2]
            qn, qnb = hd["qn"]; kn, knb = hd["kn"]; V, Vb = hd["V"]
            strip, stripb = hd["strip"]; MT, MTb = hd["MT"]
            hh = h % 4

            def strip_fn(qt, kt):
                dlt = 512 * qt - 128 * kt
                if dlt >= 1024:
                    return None
                s0 = dlt + 384
                return (strip[:, s0:s0 + 512], stripb)

            def mask_fn(qt, kt):
                n = kt // 2
                if qt < 2 or n > 2 * qt:
                    return None
                return (indn(n), MT[:, (qt - 2) * TG:(qt - 1) * TG], [self.CBb, MTb])
            far = (strip[:, STRIP_W - 1:STRIP_W], stripb)
            self.attention(qn, qnb, kn, knb, V, Vb, None, scale, strip_fn, mask_fn, far,
                           attn[:, hh, :], attnB[hh], Pr, Tr, rsr)

        proj(0)
        for h in range(8):
            if h + 1 < 8:
                proj(h + 1)
            attend(h)
            if h % 4 == 3:
                self.wo_half("moba_w_o", j, h // 4, attn, attnB, Wo, Wob)
        A.release(m0)

    def build(self):
        for s in range(self.nseq):
            self.load_x(s)
            for (kind, j, l) in self.plan:
                if kind == "mla":
                    self.mla(j, l, s)
                elif kind == "moba":
                    self.moba(j, l, s)
                else:
                    self.mlp(l)
            self.store_x(s)
        self.P.emit()
        return self.nc


_HOST_CACHE = {}


def _prep_shared(inp):
    cf, cb = _host_consts()
    sh = dict(cf=cf, cb=cb, gains=_host_gains(inp), strip=_host_strip(inp["rel_bias_table"]))
    for k in ("mla_w_in", "mla_w_uq", "mla_w_ukv", "mla_w_o", "moba_w_qkv", "moba_w_o", "mlp_w_in", "mlp_w_out"):
        sh[k] = np.ascontiguousarray(inp[k], dtype=np.float32)
    return sh


def kernel(**inputs):
    x = np.asarray(inputs["x"], np.float32); pos = np.asarray(inputs["positions"], np.int32)
    B = x.shape[0]
    ncores = 8
    nseq = B // ncores
    sh = _prep_shared(inputs)
    nc = KB(nseq=nseq).build()
    in_maps = []
    for c in range(ncores):
        m = dict(sh)
        m["x"] = np.ascontiguousarray(x[c * nseq:(c + 1) * nseq])
        m["pos"] = np.ascontiguousarray(pos[c * nseq:(c + 1) * nseq])
        in_maps.append(m)
    res = run_bass_kernel_spmd(nc, in_maps, core_ids=list(range(ncores)))
    out = np.concatenate([np.asarray(r["out"], np.float32) for r in res.results], axis=0)
    return out
```

```python
import math
import numpy as np
from contextlib import ExitStack
import concourse.bass as bass
import concourse.mybir as mybir
from concourse.bass_utils import run_bass_kernel_spmd

F32 = mybir.dt.float32; BF16 = mybir.dt.bfloat16; I32 = mybir.dt.int32
ALU = mybir.AluOpType; AF = mybir.ActivationFunctionType; AX = mybir.AxisListType

SAME_ENGINE_SYNC = True
COMPUTE = ("pe", "act", "dve", "pool")
S = 2048; D = 1024; NCH = 8; TG = 512; NG = 4; DFF = 4096
EPS = 1e-6
NEG = -30000.0
SB_BASE = 16512; SB_END = 229344


class Buf:
    __slots__ = ("name", "writers", "readers", "dsem", "dcnt", "excl")

    def __init__(self, name, excl=False):
        self.name = name; self.writers = []; self.readers = []; self.dsem = None; self.dcnt = 0
        self.excl = excl


class Op:
    __slots__ = ("eng", "idx", "fn", "deps", "marked", "dma")

    def __init__(self, eng, idx, fn, dma=None):
        self.eng = eng; self.idx = idx; self.fn = fn; self.deps = []; self.marked = False; self.dma = dma


class Prog:
    def __init__(self, nc):
        self.nc = nc
        self.streams = {e: [] for e in ("pe", "act", "dve", "pool", "sp")}
        self.seen = {e: {} for e in self.streams}
        self.dbufs = []

    def _resolve(self, eng, toks):
        out = []
        seen = self.seen[eng]
        for t in toks:
            if t[0] == "c":
                _, E, j = t
                if E == eng and (eng == "pe" or not SAME_ENGINE_SYNC):
                    continue
                if seen.get(E, -1) >= j:
                    continue
                seen[E] = j
                self.streams[E][j].marked = True
                out.append(t)
            else:
                b = t[1]
                v = b.dcnt
                key = id(b)
                if seen.get(key, -1) >= v:
                    continue
                seen[key] = v
                out.append(("d", b, v))
        return out

    def _record(self, eng, op, tok, reads, writes):
        toks = []
        def other(ts):
            return [t for t in ts if not (t[0] == "c" and t[1] == eng)]
        for b in reads:
            toks.extend(b.writers)
            if b.excl:
                toks.extend(other(b.readers))
        for b in writes:
            toks.extend(b.writers); toks.extend(other(b.readers))
        op.deps = self._resolve(eng, toks)
        for b in reads:
            b.readers.append(tok)
        for b in writes:
            b.writers = [tok]; b.readers = []
        self.streams[eng].append(op)

    def op(self, eng, fn, reads=(), writes=()):
        o = Op(eng, len(self.streams[eng]), fn)
        self._record(eng, o, ("c", eng, o.idx), reads, writes)

    def dma(self, q, out_ap, in_ap, sembuf, reads=(), writes=()):
        if sembuf.dsem is None:
            sembuf.dsem = True
            self.dbufs.append(sembuf)
        o = Op(q, len(self.streams[q]), None, dma=(out_ap, in_ap, sembuf))
        self._record(q, o, ("d", sembuf, sembuf.dcnt + 16), reads, writes)
        sembuf.dcnt += 16

    def emit(self):
        nc = self.nc
        with ExitStack() as es:
            esem = {e: es.enter_context(nc.semaphore("sem_" + e)) for e in COMPUTE}
            for i, b in enumerate(self.dbufs):
                b.dsem = es.enter_context(nc.semaphore("ds%d" % i))
            val = {}
            for e in COMPUTE:
                c = 0
                for o in self.streams[e]:
                    if o.marked:
                        c += 1
                    val[(e, o.idx)] = c
            block = es.enter_context(nc.Block())

            def run(engobj, name):
                for o in self.streams[name]:
                    for t in o.deps:
                        if t[0] == "c":
                            engobj.wait_ge(esem[t[1]], val[(t[1], t[2])])
                        else:
                            engobj.wait_ge(t[1].dsem, t[2])
                    if o.dma is not None:
                        out_ap, in_ap, sb = o.dma
                        engobj.dma_start(out=out_ap, in_=in_ap).then_inc(sb.dsem, 16)
                    else:
                        ins = o.fn(engobj)
                        if o.marked:
                            ins.then_inc(esem[name], 1)
                if name == "sp":
                    for b in self.dbufs:
                        engobj.wait_ge(b.dsem, b.dcnt)

            @block.tensor
            def _(e): run(e, "pe")

            @block.scalar
            def _(e): run(e, "act")

            @block.vector
            def _(e): run(e, "dve")

            @block.gpsimd
            def _(e): run(e, "pool")

            @block.sync
            def _(e): run(e, "sp")


def _dsize(dt):
    return 2 if dt == BF16 else 4


class Arena:
    def __init__(self, nc, base, end):
        self.nc = nc; self.base = base; self.end = end; self.top = base; self.hist = []; self.n = 0

    def alloc(self, name, shape, dtype, nbufs=None):
        nbytes = int(np.prod(shape[1:])) * _dsize(dtype)
        start = (self.top + 31) // 32 * 32
        assert start + nbytes <= self.end, "SBUF arena overflow: %s needs %d at %d (end %d)" % (name, nbytes, start, self.end)
        self.n += 1
        h = self.nc.alloc_sbuf_tensor_at("%s_%d" % (name, self.n), list(shape), dtype, offset=start)
        inh = []
        for (s0, e0, ob) in self.hist:
            if s0 < start + nbytes and e0 > start:
                inh.extend(ob.writers); inh.extend(ob.readers)
        bufs = []
        for i in range(nbufs or 1):
            b = Buf(name if nbufs is None else "%s.%d" % (name, i))
            b.writers.extend(inh)
            bufs.append(b)
        for b in bufs:
            self.hist.append((start, start + nbytes, b))
        self.top = start + nbytes
        return (h, bufs[0]) if nbufs is None else (h, bufs)

    def mark(self):
        return self.top

    def release(self, m):
        self.top = m


class Ring:
    def __init__(self, arena, name, shape, dtype, n):
        self.items = [arena.alloc("%s%d" % (name, i), shape, dtype) for i in range(n)]
        self.i = 0

    def next(self):
        it = self.items[self.i % len(self.items)]
        self.i += 1
        return it


class PsRing:
    def __init__(self, items):
        self.items = items; self.i = 0

    def next(self):
        it = self.items[self.i % len(self.items)]
        self.i += 1
        return it


CF_ID = 0; CF_ROT = 128; CF_INVF = 192; CF_NEGM = 193; CF_PAST = 257; CF_CSTRIP = 321; CF_W = 321 + 896
CB_ID = 0; CB_ONES = 128; CB_IND = 256; CB_W = 256 + 1024
STRIP_W = 1920
G_AN = 0; G_MN = 32; G_MLA = 64; G_MOBA = 82; G_W = 86


def _t5_bucket_np(d):
    n = np.maximum(d, 0)
    nf = np.maximum(n, 1).astype(np.float32)
    large = 16 + (np.log(nf / np.float32(16)) / np.float32(math.log(64.0)) * np.float32(16)).astype(np.int32)
    large = np.minimum(large, 31)
    return np.where(n < 16, n, large)


def _host_consts():
    cf = np.zeros((128, CF_W), np.float32)
    cf[:, CF_ID:CF_ID + 128] = np.eye(128, dtype=np.float32)
    rotT = np.zeros((64, 64), np.float32)
    for m in range(32):
        rotT[m + 32, m] = -1.0
        rotT[m, m + 32] = 1.0
    cf[:64, CF_ROT:CF_ROT + 64] = rotT
    inv = (10000.0 ** (-np.arange(0, 64, 2, dtype=np.float32) / np.float32(64))).astype(np.float32)
    cf[:32, CF_INVF] = inv; cf[32:64, CF_INVF] = inv
    for i in range(8):
        own = 4 + i // 2
        for n in range(8):
            cf[:, CF_NEGM + i * 8 + n] = -1e30 if n >= own else 0.0
            cf[:, CF_PAST + i * 8 + n] = 30000.0 if n < own else 0.0
    k = np.arange(128)[:, None]; j = np.arange(896)[None, :]
    cf[:, CF_CSTRIP:CF_CSTRIP + 896] = np.where(j - k - 384 >= 0, 0.0, NEG)
    cb = np.zeros((128, CB_W), np.float32)
    cb[:, CB_ID:CB_ID + 128] = np.eye(128, dtype=np.float32)
    cb[:, CB_ONES:CB_ONES + 128] = 1.0
    for n in range(8):
        cb[n, CB_IND + n * 128: CB_IND + (n + 1) * 128] = 1.0
    return cf, cb


def _host_gains(inp):
    g = np.zeros((128, G_W), np.float32)

    def put(col, vec):
        vec = np.asarray(vec, np.float32)
        n = vec.shape[0]
        if n >= 128:
            c = n // 128
            g[:, col:col + c] = vec.reshape(c, 128).T
        else:
            g[:n, col] = vec
    for l in range(4):
        put(G_AN + 8 * l, inp["attn_norm"][l]); put(G_MN + 8 * l, inp["mlp_norm"][l])
    for j in range(2):
        b = G_MLA + 9 * j
        put(b, inp["mla_q_a_norm"][j]); put(b + 3, inp["mla_kv_a_norm"][j])
        put(b + 5, inp["mla_q_nope_norm"][j]); put(b + 6, inp["mla_q_rope_norm"][j])
        put(b + 7, inp["mla_k_nope_norm"][j]); put(b + 8, inp["mla_k_rope_norm"][j])
        b2 = G_MOBA + 2 * j
        put(b2, inp["moba_q_norm"][j]); put(b2 + 1, inp["moba_k_norm"][j])
    return g


def _host_strip(table):
    k = np.arange(128)[:, None]; j = np.arange(STRIP_W)[None, :]
    d = j - k - 384
    bidx = _t5_bucket_np(d)
    t = np.asarray(table, np.float32)
    strip = np.transpose(t[bidx], (2, 0, 1)).copy()
    strip[:, d < 0] = NEG
    return np.ascontiguousarray(strip, dtype=np.float32)


class KB:
    def __init__(self, nseq=2, plan=None, debug_out=False, stop=99):
        self.nseq = nseq; self.stop = stop
        self.plan = plan if plan is not None else [("mla", 0, 0), ("mlp", 0, 0), ("moba", 0, 1), ("mlp", 1, 1),
                                                    ("mla", 1, 2), ("mlp", 2, 2), ("moba", 1, 3), ("mlp", 3, 3)]
        nc = self.nc = bass.Bass("TRN2", target_bir_lowering=False)
        self.P = Prog(nc)
        dt = nc.dram_tensor
        self.d = {}
        def inp(name, shape, dtype=F32):
            self.d[name] = dt(name, list(shape), dtype, kind="ExternalInput")
        inp("x", [nseq, S, D]); self.d["pos"] = None
        inp("cf", [128, CF_W]); inp("cb", [128, CB_W]); inp("gains", [128, G_W])
        kinds = set(k for (k, _, _) in self.plan)
        if "mla" in kinds:
            inp("pos", [nseq, S], I32)
            inp("mla_w_in", [2, 1024, 704]); inp("mla_w_uq", [2, 384, 1536]); inp("mla_w_ukv", [2, 256, 2048])
            inp("mla_w_o", [2, 1024, 1024])
        else:
            del self.d["pos"]
        if "moba" in kinds:
            inp("strip", [8, 128, STRIP_W]); inp("moba_w_qkv", [2, 1024, 3072]); inp("moba_w_o", [2, 1024, 1024])
        if "mlp" in kinds:
            inp("mlp_w_in", [4, 1024, 4096]); inp("mlp_w_out", [4, 4096, 1024])
        self.d["out"] = dt("out", [nseq, S, D], F32, kind="ExternalOutput")
        off = SB_BASE
        def static(name, shape, dtype):
            nonlocal off
            nbytes = int(np.prod(shape[1:])) * _dsize(dtype)
            h = nc.alloc_sbuf_tensor_at(name, list(shape), dtype, offset=off)
            off = (off + nbytes + 31) // 32 * 32
            return h
        self.CF = static("CF", [128, CF_W], F32); self.CFb = Buf("CF")
        self.CB = static("CB", [128, CB_W], BF16); self.CBb = Buf("CB")
        self.G = static("G", [128, G_W], F32); self.Gb = Buf("G")
        self.X = static("X", [128, NCH, S], F32)
        self.XB = [[Buf("X%d_%d" % (c, g)) for g in range(NG)] for c in range(NCH)]
        self.A = Arena(nc, off, SB_END)
        self.PS = []
        for i in range(8):
            h = nc.alloc_psum_tensor("ps%d" % i, [128, 512], F32)
            self.PS.append((h, Buf("ps%d" % i, excl=True)))
        self.psAll = PsRing(self.PS)
        self.psA = PsRing(self.PS[0:5]); self.psB = PsRing(self.PS[5:8])
        P = self.P
        P.dma("sp", self.CF[:], self.d["cf"].ap(), self.CFb, writes=[self.CFb])
        P.dma("pool", self.CB[:], self.d["cb"].ap(), self.CBb, writes=[self.CBb])
        P.dma("sp", self.G[:], self.d["gains"].ap(), self.Gb, writes=[self.Gb])
        self.ident = self.CF[:, CF_ID:CF_ID + 128]
        self.onesb = self.CB[:, CB_ONES:CB_ONES + 128]
        self.evac_i = 0

    def dump(self, name, ap, buf):
        shape = [int(v) for v in ap.shape]
        dtn = self.nc.dram_tensor("dbg_" + name, shape, F32, kind="ExternalOutput")
        self.d["dbg_" + name] = dtn
        self.P.dma("pool", dtn.ap(), ap, buf, reads=[buf])

    def mm(self, out, lhsT, rhs, start, stop, reads, writes):
        self.P.op("pe", lambda e: e.matmul(out, lhsT=lhsT, rhs=rhs, start=start, stop=stop), reads=reads, writes=writes)

    def tr(self, out, in_, reads, writes):
        ident = self.ident[0:in_.shape[0], 0:in_.shape[0]]
        self.P.op("pe", lambda e: e.transpose(out, in_, ident), reads=list(reads) + [self.CFb], writes=writes)

    def act(self, out, in_, func, reads, writes, **kw):
        self.P.op("act", lambda e: e.activation(out, in_, func, **kw), reads=reads, writes=writes)

    def copy(self, out, in_, reads, writes, eng=None):
        if eng is None:
            eng = "act" if (self.evac_i % 2 == 0) else "dve"
            self.evac_i += 1
        if eng == "act":
            self.P.op("act", lambda e: e.copy(out, in_), reads=reads, writes=writes)
        else:
            self.P.op("dve", lambda e: e.tensor_copy(out, in_), reads=reads, writes=writes)

    def dve(self, fn, reads, writes):
        self.P.op("dve", fn, reads=reads, writes=writes)

    def rstd_from_psum(self, ps, npart, n_total, rs, reads, rsb):
        self.act(rs[0:npart, :], ps[0:npart, :], AF.Ln, reads=reads, writes=[rsb], bias=EPS, scale=1.0 / n_total)
        self.act(rs[0:npart, :], rs[0:npart, :], AF.Exp, reads=[rsb], writes=[rsb], scale=-0.5)

    def load_x(self, s):
        A = self.A; m = A.mark()
        ring = Ring(A, "xin", [128, D], F32, 2)
        xd = self.d["x"].ap()
        for g in range(NG):
            banks = [self.psAll.next() for _ in range(NCH)]
            for j in range(4):
                tt = g * 4 + j
                st, sb = ring.next()
                self.P.dma("sp", st[:], xd[s, tt * 128:(tt + 1) * 128, :], sb, writes=[sb])
                for c in range(NCH):
                    ph, pb = banks[c]
                    self.tr(ph[:, j * 128:(j + 1) * 128], st[:, c * 128:(c + 1) * 128], reads=[sb], writes=[pb])
            for c in range(NCH):
                ph, pb = banks[c]
                self.copy(self.X[:, c, g * TG:(g + 1) * TG], ph[:], reads=[pb], writes=[self.XB[c][g]])
        A.release(m)

    def store_x(self, s):
        A = self.A; m = A.mark()
        ring = Ring(A, "xout", [128, D], F32, 2)
        od = self.d["out"].ap()
        for tt in range(16):
            g = tt // 4
            st, sb = ring.next()
            for half in range(2):
                ph, pb = self.psAll.next()
                for cc in range(4):
                    c = half * 4 + cc
                    self.tr(ph[:, cc * 128:(cc + 1) * 128], self.X[:, c, tt * 128:(tt + 1) * 128],
                            reads=[self.XB[c][g]], writes=[pb])
                self.copy(st[:, half * 512:(half + 1) * 512], ph[:], reads=[pb], writes=[sb])
            self.P.dma("sp", od[s, tt * 128:(tt + 1) * 128, :], st[:], sb, reads=[sb])
        A.release(m)

    def norm_x(self, gcol, H, HB, sqr, rsr, ps_ring):
        for g in range(NG):
            ph, pb = ps_ring.next()
            sl = slice(g * TG, (g + 1) * TG)
            for c in range(NCH):
                sq, sqb = sqr.next()
                self.act(sq[:], self.X[:, c, sl], AF.Square, reads=[self.XB[c][g]], writes=[sqb])
                self.mm(ph[:], self.onesb, sq[:], c == 0, c == NCH - 1, reads=[sqb, self.CBb], writes=[pb])
            rs, rsb = rsr.next()
            self.rstd_from_psum(ph, 128, float(D), rs, [pb], rsb)
            for c in range(NCH):
                gc = self.G[:, gcol + c:gcol + c + 1]
                self.dve(lambda e, c=c, gc=gc, rs=rs, sl=sl: e.scalar_tensor_tensor(
                    H[:, c, sl], self.X[:, c, sl], gc, rs[:], ALU.mult, ALU.mult),
                    reads=[self.XB[c][g], rsb, self.Gb], writes=[HB[g]])

    def mlp(self, l):
        A = self.A; P = self.P; m0 = A.mark()
        H, HB = A.alloc("H", [128, NCH, S], BF16, nbufs=NG)
        m1 = A.mark()
        sqr = Ring(A, "sq", [128, TG], BF16, 4); rsr = Ring(A, "rs", [128, TG], F32, 2)
        self.norm_x(G_MN + 8 * l, H, HB, sqr, rsr, self.psAll)
        A.release(m1)
        W1r = Ring(A, "W1", [128, 8, 1024], BF16, 2); W2r = Ring(A, "W2", [128, 8, 1024], BF16, 2)
        Ar = Ring(A, "Aa", [128, 8, TG], BF16, 2); Rr = Ring(A, "Rr", [128, TG], F32, 3)
        w_in = self.d["mlp_w_in"].ap(); w_out = self.d["mlp_w_out"].ap()

        def load_w(q):
            (w1, b1) = W1r.next(); (w2, b2) = W2r.next()
            src1 = w_in[l, :, q * 1024:(q + 1) * 1024].rearrange("(c p) n -> p c n", p=128)
            src2 = w_out[l, q * 1024:(q + 1) * 1024, :].rearrange("(c p) n -> p c n", p=128)
            for cc in range(0, 8, 2):
                P.dma("pool", w1[:, cc:cc + 2, :], src1[:, cc:cc + 2, :], b1, writes=[b1])
            for cc in range(0, 8, 2):
                P.dma("pool", w2[:, cc:cc + 2, :], src2[:, cc:cc + 2, :], b2, writes=[b2])
            return (w1, b1, w2, b2)

        def up(W, g):
            w1, b1, _, _ = W
            a, ab = Ar.next()
            sl = slice(g * TG, (g + 1) * TG)
            for f in range(8):
                ph, pb = self.psAll.next()
                for k in range(NCH):
                    self.mm(ph[:], w1[:, k, f * 128:(f + 1) * 128], H[:, k, sl], k == 0, k == NCH - 1,
                            reads=[b1, HB[g]], writes=[pb])
                r, rb = Rr.next()
                self.act(r[:], ph[:], AF.Relu, reads=[pb], writes=[rb])
                self.dve(lambda e, a=a, f=f, r=r: e.tensor_tensor(a[:, f, :], r[:], r[:], ALU.mult), reads=[rb], writes=[ab])
            return (a, ab)

        def down(W, g, at):
            _, _, w2, b2 = W
            a, ab = at
            sl = slice(g * TG, (g + 1) * TG)
            for mch in range(NCH):
                ph, pb = self.psAll.next()
                for f in range(8):
                    self.mm(ph[:], w2[:, f, mch * 128:(mch + 1) * 128], a[:, f, :], f == 0, f == 7,
                            reads=[b2, ab], writes=[pb])
                xs = self.X[:, mch, sl]
                self.dve(lambda e, xs=xs, ph=ph: e.tensor_tensor(xs, xs, ph[:], ALU.add),
                         reads=[pb, self.XB[mch][g]], writes=[self.XB[mch][g]])

        Ws = {0: load_w(0)}
        steps = [(q, g) for q in range(4) for g in range(NG)]
        pend = None
        for i, (q, g) in enumerate(steps):
            at = up(Ws[q], g)
            if pend is not None:
                down(*pend)
            pend = (Ws[q], g, at)
            if g == 0 and q + 1 < 4:
                Ws[q + 1] = load_w(q + 1)
        down(*pend)
        A.release(m0)

    def rope_tables(self, s, cos, cosb, sin, sinb):
        A = self.A; m = A.mark()
        pi_, pib = A.alloc("posi", [64, S], I32)
        ang, angb = A.alloc("ang", [64, S], F32)
        kf, kfb = A.alloc("kf", [64, S], F32)
        r, rb = A.alloc("r", [64, S], F32)
        shh, shb = A.alloc("shh", [64, S], F32)
        chh, chb = A.alloc("chh", [64, S], F32)
        src = bass.AP(self.d["pos"], s * S, [[0, 64], [1, S]])
        self.P.dma("sp", pi_[:], src, pib, writes=[pib])
        self.dve(lambda e: e.tensor_copy(ang[:], pi_[:]), reads=[pib], writes=[angb])
        invf = self.CF[0:64, CF_INVF:CF_INVF + 1]
        self.dve(lambda e: e.tensor_scalar(ang[:], ang[:], invf, None, ALU.mult), reads=[angb, self.CFb], writes=[angb])
        self.dve(lambda e: e.tensor_scalar(pi_[:], ang[:], 1.0 / (2.0 * math.pi), None, ALU.mult), reads=[angb], writes=[pib])
        self.dve(lambda e: e.tensor_copy(kf[:], pi_[:]), reads=[pib], writes=[kfb])
        self.dve(lambda e: e.scalar_tensor_tensor(r[:], kf[:], -2.0 * math.pi, ang[:], ALU.mult, ALU.add),
                 reads=[kfb, angb], writes=[rb])
        sh = 0.999999
        self.act(shh[:], r[:], AF.Sin, reads=[rb], writes=[shb], scale=0.5 * sh)
        self.act(chh[:], r[:], AF.Sin, reads=[rb], writes=[chb], bias=0.5 * math.pi * sh, scale=-0.5 * sh)
        self.dve(lambda e: e.scalar_tensor_tensor(sin[:], shh[:], 2.0, chh[:], ALU.mult, ALU.mult), reads=[shb, chb], writes=[sinb])
        self.dve(lambda e: e.tensor_tensor(kf[:], shh[:], shh[:], ALU.mult), reads=[shb], writes=[kfb])
        self.dve(lambda e: e.tensor_scalar(cos[:], kf[:], -2.0, 1.0, ALU.mult, ALU.add), reads=[kfb], writes=[cosb])
        A.release(m)

    def head_norm(self, ph, pb, npart, gcol, sqr, rsr, psr, out_fn):
        sq, sqb = sqr.next()
        self.act(sq[0:npart, :], ph[0:npart, :], AF.Square, reads=[pb], writes=[sqb])
        p2, p2b = psr.next()
        self.mm(p2[0:npart, :], self.onesb[0:npart, 0:npart], sq[0:npart, :], True, True, reads=[sqb, self.CBb], writes=[p2b])
        rs, rsb = rsr.next()
        self.rstd_from_psum(p2, npart, float(npart), rs, [p2b], rsb)
        gc = self.G[0:npart, gcol:gcol + 1]
        out_fn(gc, rs, rsb)

    def attention(self, qT, qb, kT, kb, V, Vb, extra_qk, scale, strip_fn, mask_fn, far_bias,
                  outT, outb, Pr, Tr, rsr):
        LOOK = 3
        for qt in range(4):
            qs = slice(qt * TG, (qt + 1) * TG)
            oh, ob = self.psB.next(); lh, lb = self.psB.next()
            nk = 4 * qt + 4

            def qk(kt):
                ks = slice(kt * 128, (kt + 1) * 128)
                sh_, sb_ = self.psA.next()
                mk = mask_fn(qt, kt) if mask_fn is not None else None
                last_qk = (extra_qk is None) and (mk is None)
                self.mm(sh_[:], kT[:, ks], qT[:, qs], True, last_qk, reads=[kb, qb], writes=[sb_])
                if extra_qk is not None:
                    l2, r2, rd2 = extra_qk
                    self.mm(sh_[:], l2(kt), r2(qt), False, mk is None, reads=rd2, writes=[sb_])
                if mk is not None:
                    ml, mr, mrd = mk
                    self.mm(sh_[:], ml, mr, False, True, reads=mrd, writes=[sb_])
                p, pbuf = Pr.next()
                st = strip_fn(qt, kt)
                if st is not None:
                    sap, sbuf = st
                    t, tb = Tr.next()
                    self.dve(lambda e, t=t, sh_=sh_, sap=sap: e.scalar_tensor_tensor(
                        t[:], sh_[:], scale, sap, ALU.mult, ALU.add), reads=[sb_, sbuf], writes=[tb])
                    self.act(p[:], t[:], AF.Exp, reads=[tb], writes=[pbuf])
                elif far_bias is not None:
                    fb, fbb = far_bias
                    self.act(p[:], sh_[:], AF.Exp, reads=[sb_, fbb], writes=[pbuf], bias=fb, scale=scale)
                else:
                    self.act(p[:], sh_[:], AF.Exp, reads=[sb_], writes=[pbuf], scale=scale)
                return (kt, p, pbuf)

            def pv(item):
                kt, p, pbuf = item
                self.mm(oh[:], V[:, kt, :], p[:], kt == 0, kt == nk - 1, reads=[Vb, pbuf], writes=[ob])
                self.mm(lh[:], self.onesb, p[:], kt == 0, kt == nk - 1, reads=[pbuf, self.CBb], writes=[lb])

            pend = []
            for kt in range(nk):
                pend.append(qk(kt))
                if len(pend) > LOOK:
                    pv(pend.pop(0))
            while pend:
                pv(pend.pop(0))
            rl, rlb = rsr.next()
            self.act(rl[:], lh[:], AF.Ln, reads=[lb], writes=[rlb])
            self.act(rl[:], rl[:], AF.Exp, reads=[rlb], writes=[rlb], scale=-1.0)
            self.dve(lambda e, rl=rl, oh=oh, qs=qs: e.tensor_tensor(outT[:, qs], oh[:], rl[:], ALU.mult),
                     reads=[ob, rlb], writes=[outb])

    def wo_half(self, wname, j, half, attn, attnb, Wo, Wob):
        wd = self.d[wname].ap()
        src = wd[j, half * 512:(half + 1) * 512, :].rearrange("(c p) n -> p c n", p=128)
        for hh in range(4):
            self.P.dma("pool", Wo[:, hh:hh + 1, :], src[:, hh:hh + 1, :], Wob, writes=[Wob])
        for g in range(NG):
            sl = slice(g * TG, (g + 1) * TG)
            for mch in range(NCH):
                ph, pb = self.psA.next()
                for hh in range(4):
                    self.mm(ph[:], Wo[:, hh, mch * 128:(mch + 1) * 128], attn[:, hh, sl], hh == 0, hh == 3,
                            reads=[Wob, attnb[hh]], writes=[pb])
                xs = self.X[:, mch, sl]
                self.dve(lambda e, xs=xs, ph=ph: e.tensor_tensor(xs, xs, ph[:], ALU.add),
                         reads=[pb, self.XB[mch][g]], writes=[self.XB[mch][g]])

    def mla(self, j, l, s):
        A = self.A; P = self.P; m0 = A.mark()
        gb = G_MLA + 9 * j
        cqn, cqnB = A.alloc("cqn", [128, 3, S], BF16, nbufs=NG)
        ckvn, ckvnB = A.alloc("ckvn", [128, 2, S], BF16, nbufs=NG)
        kpe, kpeb = A.alloc("kpe", [64, S], BF16)
        cos, cosb = A.alloc("cos", [64, S], F32); sin, sinb = A.alloc("sin", [64, S], F32)
        self.rope_tables(s, cos, cosb, sin, sinb)
        rotT = self.CF[0:64, CF_ROT:CF_ROT + 64]
        if self.stop <= 1:
            A.release(m0); return
        m1 = A.mark()
        H, HB = A.alloc("H", [128, NCH, S], BF16, nbufs=NG)
        Win, Winb = A.alloc("Win", [128, NCH, 704], BF16)
        src = self.d["mla_w_in"].ap()[j].rearrange("(c p) n -> p c n", p=128)
        for cc in range(0, 8, 4):
            P.dma("pool", Win[:, cc:cc + 4, :], src[:, cc:cc + 4, :], Winb, writes=[Winb])
        sqr = Ring(A, "sq", [128, TG], BF16, 4); rsr = Ring(A, "rs", [128, TG], F32, 3)
        rawr = Ring(A, "raw", [128, 3, TG], F32, 2)
        Tr = Ring(A, "tmp", [128, TG], F32, 3)
        self.norm_x(G_AN + 8 * l, H, HB, sqr, rsr, self.psA)
        for g in range(NG):
            sl = slice(g * TG, (g + 1) * TG)
            for (dst, dstB, c0, nchunk, gcol) in ((cqn, cqnB, 0, 3, gb), (ckvn, ckvnB, 3, 2, gb + 3)):
                raw, rawb = rawr.next()
                p2, p2b = self.psB.next()
                for cc in range(nchunk):
                    ph, pb = self.psA.next()
                    col = (c0 + cc) * 128
                    for k in range(NCH):
                        self.mm(ph[:], Win[:, k, col:col + 128], H[:, k, sl], k == 0, k == NCH - 1,
                                reads=[Winb, HB[g]], writes=[pb])
                    self.copy(raw[:, cc, :], ph[:], reads=[pb], writes=[rawb], eng="dve")
                    sq, sqb = sqr.next()
                    self.act(sq[:], ph[:], AF.Square, reads=[pb], writes=[sqb])
                    self.mm(p2[:], self.onesb, sq[:], cc == 0, cc == nchunk - 1, reads=[sqb, self.CBb], writes=[p2b])
                rs, rsb = rsr.next()
                self.rstd_from_psum(p2, 128, float(nchunk * 128), rs, [p2b], rsb)
                for cc in range(nchunk):
                    gc = self.G[:, gcol + cc:gcol + cc + 1]
                    self.dve(lambda e, dst=dst, cc=cc, raw=raw, gc=gc, rs=rs, sl=sl: e.scalar_tensor_tensor(
                        dst[:, cc, sl], raw[:, cc, :], gc, rs[:], ALU.mult, ALU.mult),
                        reads=[rawb, rsb, self.Gb], writes=[dstB[g]])
            ph, pb = self.psA.next()
            for k in range(NCH):
                self.mm(ph[0:64, :], Win[:, k, 640:704], H[:, k, sl], k == 0, k == NCH - 1, reads=[Winb, HB[g]], writes=[pb])
            self.rope_head(ph, pb, gb + 8, sqr, rsr, Tr, cos, cosb, sin, sinb, rotT, sl, kpe[:, sl], kpeb)
        A.release(m1)
        if self.stop <= 2:
            for nm, t, bb in (("cqn", cqn, cqnB), ("ckvn", ckvn, ckvnB)):
                for g in range(NG):
                    self.dump("%s%d" % (nm, g), t[:, :, g * TG:(g + 1) * TG], bb[g])
            self.dump("kpe", kpe[:], kpeb); self.dump("cos", cos[:], cosb); self.dump("sin", sin[:], sinb)
            A.release(m0); return
        Hd = [dict(qn=A.alloc("qn%d" % i, [128, S], BF16), qr=A.alloc("qr%d" % i, [64, S], BF16),
                   kn=A.alloc("kn%d" % i, [128, S], BF16), V=A.alloc("V%d" % i, [128, 16, 128], BF16),
                   wq=A.alloc("wq%d" % i, [128, 3, 192], BF16), wkv=A.alloc("wkv%d" % i, [128, 2, 256], BF16))
              for i in range(2)]
        attn, attnB = A.alloc("attn", [128, 4, S], BF16, nbufs=4)
        Wor = Ring(A, "Wo", [128, 4, 1024], BF16, 2)
        Pr = Ring(A, "P", [128, TG], BF16, 5)
        Tr = Ring(A, "tmp", [128, TG], F32, 3)
        sqr = Ring(A, "sq", [128, TG], BF16, 4); rsr = Ring(A, "rs", [128, TG], F32, 3)
        wuq = self.d["mla_w_uq"].ap(); wukv = self.d["mla_w_ukv"].ap()
        scale = 1.0 / math.sqrt(192.0)

        def proj(h):
            hd = Hd[h % 2]
            wq, wqb = hd["wq"]; wkv, wkvb = hd["wkv"]
            P.dma("pool", wq[:], wuq[j, :, h * 192:(h + 1) * 192].rearrange("(c p) n -> p c n", p=128), wqb, writes=[wqb])
            P.dma("pool", wkv[:], wukv[j, :, h * 256:(h + 1) * 256].rearrange("(c p) n -> p c n", p=128), wkvb, writes=[wkvb])
            qn, qnb = hd["qn"]; qr, qrb = hd["qr"]; kn, knb = hd["kn"]; V, Vb = hd["V"]
            for g in range(NG):
                sl = slice(g * TG, (g + 1) * TG)
                ph, pb = self.psA.next()
                for c in range(3):
                    self.mm(ph[:], wq[:, c, 0:128], cqn[:, c, sl], c == 0, c == 2, reads=[wqb, cqnB[g]], writes=[pb])
                self.head_norm(ph, pb, 128, gb + 5, sqr, rsr, self.psA,
                               lambda gc, rs, rsb, ph=ph, pb=pb, sl=sl: self.dve(lambda e: e.scalar_tensor_tensor(
                                   qn[:, sl], ph[:], gc, rs[:], ALU.mult, ALU.mult), reads=[pb, rsb, self.Gb], writes=[qnb]))
                ph, pb = self.psA.next()
                for c in range(3):
                    self.mm(ph[0:64, :], wq[:, c, 128:192], cqn[:, c, sl], c == 0, c == 2, reads=[wqb, cqnB[g]], writes=[pb])
                self.rope_head(ph, pb, gb + 6, sqr, rsr, Tr, cos, cosb, sin, sinb, rotT, sl, qr[:, sl], qrb)
                ph, pb = self.psA.next()
                for c in range(2):
                    self.mm(ph[:], wkv[:, c, 0:128], ckvn[:, c, sl], c == 0, c == 1, reads=[wkvb, ckvnB[g]], writes=[pb])
                self.head_norm(ph, pb, 128, gb + 7, sqr, rsr, self.psA,
                               lambda gc, rs, rsb, ph=ph, pb=pb, sl=sl: self.dve(lambda e: e.scalar_tensor_tensor(
                                   kn[:, sl], ph[:], gc, rs[:], ALU.mult, ALU.mult), reads=[pb, rsb, self.Gb], writes=[knb]))
                ph, pb = self.psA.next()
                for t4 in range(4):
                    ts_ = slice(g * TG + t4 * 128, g * TG + (t4 + 1) * 128)
                    for c in range(2):
                        self.mm(ph[:, t4 * 128:(t4 + 1) * 128], ckvn[:, c, ts_], wkv[:, c, 128:256], c == 0, c == 1,
                                reads=[wkvb, ckvnB[g]], writes=[pb])
                self.copy(V[:, g * 4:(g + 1) * 4, :], ph[:].rearrange("p (a b) -> p a b", a=4), reads=[pb], writes=[Vb])

        def attend(h):
            hd = Hd[h % 2]
            qn, qnb = hd["qn"]; qr, qrb = hd["qr"]; kn, knb = hd["kn"]; V, Vb = hd["V"]
            hh = h % 4

            def strip_fn(qt, kt):
                if kt < 4 * qt:
                    return None
                s0 = CF_CSTRIP + (512 * qt - 128 * kt) + 384
                return (self.CF[:, s0:s0 + 512], self.CFb)
            extra = (lambda kt: kpe[:, kt * 128:(kt + 1) * 128], lambda qt: qr[:, qt * TG:(qt + 1) * TG], [kpeb, qrb])
            self.attention(qn, qnb, kn, knb, V, Vb, extra, scale, strip_fn, None, None,
                           attn[:, hh, :], attnB[hh], Pr, Tr, rsr)

        proj(0)
        if self.stop <= 3:
            for nm in ("qn", "qr", "kn", "V"):
                t, bb = Hd[0][nm]
                self.dump(nm, t[:], bb)
            A.release(m0); return
        for h in range(8):
            if h + 1 < 8:
                proj(h + 1)
            attend(h)
            if self.stop <= 4:
                self.dump("attn0", attn[:, 0, :], attnB[0])
                A.release(m0); return
            if h % 4 == 3:
                Wo, Wob = Wor.next()
                self.wo_half("mla_w_o", j, h // 4, attn, attnB, Wo, Wob)
        A.release(m0)

    def rope_head(self, ph, pb, gcol, sqr, rsr, Tr, cos, cosb, sin, sinb, rotT, sl, dst, dstb):
        def fin(gc, rs, rsb):
            xn, xnb = Tr.next()
            self.dve(lambda e: e.scalar_tensor_tensor(xn[0:64, :], ph[0:64, :], gc, rs[0:64, :], ALU.mult, ALU.mult),
                     reads=[pb, rsb, self.Gb], writes=[xnb])
            p3, p3b = self.psA.next()
            self.mm(p3[0:64, :], rotT, xn[0:64, :], True, True, reads=[xnb, self.CFb], writes=[p3b])
            t1, t1b = Tr.next()
            self.dve(lambda e: e.tensor_tensor(t1[0:64, :], xn[0:64, :], cos[:, sl], ALU.mult), reads=[xnb, cosb], writes=[t1b])
            t2, t2b = Tr.next()
            self.dve(lambda e: e.tensor_tensor(t2[0:64, :], p3[0:64, :], sin[:, sl], ALU.mult), reads=[p3b, sinb], writes=[t2b])
            self.dve(lambda e: e.tensor_tensor(dst, t1[0:64, :], t2[0:64, :], ALU.add), reads=[t1b, t2b], writes=[dstb])
        self.head_norm(ph, pb, 64, gcol, sqr, rsr, self.psA, fin)

    def moba(self, j, l, s):
        A = self.A; P = self.P; m0 = A.mark()
        gb = G_MOBA + 2 * j
        H, HB = A.alloc("H", [128, NCH, S], BF16, nbufs=NG)
        m1 = A.mark()
        sqr = Ring(A, "sq", [128, TG], BF16, 4); rsr = Ring(A, "rs", [128, TG], F32, 3)
        self.norm_x(G_AN + 8 * l, H, HB, sqr, rsr, self.psA)
        A.release(m1)
        Hd = [dict(qn=A.alloc("qn%d" % i, [128, S], BF16), kn=A.alloc("kn%d" % i, [128, S], BF16),
                   V=A.alloc("V%d" % i, [128, 16, 128], BF16), w=A.alloc("w%d" % i, [128, NCH, 384], BF16),
                   strip=A.alloc("strip%d" % i, [128, STRIP_W], F32),
                   MT=A.alloc("MT%d" % i, [8, 1024], BF16))
              for i in range(2)]
        q32, q32b = A.alloc("q32", [128, 1024], F32)
        attn, attnB = A.alloc("attn", [128, 4, S], BF16, nbufs=4)
        Wo, Wob = A.alloc("Wo", [128, 4, 1024], BF16)
        Pr = Ring(A, "P", [128, TG], BF16, 5)
        Tr = Ring(A, "tmp", [128, TG], F32, 3)
        sqr = Ring(A, "sq", [128, TG], BF16, 3); rsr = Ring(A, "rs", [128, TG], F32, 3)
        km, kmb = A.alloc("kmean", [128, 8], F32)
        gm, gmb = A.alloc("gm", [128, 64], F32)
        mx, mxb = A.alloc("mx", [128, 64], F32)
        mv, mvb = A.alloc("mv", [128, 64], F32)
        wqkv = self.d["moba_w_qkv"].ap(); stripd = self.d["strip"].ap()
        scale = 1.0 / math.sqrt(128.0)
        indn = lambda n: self.CB[0:8, CB_IND + n * 128: CB_IND + (n + 1) * 128]

        def proj(h):
            hd = Hd[h % 2]
            w, wb = hd["w"]; strip, stripb = hd["strip"]
            for part in range(3):
                srcw = wqkv[j, :, part * 1024 + h * 128: part * 1024 + (h + 1) * 128].rearrange("(c p) n -> p c n", p=128)
                P.dma("pool", w[:, :, part * 128:(part + 1) * 128], srcw, wb, writes=[wb])
            P.dma("sp", strip[:], stripd[h], stripb, writes=[stripb])
            qn, qnb = hd["qn"]; kn, knb = hd["kn"]; V, Vb = hd["V"]
            for g in range(NG):
                sl = slice(g * TG, (g + 1) * TG)
                for part, (dst, dstb, gcol) in enumerate(((qn, qnb, gb), (kn, knb, gb + 1))):
                    ph, pb = self.psA.next()
                    for c in range(NCH):
                        self.mm(ph[:], w[:, c, part * 128:(part + 1) * 128], H[:, c, sl], c == 0, c == NCH - 1,
                                reads=[wb, HB[g]], writes=[pb])

                    def fin(gc, rs, rsb, ph=ph, pb=pb, part=part, dst=dst, dstb=dstb, g=g, sl=sl):
                        if part == 0 and g < 2:
                            self.dve(lambda e: e.scalar_tensor_tensor(dst[:, sl], ph[:], gc, rs[:], ALU.mult, ALU.mult),
                                     reads=[pb, rsb, self.Gb], writes=[dstb])
                            return
                        if part == 0:
                            t = q32[:, (g - 2) * TG:(g - 1) * TG]; tb = q32b
                        else:
                            t_, tb = Tr.next(); t = t_[:]
                        self.dve(lambda e: e.scalar_tensor_tensor(t, ph[:], gc, rs[:], ALU.mult, ALU.mult),
                                 reads=[pb, rsb, self.Gb], writes=[tb])
                        self.copy(dst[:, sl], t, reads=[tb], writes=[dstb], eng="act")
                        if part == 1:
                            self.dve(lambda e: e.tensor_reduce(km[:, 2 * g:2 * g + 2], t.rearrange("p (a b) -> p a b", a=2),
                                                               AX.X, ALU.add), reads=[tb], writes=[kmb])
                    self.head_norm(ph, pb, 128, gcol, sqr, rsr, self.psA, fin)
                ph, pb = self.psA.next()
                for t4 in range(4):
                    ts_ = slice(g * TG + t4 * 128, g * TG + (t4 + 1) * 128)
                    for c in range(NCH):
                        self.mm(ph[:, t4 * 128:(t4 + 1) * 128], H[:, c, ts_], w[:, c, 256:384], c == 0, c == NCH - 1,
                                reads=[wb, HB[g]], writes=[pb])
                self.copy(V[:, g * 4:(g + 1) * 4, :], ph[:].rearrange("p (a b) -> p a b", a=4), reads=[pb], writes=[Vb])
            MT, MTb = hd["MT"]
            gh, gpb = self.psA.next()
            for i in range(8):
                self.mm(gh[:, i * 8:(i + 1) * 8], q32[:, i * 128:(i + 1) * 128], km[:, :], True, True,
                        reads=[q32b, kmb], writes=[gpb])
            negm = self.CF[:, CF_NEGM:CF_NEGM + 64]; past = self.CF[:, CF_PAST:CF_PAST + 64]
            self.dve(lambda e: e.tensor_tensor(gm[:], gh[:, 0:64], negm, ALU.add), reads=[gpb, self.CFb], writes=[gmb])
            for i in range(8):
                self.dve(lambda e, i=i: e.max(mx[:, i * 8:(i + 1) * 8], gm[:, i * 8:(i + 1) * 8]), reads=[gmb], writes=[mxb])
            for i in range(8):
                self.dve(lambda e, i=i: e.tensor_scalar(mv[:, i * 8:(i + 1) * 8], gm[:, i * 8:(i + 1) * 8],
                                                        mx[:, i * 8 + 2:i * 8 + 3], 1.0, ALU.is_ge, ALU.subtract),
                         reads=[gmb, mxb], writes=[mvb])
            self.dve(lambda e: e.tensor_tensor(mv[:], mv[:], past, ALU.mult), reads=[mvb, self.CFb], writes=[mvb])
            for half in range(2):
                th, tpb = self.psA.next()
                for ii in range(4):
                    i = half * 4 + ii
                    self.tr(th[0:8, ii * 128:(ii + 1) * 128], mv[:, i * 8:(i + 1) * 8], reads=[mvb], writes=[tpb])
                self.copy(MT[:, half * 512:(half + 1) * 512], th[0:8, :], reads=[tpb], writes=[MTb], eng="act")

        def attend(h):
            hd = Hd[h % 2]
            qn, qnb = hd["qn"]; kn, knb = hd["kn"]; V, Vb = hd["V"]
            strip, stripb = hd["strip"]; MT, MTb = hd["MT"]
            hh = h % 4

            def strip_fn(qt, kt):
                dlt = 512 * qt - 128 * kt
                if dlt >= 1024:
                    return None
                s0 = dlt + 384
                return (strip[:, s0:s0 + 512], stripb)

            def mask_fn(qt, kt):
                n = kt // 2
                if qt < 2 or n > 2 * qt:
                    return None
                return (indn(n), MT[:, (qt - 2) * TG:(qt - 1) * TG], [self.CBb, MTb])
            far = (strip[:, STRIP_W - 1:STRIP_W], stripb)
            self.attention(qn, qnb, kn, knb, V, Vb, None, scale, strip_fn, mask_fn, far,
                           attn[:, hh, :], attnB[hh], Pr, Tr, rsr)

        proj(0)
        for h in range(8):
            if h + 1 < 8:
                proj(h + 1)
            attend(h)
            if h % 4 == 3:
                self.wo_half("moba_w_o", j, h // 4, attn, attnB, Wo, Wob)
        A.release(m0)

    def build(self):
        for s in range(self.nseq):
            self.load_x(s)
            for (kind, j, l) in self.plan:
                if kind == "mla":
                    self.mla(j, l, s)
                elif kind == "moba":
                    self.moba(j, l, s)
                else:
                    self.mlp(l)
            self.store_x(s)
        self.P.emit()
        return self.nc


_HOST_CACHE = {}


def _prep_shared(inp):
    cf, cb = _host_consts()
    sh = dict(cf=cf, cb=cb, gains=_host_gains(inp), strip=_host_strip(inp["rel_bias_table"]))
    for k in ("mla_w_in", "mla_w_uq", "mla_w_ukv", "mla_w_o", "moba_w_qkv", "moba_w_o", "mlp_w_in", "mlp_w_out"):
        sh[k] = np.ascontiguousarray(inp[k], dtype=np.float32)
    return sh


def kernel(**inputs):
    x = np.asarray(inputs["x"], np.float32); pos = np.asarray(inputs["positions"], np.int32)
    B = x.shape[0]
    ncores = 8
    nseq = B // ncores
    sh = _prep_shared(inputs)
    nc = KB(nseq=nseq).build()
    in_maps = []
    for c in range(ncores):
        m = dict(sh)
        m["x"] = np.ascontiguousarray(x[c * nseq:(c + 1) * nseq])
        m["pos"] = np.ascontiguousarray(pos[c * nseq:(c + 1) * nseq])
        in_maps.append(m)
    res = run_bass_kernel_spmd(nc, in_maps, core_ids=list(range(ncores)))
    out = np.concatenate([np.asarray(r["out"], np.float32) for r in res.results], axis=0)
    return out
```

```python
import math
import numpy as np
from contextlib import ExitStack
import concourse.bass as bass
import concourse.mybir as mybir
from concourse.bass_utils import run_bass_kernel_spmd

F32 = mybir.dt.float32; BF16 = mybir.dt.bfloat16; I32 = mybir.dt.int32
ALU = mybir.AluOpType; AF = mybir.ActivationFunctionType; AX = mybir.AxisListType

SAME_ENGINE_SYNC = True
COMPUTE = ("pe", "act", "dve", "pool")
S = 2048; D = 1024; NCH = 8; TG = 512; NG = 4; DFF = 4096
EPS = 1e-6
NEG = -30000.0
SB_BASE = 16512; SB_END = 229344


class Buf:
    __slots__ = ("name", "writers", "readers", "dsem", "dcnt", "excl")

    def __init__(self, name, excl=False):
        self.name = name; self.writers = []; self.readers = []; self.dsem = None; self.dcnt = 0
        self.excl = excl


class Op:
    __slots__ = ("eng", "idx", "fn", "deps", "marked", "dma")

    def __init__(self, eng, idx, fn, dma=None):
        self.eng = eng; self.idx = idx; self.fn = fn; self.deps = []; self.marked = False; self.dma = dma


class Prog:
    def __init__(self, nc):
        self.nc = nc
        self.streams = {e: [] for e in ("pe", "act", "dve", "pool", "sp")}
        self.seen = {e: {} for e in self.streams}
        self.dbufs = []

    def _resolve(self, eng, toks):
        out = []
        seen = self.seen[eng]
        for t in toks:
            if t[0] == "c":
                _, E, j = t
                if E == eng and (eng == "pe" or not SAME_ENGINE_SYNC):
                    continue
                if seen.get(E, -1) >= j:
                    continue
                seen[E] = j
                self.streams[E][j].marked = True
                out.append(t)
            else:
                b = t[1]
                v = b.dcnt
                key = id(b)
                if seen.get(key, -1) >= v:
                    continue
                seen[key] = v
                out.append(("d", b, v))
        return out

    def _record(self, eng, op, tok, reads, writes):
        toks = []
        def other(ts):
            return [t for t in ts if not (t[0] == "c" and t[1] == eng)]
        for b in reads:
            toks.extend(b.writers)
            if b.excl:
                toks.extend(other(b.readers))
        for b in writes:
            toks.extend(b.writers); toks.extend(other(b.readers))
        op.deps = self._resolve(eng, toks)
        for b in reads:
            b.readers.append(tok)
        for b in writes:
            b.writers = [tok]; b.readers = []
        self.streams[eng].append(op)

    def op(self, eng, fn, reads=(), writes=()):
        o = Op(eng, len(self.streams[eng]), fn)
        self._record(eng, o, ("c", eng, o.idx), reads, writes)

    def dma(self, q, out_ap, in_ap, sembuf, reads=(), writes=()):
        if sembuf.dsem is None:
            sembuf.dsem = True
            self.dbufs.append(sembuf)
        o = Op(q, len(self.streams[q]), None, dma=(out_ap, in_ap, sembuf))
        self._record(q, o, ("d", sembuf, sembuf.dcnt + 16), reads, writes)
        sembuf.dcnt += 16

    def emit(self):
        nc = self.nc
        with ExitStack() as es:
            esem = {e: es.enter_context(nc.semaphore("sem_" + e)) for e in COMPUTE}
            for i, b in enumerate(self.dbufs):
                b.dsem = es.enter_context(nc.semaphore("ds%d" % i))
            val = {}
            for e in COMPUTE:
                c = 0
                for o in self.streams[e]:
                    if o.marked:
                        c += 1
                    val[(e, o.idx)] = c
            block = es.enter_context(nc.Block())

            def run(engobj, name):
                for o in self.streams[name]:
                    for t in o.deps:
                        if t[0] == "c":
                            engobj.wait_ge(esem[t[1]], val[(t[1], t[2])])
                        else:
                            engobj.wait_ge(t[1].dsem, t[2])
                    if o.dma is not None:
                        out_ap, in_ap, sb = o.dma
                        engobj.dma_start(out=out_ap, in_=in_ap).then_inc(sb.dsem, 16)
                    else:
                        ins = o.fn(engobj)
                        if o.marked:
                            ins.then_inc(esem[name], 1)
                if name == "sp":
                    for b in self.dbufs:
                        engobj.wait_ge(b.dsem, b.dcnt)

            @block.tensor
            def _(e): run(e, "pe")

            @block.scalar
            def _(e): run(e, "act")

            @block.vector
            def _(e): run(e, "dve")

            @block.gpsimd
            def _(e): run(e, "pool")

            @block.sync
            def _(e): run(e, "sp")


def _dsize(dt):
    return 2 if dt == BF16 else 4


class Arena:
    def __init__(self, nc, base, end):
        self.nc = nc; self.base = base; self.end = end; self.top = base; self.hist = []; self.n = 0

    def alloc(self, name, shape, dtype, nbufs=None):
        nbytes = int(np.prod(shape[1:])) * _dsize(dtype)
        start = (self.top + 31) // 32 * 32
        assert start + nbytes <= self.end, "SBUF arena overflow: %s needs %d at %d (end %d)" % (name, nbytes, start, self.end)
        self.n += 1
        h = self.nc.alloc_sbuf_tensor_at("%s_%d" % (name, self.n), list(shape), dtype, offset=start)
        inh = []
        for (s0, e0, ob) in self.hist:
            if s0 < start + nbytes and e0 > start:
                inh.extend(ob.writers); inh.extend(ob.readers)
        bufs = []
        for i in range(nbufs or 1):
            b = Buf(name if nbufs is None else "%s.%d" % (name, i))
            b.writers.extend(inh)
            bufs.append(b)
        for b in bufs:
            self.hist.append((start, start + nbytes, b))
        self.top = start + nbytes
        return (h, bufs[0]) if nbufs is None else (h, bufs)

    def mark(self):
        return self.top

    def release(self, m):
        self.top = m


class Ring:
    def __init__(self, arena, name, shape, dtype, n):
        self.items = [arena.alloc("%s%d" % (name, i), shape, dtype) for i in range(n)]
        self.i = 0

    def next(self):
        it = self.items[self.i % len(self.items)]
        self.i += 1
        return it


class PsRing:
    def __init__(self, items):
        self.items = items; self.i = 0

    def next(self):
        it = self.items[self.i % len(self.items)]
        self.i += 1
        return it


CF_ID = 0; CF_ROT = 128; CF_INVF = 192; CF_NEGM = 193; CF_PAST = 257; CF_CSTRIP = 321; CF_W = 321 + 896
CB_ID = 0; CB_ONES = 128; CB_IND = 256; CB_W = 256 + 1024
STRIP_W = 1920
G_AN = 0; G_MN = 32; G_MLA = 64; G_MOBA = 82; G_W = 86


def _t5_bucket_np(d):
    n = np.maximum(d, 0)
    nf = np.maximum(n, 1).astype(np.float32)
    large = 16 + (np.log(nf / np.float32(16)) / np.float32(math.log(64.0)) * np.float32(16)).astype(np.int32)
    large = np.minimum(large, 31)
    return np.where(n < 16, n, large)


def _host_consts():
    cf = np.zeros((128, CF_W), np.float32)
    cf[:, CF_ID:CF_ID + 128] = np.eye(128, dtype=np.float32)
    rotT = np.zeros((64, 64), np.float32)
    for m in range(32):
        rotT[m + 32, m] = -1.0
        rotT[m, m + 32] = 1.0
    cf[:64, CF_ROT:CF_ROT + 64] = rotT
    inv = (10000.0 ** (-np.arange(0, 64, 2, dtype=np.float32) / np.float32(64))).astype(np.float32)
    cf[:32, CF_INVF] = inv; cf[32:64, CF_INVF] = inv
    for i in range(8):
        own = 4 + i // 2
        for n in range(8):
            cf[:, CF_NEGM + i * 8 + n] = -1e30 if n >= own else 0.0
            cf[:, CF_PAST + i * 8 + n] = 30000.0 if n < own else 0.0
    k = np.arange(128)[:, None]; j = np.arange(896)[None, :]
    cf[:, CF_CSTRIP:CF_CSTRIP + 896] = np.where(j - k - 384 >= 0, 0.0, NEG)
    cb = np.zeros((128, CB_W), np.float32)
    cb[:, CB_ID:CB_ID + 128] = np.eye(128, dtype=np.float32)
    cb[:, CB_ONES:CB_ONES + 128] = 1.0
    for n in range(8):
        cb[n, CB_IND + n * 128: CB_IND + (n + 1) * 128] = 1.0
    return cf, cb


def _host_gains(inp):
    g = np.zeros((128, G_W), np.float32)

    def put(col, vec):
        vec = np.asarray(vec, np.float32)
        n = vec.shape[0]
        if n >= 128:
            c = n // 128
            g[:, col:col + c] = vec.reshape(c, 128).T
        else:
            g[:n, col] = vec
    for l in range(4):
        put(G_AN + 8 * l, inp["attn_norm"][l]); put(G_MN + 8 * l, inp["mlp_norm"][l])
    for j in range(2):
        b = G_MLA + 9 * j
        put(b, inp["mla_q_a_norm"][j]); put(b + 3, inp["mla_kv_a_norm"][j])
        put(b + 5, inp["mla_q_nope_norm"][j]); put(b + 6, inp["mla_q_rope_norm"][j])
        put(b + 7, inp["mla_k_nope_norm"][j]); put(b + 8, inp["mla_k_rope_norm"][j])
        b2 = G_MOBA + 2 * j
        put(b2, inp["moba_q_norm"][j]); put(b2 + 1, inp["moba_k_norm"][j])
    return g


def _host_strip(table):
    k = np.arange(128)[:, None]; j = np.arange(STRIP_W)[None, :]
    d = j - k - 384
    bidx = _t5_bucket_np(d)
    t = np.asarray(table, np.float32)
    strip = np.transpose(t[bidx], (2, 0, 1)).copy()
    strip[:, d < 0] = NEG
    return np.ascontiguousarray(strip, dtype=np.float32)


class KB:
    def __init__(self, nseq=2, plan=None, debug_out=False, stop=99):
        self.nseq = nseq; self.stop = stop
        self.plan = plan if plan is not None else [("mla", 0, 0), ("mlp", 0, 0), ("moba", 0, 1), ("mlp", 1, 1),
                                                    ("mla", 1, 2), ("mlp", 2, 2), ("moba", 1, 3), ("mlp", 3, 3)]
        nc = self.nc = bass.Bass("TRN2", target_bir_lowering=False)
        self.P = Prog(nc)
        dt = nc.dram_tensor
        self.d = {}
        def inp(name, shape, dtype=F32):
            self.d[name] = dt(name, list(shape), dtype, kind="ExternalInput")
        inp("x", [nseq, S, D]); self.d["pos"] = None
        inp("cf", [128, CF_W]); inp("cb", [128, CB_W]); inp("gains", [128, G_W])
        kinds = set(k for (k, _, _) in self.plan)
        if "mla" in kinds:
            inp("pos", [nseq, S], I32)
            inp("mla_w_in", [2, 1024, 704]); inp("mla_w_uq", [2, 384, 1536]); inp("mla_w_ukv", [2, 256, 2048])
            inp("mla_w_o", [2, 1024, 1024])
        else:
            del self.d["pos"]
        if "moba" in kinds:
            inp("strip", [8, 128, STRIP_W]); inp("moba_w_qkv", [2, 1024, 3072]); inp("moba_w_o", [2, 1024, 1024])
        if "mlp" in kinds:
            inp("mlp_w_in", [4, 1024, 4096]); inp("mlp_w_out", [4, 4096, 1024])
        self.d["out"] = dt("out", [nseq, S, D], F32, kind="ExternalOutput")
        off = SB_BASE
        def static(name, shape, dtype):
            nonlocal off
            nbytes = int(np.prod(shape[1:])) * _dsize(dtype)
            h = nc.alloc_sbuf_tensor_at(name, list(shape), dtype, offset=off)
            off = (off + nbytes + 31) // 32 * 32
            return h
        self.CF = static("CF", [128, CF_W], F32); self.CFb = Buf("CF")
        self.CB = static("CB", [128, CB_W], BF16); self.CBb = Buf("CB")
        self.G = static("G", [128, G_W], F32); self.Gb = Buf("G")
        self.X = static("X", [128, NCH, S], F32)
        self.XB = [[Buf("X%d_%d" % (c, g)) for g in range(NG)] for c in range(NCH)]
        self.A = Arena(nc, off, SB_END)
        self.PS = []
        for i in range(8):
            h = nc.alloc_psum_tensor("ps%d" % i, [128, 512], F32)
            self.PS.append((h, Buf("ps%d" % i, excl=True)))
        self.psAll = PsRing(self.PS)
        self.psA = PsRing(self.PS[0:5]); self.psB = PsRing(self.PS[5:8])
        P = self.P
        P.dma("sp", self.CF[:], self.d["cf"].ap(), self.CFb, writes=[self.CFb])
        P.dma("pool", self.CB[:], self.d["cb"].ap(), self.CBb, writes=[self.CBb])
        P.dma("sp", self.G[:], self.d["gains"].ap(), self.Gb, writes=[self.Gb])
        self.ident = self.CF[:, CF_ID:CF_ID + 128]
        self.onesb = self.CB[:, CB_ONES:CB_ONES + 128]
        self.evac_i = 0

    def dump(self, name, ap, buf):
        shape = [int(v) for v in ap.shape]
        dtn = self.nc.dram_tensor("dbg_" + name, shape, F32, kind="ExternalOutput")
        self.d["dbg_" + name] = dtn
        self.P.dma("pool", dtn.ap(), ap, buf, reads=[buf])

    def mm(self, out, lhsT, rhs, start, stop, reads, writes):
        self.P.op("pe", lambda e: e.matmul(out, lhsT=lhsT, rhs=rhs, start=start, stop=stop), reads=reads, writes=writes)

    def tr(self, out, in_, reads, writes):
        ident = self.ident[0:in_.shape[0], 0:in_.shape[0]]
        self.P.op("pe", lambda e: e.transpose(out, in_, ident), reads=list(reads) + [self.CFb], writes=writes)

    def act(self, out, in_, func, reads, writes, **kw):
        self.P.op("act", lambda e: e.activation(out, in_, func, **kw), reads=reads, writes=writes)

    def copy(self, out, in_, reads, writes, eng=None):
        if eng is None:
            eng = "act" if (self.evac_i % 2 == 0) else "dve"
            self.evac_i += 1
        if eng == "act":
            self.P.op("act", lambda e: e.copy(out, in_), reads=reads, writes=writes)
        else:
            self.P.op("dve", lambda e: e.tensor_copy(out, in_), reads=reads, writes=writes)

    def dve(self, fn, reads, writes):
        self.P.op("dve", fn, reads=reads, writes=writes)

    def rstd_from_psum(self, ps, npart, n_total, rs, reads, rsb):
        self.act(rs[0:npart, :], ps[0:npart, :], AF.Ln, reads=reads, writes=[rsb], bias=EPS, scale=1.0 / n_total)
        self.act(rs[0:npart, :], rs[0:npart, :], AF.Exp, reads=[rsb], writes=[rsb], scale=-0.5)

    def load_x(self, s):
        A = self.A; m = A.mark()
        ring = Ring(A, "xin", [128, D], F32, 2)
        xd = self.d["x"].ap()
        for g in range(NG):
            banks = [self.psAll.next() for _ in range(NCH)]
            for j in range(4):
                tt = g * 4 + j
                st, sb = ring.next()
                self.P.dma("sp", st[:], xd[s, tt * 128:(tt + 1) * 128, :], sb, writes=[sb])
                for c in range(NCH):
                    ph, pb = banks[c]
                    self.tr(ph[:, j * 128:(j + 1) * 128], st[:, c * 128:(c + 1) * 128], reads=[sb], writes=[pb])
            for c in range(NCH):
                ph, pb = banks[c]
                self.copy(self.X[:, c, g * TG:(g + 1) * TG], ph[:], reads=[pb], writes=[self.XB[c][g]])
        A.release(m)

    def store_x(self, s):
        A = self.A; m = A.mark()
        ring = Ring(A, "xout", [128, D], F32, 2)
        od = self.d["out"].ap()
        for tt in range(16):
            g = tt // 4
            st, sb = ring.next()
            for half in range(2):
                ph, pb = self.psAll.next()
                for cc in range(4):
                    c = half * 4 + cc
                    self.tr(ph[:, cc * 128:(cc + 1) * 128], self.X[:, c, tt * 128:(tt + 1) * 128],
                            reads=[self.XB[c][g]], writes=[pb])
                self.copy(st[:, half * 512:(half + 1) * 512], ph[:], reads=[pb], writes=[sb])
            self.P.dma("sp", od[s, tt * 128:(tt + 1) * 128, :], st[:], sb, reads=[sb])
        A.release(m)

    def norm_x(self, gcol, H, HB, sqr, rsr, ps_ring):
        for g in range(NG):
            ph, pb = ps_ring.next()
            sl = slice(g * TG, (g + 1) * TG)
            for c in range(NCH):
                sq, sqb = sqr.next()
                self.act(sq[:], self.X[:, c, sl], AF.Square, reads=[self.XB[c][g]], writes=[sqb])
                self.mm(ph[:], self.onesb, sq[:], c == 0, c == NCH - 1, reads=[sqb, self.CBb], writes=[pb])
            rs, rsb = rsr.next()
            self.rstd_from_psum(ph, 128, float(D), rs, [pb], rsb)
            for c in range(NCH):
                gc = self.G[:, gcol + c:gcol + c + 1]
                self.dve(lambda e, c=c, gc=gc, rs=rs, sl=sl: e.scalar_tensor_tensor(
                    H[:, c, sl], self.X[:, c, sl], gc, rs[:], ALU.mult, ALU.mult),
                    reads=[self.XB[c][g], rsb, self.Gb], writes=[HB[g]])

    def mlp(self, l):
        A = self.A; P = self.P; m0 = A.mark()
        H, HB = A.alloc("H", [128, NCH, S], BF16, nbufs=NG)
        W1r = Ring(A, "W1", [128, 8, 1024], BF16, 2); W2r = Ring(A, "W2", [128, 8, 1024], BF16, 2)
        Ar = Ring(A, "Aa", [128, 8, TG], BF16, 2); Rr = Ring(A, "Rr", [128, TG], F32, 3)
        w_in = self.d["mlp_w_in"].ap(); w_out = self.d["mlp_w_out"].ap()

        def load_w(q):
            (w1, b1) = W1r.next(); (w2, b2) = W2r.next()
            src1 = w_in[l, :, q * 1024:(q + 1) * 1024].rearrange("(c p) n -> p c n", p=128)
            src2 = w_out[l, q * 1024:(q + 1) * 1024, :].rearrange("(c p) n -> p c n", p=128)
            for cc in range(0, 8, 2):
                P.dma("pool", w1[:, cc:cc + 2, :], src1[:, cc:cc + 2, :], b1, writes=[b1])
            for cc in range(0, 8, 2):
                P.dma("pool", w2[:, cc:cc + 2, :], src2[:, cc:cc + 2, :], b2, writes=[b2])
            return (w1, b1, w2, b2)

        def up(W, g):
            w1, b1, _, _ = W
            a, ab = Ar.next()
            sl = slice(g * TG, (g + 1) * TG)
            for f in range(8):
                ph, pb = self.psAll.next()
                for k in range(NCH):
                    self.mm(ph[:], w1[:, k, f * 128:(f + 1) * 128], H[:, k, sl], k == 0, k == NCH - 1,
                            reads=[b1, HB[g]], writes=[pb])
                r, rb = Rr.next()
                self.act(r[:], ph[:], AF.Relu, reads=[pb], writes=[rb])
                self.dve(lambda e, a=a, f=f, r=r: e.tensor_tensor(a[:, f, :], r[:], r[:], ALU.mult), reads=[rb], writes=[ab])
            return (a, ab)

        def down(W, g, at):
            _, _, w2, b2 = W
            a, ab = at
            sl = slice(g * TG, (g + 1) * TG)
            for mch in range(NCH):
                ph, pb = self.psAll.next()
                for f in range(8):
                    self.mm(ph[:], w2[:, f, mch * 128:(mch + 1) * 128], a[:, f, :], f == 0, f == 7,
                            reads=[b2, ab], writes=[pb])
                xs = self.X[:, mch, sl]
                self.dve(lambda e, xs=xs, ph=ph: e.tensor_tensor(xs, xs, ph[:], ALU.add),
                         reads=[pb, self.XB[mch][g]], writes=[self.XB[mch][g]])

        Ws = {0: load_w(0)}
        m1 = A.mark()
        sqr = Ring(A, "sq", [128, TG], BF16, 4); rsr = Ring(A, "rs", [128, TG], F32, 2)
        self.norm_x(G_MN + 8 * l, H, HB, sqr, rsr, self.psAll)
        A.release(m1)
        steps = [(q, g) for q in range(4) for g in range(NG)]
        pend = None
        for i, (q, g) in enumerate(steps):
            at = up(Ws[q], g)
            if pend is not None:
                down(*pend)
            pend = (Ws[q], g, at)
            if g == 0 and q + 1 < 4:
                Ws[q + 1] = load_w(q + 1)
        down(*pend)
        A.release(m0)

    def rope_tables(self, s, cos, cosb, sin, sinb):
        A = self.A; m = A.mark()
        pi_, pib = A.alloc("posi", [64, S], I32)
        ang, angb = A.alloc("ang", [64, S], F32)
        kf, kfb = A.alloc("kf", [64, S], F32)
        r, rb = A.alloc("r", [64, S], F32)
        shh, shb = A.alloc("shh", [64, S], F32)
        chh, chb = A.alloc("chh", [64, S], F32)
        src = bass.AP(self.d["pos"], s * S, [[0, 64], [1, S]])
        self.P.dma("sp", pi_[:], src, pib, writes=[pib])
        self.dve(lambda e: e.tensor_copy(ang[:], pi_[:]), reads=[pib], writes=[angb])
        invf = self.CF[0:64, CF_INVF:CF_INVF + 1]
        self.dve(lambda e: e.tensor_scalar(ang[:], ang[:], invf, None, ALU.mult), reads=[angb, self.CFb], writes=[angb])
        self.dve(lambda e: e.tensor_scalar(pi_[:], ang[:], 1.0 / (2.0 * math.pi), None, ALU.mult), reads=[angb], writes=[pib])
        self.dve(lambda e: e.tensor_copy(kf[:], pi_[:]), reads=[pib], writes=[kfb])
        self.dve(lambda e: e.scalar_tensor_tensor(r[:], kf[:], -2.0 * math.pi, ang[:], ALU.mult, ALU.add),
                 reads=[kfb, angb], writes=[rb])
        sh = 0.999999
        self.act(shh[:], r[:], AF.Sin, reads=[rb], writes=[shb], scale=0.5 * sh)
        self.act(chh[:], r[:], AF.Sin, reads=[rb], writes=[chb], bias=0.5 * math.pi * sh, scale=-0.5 * sh)
        self.dve(lambda e: e.scalar_tensor_tensor(sin[:], shh[:], 2.0, chh[:], ALU.mult, ALU.mult), reads=[shb, chb], writes=[sinb])
        self.dve(lambda e: e.tensor_tensor(kf[:], shh[:], shh[:], ALU.mult), reads=[shb], writes=[kfb])
        self.dve(lambda e: e.tensor_scalar(cos[:], kf[:], -2.0, 1.0, ALU.mult, ALU.add), reads=[kfb], writes=[cosb])
        A.release(m)

    def head_norm(self, ph, pb, npart, gcol, sqr, rsr, psr, out_fn):
        sq, sqb = sqr.next()
        self.act(sq[0:npart, :], ph[0:npart, :], AF.Square, reads=[pb], writes=[sqb])
        p2, p2b = psr.next()
        self.mm(p2[0:npart, :], self.onesb[0:npart, 0:npart], sq[0:npart, :], True, True, reads=[sqb, self.CBb], writes=[p2b])
        rs, rsb = rsr.next()
        self.rstd_from_psum(p2, npart, float(npart), rs, [p2b], rsb)
        gc = self.G[0:npart, gcol:gcol + 1]
        out_fn(gc, rs, rsb)

    def attention(self, qT, qb, kT, kb, V, Vb, extra_qk, scale, strip_fn, mask_fn, far_bias,
                  outT, outb, Pr, Tr, rsr):
        LOOK = 3
        for qt in range(4):
            qs = slice(qt * TG, (qt + 1) * TG)
            oh, ob = self.psB.next(); lh, lb = self.psB.next()
            nk = 4 * qt + 4

            def qk(kt):
                ks = slice(kt * 128, (kt + 1) * 128)
                sh_, sb_ = self.psA.next()
                mk = mask_fn(qt, kt) if mask_fn is not None else None
                last_qk = (extra_qk is None) and (mk is None)
                self.mm(sh_[:], kT[:, ks], qT[:, qs], True, last_qk, reads=[kb, qb], writes=[sb_])
                if extra_qk is not None:
                    l2, r2, rd2 = extra_qk
                    self.mm(sh_[:], l2(kt), r2(qt), False, mk is None, reads=rd2, writes=[sb_])
                if mk is not None:
                    ml, mr, mrd = mk
                    self.mm(sh_[:], ml, mr, False, True, reads=mrd, writes=[sb_])
                p, pbuf = Pr.next()
                st = strip_fn(qt, kt)
                if st is not None:
                    sap, sbuf = st
                    t, tb = Tr.next()
                    self.dve(lambda e, t=t, sh_=sh_, sap=sap: e.scalar_tensor_tensor(
                        t[:], sh_[:], scale, sap, ALU.mult, ALU.add), reads=[sb_, sbuf], writes=[tb])
                    self.act(p[:], t[:], AF.Exp, reads=[tb], writes=[pbuf])
                elif far_bias is not None:
                    fb, fbb = far_bias
                    self.act(p[:], sh_[:], AF.Exp, reads=[sb_, fbb], writes=[pbuf], bias=fb, scale=scale)
                else:
                    self.act(p[:], sh_[:], AF.Exp, reads=[sb_], writes=[pbuf], scale=scale)
                return (kt, p, pbuf)

            def pv(item):
                kt, p, pbuf = item
                self.mm(oh[:], V[:, kt, :], p[:], kt == 0, kt == nk - 1, reads=[Vb, pbuf], writes=[ob])
                self.mm(lh[:], self.onesb, p[:], kt == 0, kt == nk - 1, reads=[pbuf, self.CBb], writes=[lb])

            pend = []
            for kt in range(nk):
                pend.append(qk(kt))
                if len(pend) > LOOK:
                    pv(pend.pop(0))
            while pend:
                pv(pend.pop(0))
            rl, rlb = rsr.next()
            self.act(rl[:], lh[:], AF.Ln, reads=[lb], writes=[rlb])
            self.act(rl[:], rl[:], AF.Exp, reads=[rlb], writes=[rlb], scale=-1.0)
            self.dve(lambda e, rl=rl, oh=oh, qs=qs: e.tensor_tensor(outT[:, qs], oh[:], rl[:], ALU.mult),
                     reads=[ob, rlb], writes=[outb])

    def wo_half(self, wname, j, half, attn, attnb, Wo, Wob):
        wd = self.d[wname].ap()
        src = wd[j, half * 512:(half + 1) * 512, :].rearrange("(c p) n -> p c n", p=128)
        for hh in range(4):
            self.P.dma("pool", Wo[:, hh:hh + 1, :], src[:, hh:hh + 1, :], Wob, writes=[Wob])
        for g in range(NG):
            sl = slice(g * TG, (g + 1) * TG)
            for mch in range(NCH):
                ph, pb = self.psA.next()
                for hh in range(4):
                    self.mm(ph[:], Wo[:, hh, mch * 128:(mch + 1) * 128], attn[:, hh, sl], hh == 0, hh == 3,
                            reads=[Wob, attnb[hh]], writes=[pb])
                xs = self.X[:, mch, sl]
                self.dve(lambda e, xs=xs, ph=ph: e.tensor_tensor(xs, xs, ph[:], ALU.add),
                         reads=[pb, self.XB[mch][g]], writes=[self.XB[mch][g]])

    def mla(self, j, l, s):
        A = self.A; P = self.P; m0 = A.mark()
        gb = G_MLA + 9 * j
        cqn, cqnB = A.alloc("cqn", [128, 3, S], BF16, nbufs=NG)
        ckvn, ckvnB = A.alloc("ckvn", [128, 2, S], BF16, nbufs=NG)
        kpe, kpeb = A.alloc("kpe", [64, S], BF16)
        cos, cosb = A.alloc("cos", [64, S], F32); sin, sinb = A.alloc("sin", [64, S], F32)
        self.rope_tables(s, cos, cosb, sin, sinb)
        rotT = self.CF[0:64, CF_ROT:CF_ROT + 64]
        if self.stop <= 1:
            A.release(m0); return
        m1 = A.mark()
        H, HB = A.alloc("H", [128, NCH, S], BF16, nbufs=NG)
        Win, Winb = A.alloc("Win", [128, NCH, 704], BF16)
        src = self.d["mla_w_in"].ap()[j].rearrange("(c p) n -> p c n", p=128)
        for cc in range(0, 8, 4):
            P.dma("pool", Win[:, cc:cc + 4, :], src[:, cc:cc + 4, :], Winb, writes=[Winb])
        sqr = Ring(A, "sq", [128, TG], BF16, 4); rsr = Ring(A, "rs", [128, TG], F32, 3)
        rawr = Ring(A, "raw", [128, 3, TG], F32, 2)
        Tr = Ring(A, "tmp", [128, TG], F32, 3)
        self.norm_x(G_AN + 8 * l, H, HB, sqr, rsr, self.psA)
        for g in range(NG):
            sl = slice(g * TG, (g + 1) * TG)
            for (dst, dstB, c0, nchunk, gcol) in ((cqn, cqnB, 0, 3, gb), (ckvn, ckvnB, 3, 2, gb + 3)):
                raw, rawb = rawr.next()
                p2, p2b = self.psB.next()
                for cc in range(nchunk):
                    ph, pb = self.psA.next()
                    col = (c0 + cc) * 128
                    for k in range(NCH):
                        self.mm(ph[:], Win[:, k, col:col + 128], H[:, k, sl], k == 0, k == NCH - 1,
                                reads=[Winb, HB[g]], writes=[pb])
                    self.copy(raw[:, cc, :], ph[:], reads=[pb], writes=[rawb], eng="dve")
                    sq, sqb = sqr.next()
                    self.act(sq[:], ph[:], AF.Square, reads=[pb], writes=[sqb])
                    self.mm(p2[:], self.onesb, sq[:], cc == 0, cc == nchunk - 1, reads=[sqb, self.CBb], writes=[p2b])
                rs, rsb = rsr.next()
                self.rstd_from_psum(p2, 128, float(nchunk * 128), rs, [p2b], rsb)
                for cc in range(nchunk):
                    gc = self.G[:, gcol + cc:gcol + cc + 1]
                    self.dve(lambda e, dst=dst, cc=cc, raw=raw, gc=gc, rs=rs, sl=sl: e.scalar_tensor_tensor(
                        dst[:, cc, sl], raw[:, cc, :], gc, rs[:], ALU.mult, ALU.mult),
                        reads=[rawb, rsb, self.Gb], writes=[dstB[g]])
            ph, pb = self.psA.next()
            for k in range(NCH):
                self.mm(ph[0:64, :], Win[:, k, 640:704], H[:, k, sl], k == 0, k == NCH - 1, reads=[Winb, HB[g]], writes=[pb])
            self.rope_head(ph, pb, gb + 8, sqr, rsr, Tr, cos, cosb, sin, sinb, rotT, sl, kpe[:, sl], kpeb)
        A.release(m1)
        if self.stop <= 2:
            for nm, t, bb in (("cqn", cqn, cqnB), ("ckvn", ckvn, ckvnB)):
                for g in range(NG):
                    self.dump("%s%d" % (nm, g), t[:, :, g * TG:(g + 1) * TG], bb[g])
            self.dump("kpe", kpe[:], kpeb); self.dump("cos", cos[:], cosb); self.dump("sin", sin[:], sinb)
            A.release(m0); return
        Hd = [dict(qn=A.alloc("qn%d" % i, [128, S], BF16), qr=A.alloc("qr%d" % i, [64, S], BF16),
                   kn=A.alloc("kn%d" % i, [128, S], BF16), V=A.alloc("V%d" % i, [128, 16, 128], BF16),
                   wq=A.alloc("wq%d" % i, [128, 3, 192], BF16), wkv=A.alloc("wkv%d" % i, [128, 2, 256], BF16))
              for i in range(2)]
        attn, attnB = A.alloc("attn", [128, 4, S], BF16, nbufs=4)
        Wor = Ring(A, "Wo", [128, 4, 1024], BF16, 2)
        Pr = Ring(A, "P", [128, TG], BF16, 5)
        Tr = Ring(A, "tmp", [128, TG], F32, 3)
        sqr = Ring(A, "sq", [128, TG], BF16, 4); rsr = Ring(A, "rs", [128, TG], F32, 3)
        wuq = self.d["mla_w_uq"].ap(); wukv = self.d["mla_w_ukv"].ap()
        scale = 1.0 / math.sqrt(192.0)

        def proj(h):
            hd = Hd[h % 2]
            wq, wqb = hd["wq"]; wkv, wkvb = hd["wkv"]
            P.dma("pool", wq[:], wuq[j, :, h * 192:(h + 1) * 192].rearrange("(c p) n -> p c n", p=128), wqb, writes=[wqb])
            P.dma("pool", wkv[:], wukv[j, :, h * 256:(h + 1) * 256].rearrange("(c p) n -> p c n", p=128), wkvb, writes=[wkvb])
            qn, qnb = hd["qn"]; qr, qrb = hd["qr"]; kn, knb = hd["kn"]; V, Vb = hd["V"]
            for g in range(NG):
                sl = slice(g * TG, (g + 1) * TG)
                ph, pb = self.psA.next()
                for c in range(3):
                    self.mm(ph[:], wq[:, c, 0:128], cqn[:, c, sl], c == 0, c == 2, reads=[wqb, cqnB[g]], writes=[pb])
                self.head_norm(ph, pb, 128, gb + 5, sqr, rsr, self.psA,
                               lambda gc, rs, rsb, ph=ph, pb=pb, sl=sl: self.dve(lambda e: e.scalar_tensor_tensor(
                                   qn[:, sl], ph[:], gc, rs[:], ALU.mult, ALU.mult), reads=[pb, rsb, self.Gb], writes=[qnb]))
                ph, pb = self.psA.next()
                for c in range(3):
                    self.mm(ph[0:64, :], wq[:, c, 128:192], cqn[:, c, sl], c == 0, c == 2, reads=[wqb, cqnB[g]], writes=[pb])
                self.rope_head(ph, pb, gb + 6, sqr, rsr, Tr, cos, cosb, sin, sinb, rotT, sl, qr[:, sl], qrb)
                ph, pb = self.psA.next()
                for c in range(2):
                    self.mm(ph[:], wkv[:, c, 0:128], ckvn[:, c, sl], c == 0, c == 1, reads=[wkvb, ckvnB[g]], writes=[pb])
                self.head_norm(ph, pb, 128, gb + 7, sqr, rsr, self.psA,
                               lambda gc, rs, rsb, ph=ph, pb=pb, sl=sl: self.dve(lambda e: e.scalar_tensor_tensor(
                                   kn[:, sl], ph[:], gc, rs[:], ALU.mult, ALU.mult), reads=[pb, rsb, self.Gb], writes=[knb]))
                ph, pb = self.psA.next()
                for t4 in range(4):
                    ts_ = slice(g * TG + t4 * 128, g * TG + (t4 + 1) * 128)
                    for c in range(2):
                        self.mm(ph[:, t4 * 128:(t4 + 1) * 128], ckvn[:, c, ts_], wkv[:, c, 128:256], c == 0, c == 1,
                                reads=[wkvb, ckvnB[g]], writes=[pb])
                self.copy(V[:, g * 4:(g + 1) * 4, :], ph[:].rearrange("p (a b) -> p a b", a=4), reads=[pb], writes=[Vb])

        def attend(h):
            hd = Hd[h % 2]
            qn, qnb = hd["qn"]; qr, qrb = hd["qr"]; kn, knb = hd["kn"]; V, Vb = hd["V"]
            hh = h % 4

            def strip_fn(qt, kt):
                if kt < 4 * qt:
                    return None
                s0 = CF_CSTRIP + (512 * qt - 128 * kt) + 384
                return (self.CF[:, s0:s0 + 512], self.CFb)
            extra = (lambda kt: kpe[:, kt * 128:(kt + 1) * 128], lambda qt: qr[:, qt * TG:(qt + 1) * TG], [kpeb, qrb])
            self.attention(qn, qnb, kn, knb, V, Vb, extra, scale, strip_fn, None, None,
                           attn[:, hh, :], attnB[hh], Pr, Tr, rsr)

        proj(0)
        if self.stop <= 3:
            for nm in ("qn", "qr", "kn", "V"):
                t, bb = Hd[0][nm]
                self.dump(nm, t[:], bb)
            A.release(m0); return
        for h in range(8):
            if h + 1 < 8:
                proj(h + 1)
            attend(h)
            if self.stop <= 4:
                self.dump("attn0", attn[:, 0, :], attnB[0])
                A.release(m0); return
            if h % 4 == 3:
                Wo, Wob = Wor.next()
                self.wo_half("mla_w_o", j, h // 4, attn, attnB, Wo, Wob)
        A.release(m0)

    def rope_head(self, ph, pb, gcol, sqr, rsr, Tr, cos, cosb, sin, sinb, rotT, sl, dst, dstb):
        def fin(gc, rs, rsb):
            xn, xnb = Tr.next()
            self.dve(lambda e: e.scalar_tensor_tensor(xn[0:64, :], ph[0:64, :], gc, rs[0:64, :], ALU.mult, ALU.mult),
                     reads=[pb, rsb, self.Gb], writes=[xnb])
            p3, p3b = self.psA.next()
            self.mm(p3[0:64, :], rotT, xn[0:64, :], True, True, reads=[xnb, self.CFb], writes=[p3b])
            t1, t1b = Tr.next()
            self.dve(lambda e: e.tensor_tensor(t1[0:64, :], xn[0:64, :], cos[:, sl], ALU.mult), reads=[xnb, cosb], writes=[t1b])
            t2, t2b = Tr.next()
            self.dve(lambda e: e.tensor_tensor(t2[0:64, :], p3[0:64, :], sin[:, sl], ALU.mult), reads=[p3b, sinb], writes=[t2b])
            self.dve(lambda e: e.tensor_tensor(dst, t1[0:64, :], t2[0:64, :], ALU.add), reads=[t1b, t2b], writes=[dstb])
        self.head_norm(ph, pb, 64, gcol, sqr, rsr, self.psA, fin)

    def moba(self, j, l, s):
        A = self.A; P = self.P; m0 = A.mark()
        gb = G_MOBA + 2 * j
        H, HB = A.alloc("H", [128, NCH, S], BF16, nbufs=NG)
        m1 = A.mark()
        sqr = Ring(A, "sq", [128, TG], BF16, 4); rsr = Ring(A, "rs", [128, TG], F32, 3)
        self.norm_x(G_AN + 8 * l, H, HB, sqr, rsr, self.psA)
        A.release(m1)
        Hd = [dict(qn=A.alloc("qn%d" % i, [128, S], BF16), kn=A.alloc("kn%d" % i, [128, S], BF16),
                   V=A.alloc("V%d" % i, [128, 16, 128], BF16), w=A.alloc("w%d" % i, [128, NCH, 384], BF16),
                   strip=A.alloc("strip%d" % i, [128, STRIP_W], F32),
                   MT=A.alloc("MT%d" % i, [8, 1024], BF16))
              for i in range(2)]
        q32, q32b = A.alloc("q32", [128, 1024], F32)
        attn, attnB = A.alloc("attn", [128, 4, S], BF16, nbufs=4)
        Wo, Wob = A.alloc("Wo", [128, 4, 1024], BF16)
        Pr = Ring(A, "P", [128, TG], BF16, 5)
        Tr = Ring(A, "tmp", [128, TG], F32, 3)
        sqr = Ring(A, "sq", [128, TG], BF16, 3); rsr = Ring(A, "rs", [128, TG], F32, 3)
        km, kmb = A.alloc("kmean", [128, 8], F32)
        gm, gmb = A.alloc("gm", [128, 64], F32)
        mx, mxb = A.alloc("mx", [128, 64], F32)
        mv, mvb = A.alloc("mv", [128, 64], F32)
        wqkv = self.d["moba_w_qkv"].ap(); stripd = self.d["strip"].ap()
        scale = 1.0 / math.sqrt(128.0)
        indn = lambda n: self.CB[0:8, CB_IND + n * 128: CB_IND + (n + 1) * 128]

        def proj(h):
            hd = Hd[h % 2]
            w, wb = hd["w"]; strip, stripb = hd["strip"]
            for part in range(3):
                srcw = wqkv[j, :, part * 1024 + h * 128: part * 1024 + (h + 1) * 128].rearrange("(c p) n -> p c n", p=128)
                P.dma("pool", w[:, :, part * 128:(part + 1) * 128], srcw, wb, writes=[wb])
            P.dma("sp", strip[:], stripd[h], stripb, writes=[stripb])
            qn, qnb = hd["qn"]; kn, knb = hd["kn"]; V, Vb = hd["V"]
            for g in range(NG):
                sl = slice(g * TG, (g + 1) * TG)
                for part, (dst, dstb, gcol) in enumerate(((qn, qnb, gb), (kn, knb, gb + 1))):
                    ph, pb = self.psA.next()
                    for c in range(NCH):
                        self.mm(ph[:], w[:, c, part * 128:(part + 1) * 128], H[:, c, sl], c == 0, c == NCH - 1,
                                reads=[wb, HB[g]], writes=[pb])

                    def fin(gc, rs, rsb, ph=ph, pb=pb, part=part, dst=dst, dstb=dstb, g=g, sl=sl):
                        if part == 0 and g < 2:
                            self.dve(lambda e: e.scalar_tensor_tensor(dst[:, sl], ph[:], gc, rs[:], ALU.mult, ALU.mult),
                                     reads=[pb, rsb, self.Gb], writes=[dstb])
                            return
                        if part == 0:
                            t = q32[:, (g - 2) * TG:(g - 1) * TG]; tb = q32b
                        else:
                            t_, tb = Tr.next(); t = t_[:]
                        self.dve(lambda e: e.scalar_tensor_tensor(t, ph[:], gc, rs[:], ALU.mult, ALU.mult),
                                 reads=[pb, rsb, self.Gb], writes=[tb])
                        self.copy(dst[:, sl], t, reads=[tb], writes=[dstb], eng="act")
                        if part == 1:
                            self.dve(lambda e: e.tensor_reduce(km[:, 2 * g:2 * g + 2], t.rearrange("p (a b) -> p a b", a=2),
                                                               AX.X, ALU.add), reads=[tb], writes=[kmb])
                    self.head_norm(ph, pb, 128, gcol, sqr, rsr, self.psA, fin)
                ph, pb = self.psA.next()
                for t4 in range(4):
                    ts_ = slice(g * TG + t4 * 128, g * TG + (t4 + 1) * 128)
                    for c in range(NCH):
                        self.mm(ph[:, t4 * 128:(t4 + 1) * 128], H[:, c, ts_], w[:, c, 256:384], c == 0, c == NCH - 1,
                                reads=[wb, HB[g]], writes=[pb])
                self.copy(V[:, g * 4:(g + 1) * 4, :], ph[:].rearrange("p (a b) -> p a b", a=4), reads=[pb], writes=[Vb])
            MT, MTb = hd["MT"]
            gh, gpb = self.psA.next()
            for i in range(8):
                self.mm(gh[:, i * 8:(i + 1) * 8], q32[:, i * 128:(i + 1) * 128], km[:, :], True, True,
                        reads=[q32b, kmb], writes=[gpb])
            negm = self.CF[:, CF_NEGM:CF_NEGM + 64]; past = self.CF[:, CF_PAST:CF_PAST + 64]
            self.dve(lambda e: e.tensor_tensor(gm[:], gh[:, 0:64], negm, ALU.add), reads=[gpb, self.CFb], writes=[gmb])
            for i in range(8):
                self.dve(lambda e, i=i: e.max(mx[:, i * 8:(i + 1) * 8], gm[:, i * 8:(i + 1) * 8]), reads=[gmb], writes=[mxb])
            for i in range(8):
                self.dve(lambda e, i=i: e.tensor_scalar(mv[:, i * 8:(i + 1) * 8], gm[:, i * 8:(i + 1) * 8],
                                                        mx[:, i * 8 + 2:i * 8 + 3], 1.0, ALU.is_ge, ALU.subtract),
                         reads=[gmb, mxb], writes=[mvb])
            self.dve(lambda e: e.tensor_tensor(mv[:], mv[:], past, ALU.mult), reads=[mvb, self.CFb], writes=[mvb])
            for half in range(2):
                th, tpb = self.psA.next()
                for ii in range(4):
                    i = half * 4 + ii
                    self.tr(th[0:8, ii * 128:(ii + 1) * 128], mv[:, i * 8:(i + 1) * 8], reads=[mvb], writes=[tpb])
                self.copy(MT[:, half * 512:(half + 1) * 512], th[0:8, :], reads=[tpb], writes=[MTb], eng="act")

        def attend(h):
            hd = Hd[h % 2]
            qn, qnb = hd["qn"]; kn, knb = hd["kn"]; V, Vb = hd["V"]
            strip, stripb = hd["strip"]; MT, MTb = hd["MT"]
            hh = h % 4

            def strip_fn(qt, kt):
                dlt = 512 * qt - 128 * kt
                if dlt >= 1024:
                    return None
                s0 = dlt + 384
                return (strip[:, s0:s0 + 512], stripb)

            def mask_fn(qt, kt):
                n = kt // 2
                if qt < 2 or n > 2 * qt:
                    return None
                return (indn(n), MT[:, (qt - 2) * TG:(qt - 1) * TG], [self.CBb, MTb])
            far = (strip[:, STRIP_W - 1:STRIP_W], stripb)
            self.attention(qn, qnb, kn, knb, V, Vb, None, scale, strip_fn, mask_fn, far,
                           attn[:, hh, :], attnB[hh], Pr, Tr, rsr)

        proj(0)
        for h in range(8):
            if h + 1 < 8:
                proj(h + 1)
            attend(h)
            if h % 4 == 3:
                self.wo_half("moba_w_o", j, h // 4, attn, attnB, Wo, Wob)
        A.release(m0)

    def build(self):
        for s in range(self.nseq):
            self.load_x(s)
            for (kind, j, l) in self.plan:
                if kind == "mla":
                    self.mla(j, l, s)
                elif kind == "moba":
                    self.moba(j, l, s)
                else:
                    self.mlp(l)
            self.store_x(s)
        self.P.emit()
        return self.nc


_HOST_CACHE = {}


def _prep_shared(inp):
    cf, cb = _host_consts()
    sh = dict(cf=cf, cb=cb, gains=_host_gains(inp), strip=_host_strip(inp["rel_bias_table"]))
    for k in ("mla_w_in", "mla_w_uq", "mla_w_ukv", "mla_w_o", "moba_w_qkv", "moba_w_o", "mlp_w_in", "mlp_w_out"):
        sh[k] = np.ascontiguousarray(inp[k], dtype=np.float32)
    return sh


def kernel(**inputs):
    x = np.asarray(inputs["x"], np.float32); pos = np.asarray(inputs["positions"], np.int32)
    B = x.shape[0]
    ncores = 8
    nseq = B // ncores
    sh = _prep_shared(inputs)
    nc = KB(nseq=nseq).build()
    in_maps = []
    for c in range(ncores):
        m = dict(sh)
        m["x"] = np.ascontiguousarray(x[c * nseq:(c + 1) * nseq])
        m["pos"] = np.ascontiguousarray(pos[c * nseq:(c + 1) * nseq])
        in_maps.append(m)
    res = run_bass_kernel_spmd(nc, in_maps, core_ids=list(range(ncores)))
    out = np.concatenate([np.asarray(r["out"], np.float32) for r in res.results], axis=0)
    return out
```
